# Optimizing a Trainium2 kernel written in Bass

```python
import math
import jax, jax.numpy as jnp
from jax import lax
import numpy as np

D_MODEL = 1024
BATCH = 8
SEQ = 2048
DEPTH = 2
DEC_BATCH = 4
DEC_SEQ = 8192
PAST_LEN = 128

N_DIR = 2
EPS = 1e-6
S5_WIDTH = 3 * D_MODEL // 4
S5_GROUP = 16
S5_GROUPS = S5_WIDTH // S5_GROUP
S5_STATE = 64
M2_INNER = 3 * D_MODEL // 2
M2_HEADDIM = 64
M2_HEADS = M2_INNER // M2_HEADDIM
M2_GROUPS = 4
M2_STATE = 128
M2_CONV = 4
M2_PAD_L = (M2_CONV - 1) // 2
M2_PAD_R = M2_CONV - 1 - M2_PAD_L
M2_CHUNK = 128
M2_GN = M2_GROUPS * M2_STATE
M2_CONV_DIM = M2_INNER + 2 * M2_GN
D_FF = 4 * D_MODEL
OFF_U = 0
OFF_Z = OFF_U + S5_WIDTH
OFF_XBC = OFF_Z + M2_INNER
OFF_DT = OFF_XBC + M2_CONV_DIM
OFF_GATE = OFF_DT + N_DIR * M2_HEADS
D_IN_PROJ = OFF_GATE + 2 * D_MODEL

kernel_name = 'hybrid_s5_ssd_gated_encoder'


def rmsnorm(x, w):
    xf = x.astype(jnp.float32)
    y = xf * lax.rsqrt(jnp.mean(jnp.square(xf), axis=-1, keepdims=True) + EPS)
    return (y * w.astype(jnp.float32)).astype(x.dtype)


def gated_rmsnorm(y, z, w):
    g = y * jax.nn.silu(z.astype(jnp.float32))
    shp = g.shape
    g = g.reshape(shp[:-1] + (M2_GROUPS, M2_INNER // M2_GROUPS))
    g = g * lax.rsqrt(jnp.mean(jnp.square(g), axis=-1, keepdims=True) + EPS)
    return g.reshape(shp) * w.astype(jnp.float32)


def complex_linear_scan(a_re, a_im, b_re, b_im, reverse):
    a_re = jnp.broadcast_to(a_re, b_re.shape)
    a_im = jnp.broadcast_to(a_im, b_re.shape)

    def combine(e1, e2):
        a1r, a1i, b1r, b1i = e1
        a2r, a2i, b2r, b2i = e2
        return (a2r * a1r - a2i * a1i,
                a2r * a1i + a2i * a1r,
                a2r * b1r - a2i * b1i + b2r,
                a2r * b1i + a2i * b1r + b2i)

    _, _, s_re, s_im = lax.associative_scan(combine, (a_re, a_im, b_re, b_im),
                                            reverse=reverse, axis=1)
    return s_re, s_im


def s5_mixer(u, lam_re, lam_im, log_dt, b_re, b_im, c_re, c_im, d_skip, w_glu, b_glu):
    f32 = jnp.float32
    bsz, L, _ = u.shape
    uf = u.astype(f32)
    ug = uf.reshape(bsz, L, S5_GROUPS, S5_GROUP)
    y = d_skip.astype(f32) * uf
    for d in range(N_DIR):
        lr = lam_re[d].astype(f32)
        li = lam_im[d].astype(f32)
        delta = jnp.exp(log_dt[d].astype(f32))[:, None]
        mag = jnp.exp(lr * delta)
        a_re = mag * jnp.cos(li * delta)
        a_im = mag * jnp.sin(li * delta)
        inv = 1.0 / (lr * lr + li * li)
        q_re = ((a_re - 1.0) * lr + a_im * li) * inv
        q_im = (a_im * lr - (a_re - 1.0) * li) * inv
        br = b_re[d].astype(f32)
        bi = b_im[d].astype(f32)
        bbar_re = q_re[..., None] * br - q_im[..., None] * bi
        bbar_im = q_re[..., None] * bi + q_im[..., None] * br
        bu_re = jnp.einsum('blgc,gpc->blgp', ug, bbar_re)
        bu_im = jnp.einsum('blgc,gpc->blgp', ug, bbar_im)
        s_re, s_im = complex_linear_scan(a_re, a_im, bu_re, bu_im, reverse=(d == 1))
        y_dir = (jnp.einsum('gcp,blgp->blgc', c_re[d].astype(f32), s_re)
                 - jnp.einsum('gcp,blgp->blgc', c_im[d].astype(f32), s_im))
        y = y + y_dir.reshape(bsz, L, S5_WIDTH)
    h = jax.nn.gelu(y)
    return h * jax.nn.sigmoid(h @ w_glu.astype(f32) + b_glu.astype(f32))


def ssd_scan(x, dt, a, b_in, c_in):
    bsz, L = x.shape[:2]
    nc = L // M2_CHUNK
    k = M2_HEADS // M2_GROUPS
    x = x.reshape(bsz, nc, M2_CHUNK, M2_GROUPS, k, M2_HEADDIM)
    dt = dt.reshape(bsz, nc, M2_CHUNK, M2_GROUPS, k)
    bm = b_in.reshape(bsz, nc, M2_CHUNK, M2_GROUPS, M2_STATE)
    cm = c_in.reshape(bsz, nc, M2_CHUNK, M2_GROUPS, M2_STATE)
    da_cs = jnp.cumsum(dt * a.reshape(M2_GROUPS, k), axis=2)
    xdt = x * dt[..., None]
    lower = jnp.tril(jnp.ones((M2_CHUNK, M2_CHUNK), dtype=bool))[:, :, None, None]
    seg = da_cs[:, :, :, None] - da_cs[:, :, None, :]
    decay = jnp.exp(jnp.where(lower, seg, -jnp.inf))
    cb = jnp.einsum('bclgn,bcsgn->bclsg', cm, bm)
    y_diag = jnp.einsum('bclsg,bclsgk,bcsgkp->bclgkp', cb, decay, xdt)
    decay_in = jnp.exp(da_cs[:, :, -1:] - da_cs)
    chunk_states = jnp.einsum('bcsgn,bcsgk,bcsgkp->bcgkpn', bm, decay_in, xdt)
    chunk_decay = jnp.exp(da_cs[:, :, -1])

    def step(carry, inp):
        st, dec = inp
        return carry * dec[..., None, None] + st, carry

    init = jnp.zeros((bsz, M2_GROUPS, k, M2_HEADDIM, M2_STATE), jnp.float32)
    _, prev = lax.scan(step, init, (jnp.moveaxis(chunk_states, 1, 0),
                                    jnp.moveaxis(chunk_decay, 1, 0)))
    prev = jnp.moveaxis(prev, 0, 1)
    y_off = jnp.einsum('bclgn,bcgkpn,bclgk->bclgkp', cm, prev, jnp.exp(da_cs))
    return (y_diag + y_off).reshape(bsz, L, M2_HEADS, M2_HEADDIM)


def mamba2_mixer(z, xbc, dt_raw, conv_w, conv_b, dt_bias, a_log, d_skip, norm_w):
    f32 = jnp.float32
    bsz, L, _ = z.shape
    xbc = lax.conv_general_dilated(xbc.astype(f32), conv_w.astype(f32)[:, None, :], (1,),
                                   [(M2_PAD_L, M2_PAD_R)],
                                   dimension_numbers=('NWC', 'WIO', 'NWC'),
                                   feature_group_count=M2_CONV_DIM)
    xbc = jax.nn.silu(xbc + conv_b.astype(f32))
    xs = xbc[..., :M2_INNER].reshape(bsz, L, M2_HEADS, M2_HEADDIM)
    bm = xbc[..., M2_INNER:M2_INNER + M2_GN].reshape(bsz, L, M2_GROUPS, M2_STATE)
    cm = xbc[..., M2_INNER + M2_GN:].reshape(bsz, L, M2_GROUPS, M2_STATE)
    dt_raw = dt_raw.astype(f32).reshape(bsz, L, N_DIR, M2_HEADS)
    y = d_skip.astype(f32)[:, None] * xs
    for d in range(N_DIR):
        dt = jax.nn.softplus(dt_raw[:, :, d] + dt_bias[d].astype(f32))
        a = -jnp.exp(a_log[d].astype(f32))
        if d == 0:
            y = y + ssd_scan(xs, dt, a, bm, cm)
        else:
            y = y + jnp.flip(ssd_scan(jnp.flip(xs, 1), jnp.flip(dt, 1), a,
                                      jnp.flip(bm, 1), jnp.flip(cm, 1)), 1)
    return gated_rmsnorm(y.reshape(bsz, L, M2_INNER), z, norm_w)


def encoder_layer(x, norm1_w, w_in, lam_re, lam_im, log_dt, b_re, b_im, c_re, c_im,
                  d_s5, w_glu, b_glu, w_s5_out, conv_w, conv_b, dt_bias, a_log, d_m2,
                  m2_norm_w, w_m2_out, w_o, norm2_w, w_up, w_down):
    f32 = jnp.float32
    h = rmsnorm(x, norm1_w)
    proj = h @ w_in
    u = proj[..., OFF_U:OFF_Z]
    z = proj[..., OFF_Z:OFF_XBC]
    xbc = proj[..., OFF_XBC:OFF_DT]
    dt_raw = proj[..., OFF_DT:OFF_GATE]
    gates = jax.nn.sigmoid(proj[..., OFF_GATE:].astype(f32))
    s5 = s5_mixer(u, lam_re, lam_im, log_dt, b_re, b_im, c_re, c_im, d_s5, w_glu, b_glu)
    s5 = s5 @ w_s5_out.astype(f32)
    m2 = mamba2_mixer(z, xbc, dt_raw, conv_w, conv_b, dt_bias, a_log, d_m2, m2_norm_w)
    m2 = m2 @ w_m2_out.astype(f32)
    merged = gates[..., :D_MODEL] * s5 + gates[..., D_MODEL:] * m2
    x = x + (merged @ w_o.astype(f32)).astype(x.dtype)
    h = rmsnorm(x, norm2_w)
    x = x + (jnp.square(jax.nn.relu(h @ w_up)) @ w_down).astype(x.dtype)
    return x


def trunk(x, norm1_w, w_in, lam_re, lam_im, log_dt, b_re, b_im, c_re, c_im, d_s5, w_glu,
          b_glu, w_s5_out, conv_w, conv_b, dt_bias, a_log, d_m2, m2_norm_w, w_m2_out, w_o,
          norm2_w, w_up, w_down, final_norm_w):
    for i in range(DEPTH):
        x = encoder_layer(x, norm1_w[i], w_in[i], lam_re[i], lam_im[i], log_dt[i], b_re[i],
                          b_im[i], c_re[i], c_im[i], d_s5[i], w_glu[i], b_glu[i],
                          w_s5_out[i], conv_w[i], conv_b[i], dt_bias[i], a_log[i], d_m2[i],
                          m2_norm_w[i], w_m2_out[i], w_o[i], norm2_w[i], w_up[i], w_down[i])
    return rmsnorm(x, final_norm_w)


def setup_inputs(seed: int = 0) -> dict:
    key = jax.random.key(seed)
    ks = jax.random.split(key, 32)
    f32 = jnp.float32
    nrm = lambda k, shp, s: jax.random.normal(k, shp, f32) * s
    G, P = S5_GROUPS, S5_STATE
    lam_im_base = math.pi * jnp.arange(P, dtype=f32)
    dt_m2 = jnp.exp(jax.random.uniform(ks[16], (DEPTH, N_DIR, M2_HEADS), f32,
                                       math.log(1e-3), math.log(1e-1)))
    return {
        'x_prompt': nrm(ks[0], (BATCH, SEQ, D_MODEL), 1.0),
        'x_sample': nrm(ks[1], (DEC_BATCH, DEC_SEQ, D_MODEL), 1.0),
        'norm1_w': 1.0 + nrm(ks[2], (DEPTH, D_MODEL), 0.01),
        'w_in': nrm(ks[3], (DEPTH, D_MODEL, D_IN_PROJ), D_MODEL ** -0.5),
        'lam_re': -0.5 + nrm(ks[4], (DEPTH, N_DIR, G, P), 0.01),
        'lam_im': lam_im_base + nrm(ks[5], (DEPTH, N_DIR, G, P), 0.01),
        'log_dt': jax.random.uniform(ks[6], (DEPTH, N_DIR, G), f32,
                                     math.log(1e-3), math.log(1e-1)),
        'b_re': nrm(ks[7], (DEPTH, N_DIR, G, P, S5_GROUP), (2 * S5_GROUP) ** -0.5),
        'b_im': nrm(ks[8], (DEPTH, N_DIR, G, P, S5_GROUP), (2 * S5_GROUP) ** -0.5),
        'c_re': nrm(ks[9], (DEPTH, N_DIR, G, S5_GROUP, P), P ** -0.5),
        'c_im': nrm(ks[10], (DEPTH, N_DIR, G, S5_GROUP, P), P ** -0.5),
        'd_s5': nrm(ks[11], (DEPTH, S5_WIDTH), 1.0),
        'w_glu': nrm(ks[12], (DEPTH, S5_WIDTH, S5_WIDTH), S5_WIDTH ** -0.5),
        'b_glu': nrm(ks[13], (DEPTH, S5_WIDTH), 0.01),
        'w_s5_out': nrm(ks[14], (DEPTH, S5_WIDTH, D_MODEL), S5_WIDTH ** -0.5),
        'conv_w': nrm(ks[15], (DEPTH, M2_CONV, M2_CONV_DIM), M2_CONV ** -0.5),
        'conv_b': nrm(ks[17], (DEPTH, M2_CONV_DIM), 0.01),
        'dt_bias': dt_m2 + jnp.log(-jnp.expm1(-dt_m2)),
        'a_log': jnp.log(jax.random.uniform(ks[18], (DEPTH, N_DIR, M2_HEADS), f32, 1.0, 16.0)),
        'd_m2': 1.0 + nrm(ks[19], (DEPTH, M2_HEADS), 0.01),
        'm2_norm_w': 1.0 + nrm(ks[20], (DEPTH, M2_INNER), 0.01),
        'w_m2_out': nrm(ks[21], (DEPTH, M2_INNER, D_MODEL), M2_INNER ** -0.5),
        'w_o': nrm(ks[22], (DEPTH, D_MODEL, D_MODEL), D_MODEL ** -0.5),
        'norm2_w': 1.0 + nrm(ks[23], (DEPTH, D_MODEL), 0.01),
        'w_up': nrm(ks[24], (DEPTH, D_MODEL, D_FF), D_MODEL ** -0.5),
        'w_down': nrm(ks[25], (DEPTH, D_FF, D_MODEL), D_FF ** -0.5),
        'final_norm_w': 1.0 + nrm(ks[26], (D_MODEL,), 0.01),
    }


def reference(x_prompt, x_sample, norm1_w, w_in, lam_re, lam_im, log_dt, b_re, b_im, c_re,
              c_im, d_s5, w_glu, b_glu, w_s5_out, conv_w, conv_b, dt_bias, a_log, d_m2,
              m2_norm_w, w_m2_out, w_o, norm2_w, w_up, w_down, final_norm_w):
    y_prompt = trunk(x_prompt, norm1_w, w_in, lam_re, lam_im, log_dt, b_re, b_im, c_re, c_im,
                     d_s5, w_glu, b_glu, w_s5_out, conv_w, conv_b, dt_bias, a_log, d_m2,
                     m2_norm_w, w_m2_out, w_o, norm2_w, w_up, w_down, final_norm_w)
    y_sample = trunk(x_sample, norm1_w, w_in, lam_re, lam_im, log_dt, b_re, b_im, c_re, c_im,
                     d_s5, w_glu, b_glu, w_s5_out, conv_w, conv_b, dt_bias, a_log, d_m2,
                     m2_norm_w, w_m2_out, w_o, norm2_w, w_up, w_down, final_norm_w)
    return (y_prompt, y_sample)
```

```python
import numpy as np
from contextlib import ExitStack
import concourse.bass as bass
import concourse.mybir as mybir
from concourse.bass_utils import run_bass_kernel_spmd

F32 = mybir.dt.float32
BF16 = mybir.dt.bfloat16
AF = mybir.ActivationFunctionType
ALU = mybir.AluOpType
AX = mybir.AxisListType

D = 1024
DEPTH = 2
EPS = 1e-6
S5W, S5G, S5P = 768, 48, 64
M2I, M2H, M2P, M2G, M2N = 1536, 24, 64, 4, 128
XBC = 2560
OFF_U, OFF_Z, OFF_XBC, OFF_DT, OFF_GATE, DPROJ = 0, 768, 2304, 4864, 4912, 6960
DFF = 4096
PI = float(np.pi)
SUB = ""

C_ID, C_UT, C_LT, C_SU, C_SL, C_ONES, C_MF, C_MB, C_E = range(9)
NCONST = 9


def host_consts():
    r = np.arange(128)[:, None]
    t = np.arange(128)[None, :]
    cs = np.zeros((NCONST, 128, 128), np.float32)
    cs[C_ID] = (r == t)
    cs[C_UT] = (r <= t)
    cs[C_LT] = (r >= t)
    cs[C_SU] = (r < t)
    cs[C_SL] = (r > t)
    cs[C_ONES] = 1.0
    cs[C_MF] = ((t // 16) >= (r // 16))
    cs[C_MB] = ((t // 16) <= (r // 16))
    cs[C_E] = ((t % 16) == r)
    return np.ascontiguousarray(cs.transpose(1, 0, 2).reshape(128, NCONST * 128))


class Buf:
    __slots__ = ("w", "r", "excl")

    def __init__(self, excl=False):
        self.w = None
        self.r = {}
        self.excl = excl


class Eng:
    def __init__(self, K, name, h, self_sync):
        self.K, self.name, self.h, self.self_sync = K, name, h, self_sync
        self.sem = K.es.enter_context(K.nc.semaphore("e_" + name))
        self.sid = "e_" + name
        self.count = 0
        self.known = {}
        self.last = None
        self.dsems = []
        self.dcnt = []
        self.dpos = 0

    def wait(self, tok):
        if tok is None:
            return
        sid, sem, val = tok
        if self.known.get(sid, 0) >= val:
            return
        self.h.wait_ge(sem, val)
        self.known[sid] = val

    def issue(self, ins):
        self.count += 1
        ins.then_inc(self.sem, 1)
        tok = (self.sid, self.sem, self.count)
        if not self.self_sync:
            self.known[self.sid] = self.count
        self.last = tok
        return tok


class Kern:
    def __init__(self, nc, es, ndma=12):
        self.nc, self.es = nc, es
        self.pe = Eng(self, "pe", nc.tensor, False)
        self.act = Eng(self, "act", nc.scalar, True)
        self.dve = Eng(self, "dve", nc.vector, True)
        self.pool = Eng(self, "pool", nc.gpsimd, True)
        self.sp = Eng(self, "sp", nc.sync, False)
        self.engs = [self.pe, self.act, self.dve, self.pool, self.sp]
        for i in range(ndma):
            self.sp.dsems.append(es.enter_context(nc.semaphore("d_%d" % i)))
            self.sp.dcnt.append(0)
        self.dma_toks = [None] * ndma

    def _deps(self, eng, reads, writes):
        for b in reads:
            eng.wait(b.w)
        for b in writes:
            eng.wait(b.w)
            for t in b.r.values():
                eng.wait(t)

    def _mark(self, tok, reads, writes):
        for b in writes:
            b.w = tok
            b.r = {}
        for b in reads:
            b.r[tok[0]] = tok

    def op(self, eng, emit, reads=(), writes=()):
        ex = [b for b in reads if b.excl]
        if ex:
            reads = [b for b in reads if not b.excl]
            writes = list(writes) + ex
        self._deps(eng, reads, writes)
        tok = eng.issue(emit())
        self._mark(tok, reads, writes)
        return tok

    def dma(self, out, in_, reads=(), writes=(), **kw):
        eng = self.sp
        self._deps(eng, reads, writes)
        j = eng.dpos
        eng.dpos = (eng.dpos + 1) % len(eng.dsems)
        eng.wait(self.dma_toks[j])
        eng.dcnt[j] += 16
        eng.h.dma_start(out=out, in_=in_, **kw).then_inc(eng.dsems[j], 16)
        tok = ("d_%d" % j, eng.dsems[j], eng.dcnt[j])
        self.dma_toks[j] = tok
        self._mark(tok, reads, writes)
        return tok

    def barrier(self):
        toks = [e.last for e in self.engs if e.last is not None] + [t for t in self.dma_toks if t is not None]
        for e in self.engs:
            for t in toks:
                if t[0] == e.sid and not e.self_sync:
                    continue
                e.wait(t)


def build_program(NU, CPU, debug=False, LVL=9):
    NCH = NU * CPU
    nc = bass.Bass("TRN2", target_bir_lowering=False)
    dt_in = lambda n, s, d=F32: nc.dram_tensor(n, list(s), d, kind="ExternalInput").ap()
    kind_dbg = "ExternalOutput" if debug else "Internal"
    dt_sc = lambda n, s, d=F32: nc.dram_tensor(n, list(s), d, kind=kind_dbg).ap()

    x_in = dt_in("x", [NCH, 128, D])
    seqmask_in = dt_in("seqmask", [128, NCH + 1])
    consts_in = dt_in("consts", [128, NCONST * 128])
    P = {}
    for n, s in [("norm1_w", (DEPTH, D)), ("w_in", (DEPTH, D, DPROJ)), ("lam_re", (DEPTH, 2, S5G, S5P)),
                 ("lam_im", (DEPTH, 2, S5G, S5P)), ("log_dt", (DEPTH, 2, S5G)),
                 ("b_re", (DEPTH, 2, S5G, S5P, 16)), ("b_im", (DEPTH, 2, S5G, S5P, 16)),
                 ("c_re", (DEPTH, 2, S5G, 16, S5P)), ("c_im", (DEPTH, 2, S5G, 16, S5P)),
                 ("d_s5", (DEPTH, S5W)), ("w_glu", (DEPTH, S5W, S5W)), ("b_glu", (DEPTH, S5W)),
                 ("w_s5_out", (DEPTH, S5W, D)), ("conv_w", (DEPTH, 4, XBC)), ("conv_b", (DEPTH, XBC)),
                 ("dt_bias", (DEPTH, 2, M2H)), ("a_log", (DEPTH, 2, M2H)), ("d_m2", (DEPTH, M2H)),
                 ("m2_norm_w", (DEPTH, M2I)), ("w_m2_out", (DEPTH, M2I, D)), ("w_o", (DEPTH, D, D)),
                 ("norm2_w", (DEPTH, D)), ("w_up", (DEPTH, D, DFF)), ("w_down", (DEPTH, DFF, D)),
                 ("final_norm_w", (D,))]:
        P[n] = dt_in(n, s)
    y_out = nc.dram_tensor("y", [NCH, 128, D], F32, kind="ExternalOutput").ap()

    s_u = dt_sc("s_u", [NCH, 128, S5W], BF16)
    s_z = dt_sc("s_z", [NCH, 128, M2I], BF16)
    s_dt = dt_sc("s_dt", [NCH, 128, 48], F32)
    s_gate = dt_sc("s_gate", [NCH, 128, 2 * D], BF16)
    s_xbc = dt_sc("s_xbc", [NCH, 128, 20, 128], BF16)
    s_ys5 = dt_sc("s_ys5", [NCH, 128, S5W], F32)
    s_ym2 = dt_sc("s_ym2", [NCH, 128, M2I], F32)
    s_x1 = dt_sc("s_x1", [NCH, 128, D], F32)
    s_x2 = dt_sc("s_x2", [NCH, 128, D], F32)
    s_dbg1 = dt_sc("s_dbg1", [NCH, 128, D], F32)
    s_dbg2 = dt_sc("s_dbg2", [NCH, 128, D], F32)
    B_u = [Buf() for _ in range(NCH)]
    B_z = [Buf() for _ in range(NCH)]
    B_dt = [Buf() for _ in range(NCH)]
    B_gate = [Buf() for _ in range(NCH)]
    B_xbc = [Buf() for _ in range(NCH)]
    B_ys5 = [Buf() for _ in range(NCH)]
    B_ym2 = [Buf() for _ in range(NCH)]
    B_x1 = [Buf() for _ in range(NCH)]
    B_x2 = [Buf() for _ in range(NCH)]

    with ExitStack() as es:
        K = Kern(nc, es)
        pe, act, dve, pool = K.pe, K.act, K.dve, K.pool
        uid = [0]

        def sb(es_, shape, dtype=F32, name=None):
            uid[0] += 1
            t = es_.enter_context(nc.sbuf_tensor("%s_%d" % (name or "t", uid[0]), list(shape), dtype))
            return t, Buf()

        psum = es.enter_context(nc.psum_tensor("psum", [128, 8, 512], F32))
        PB = [Buf(excl=True) for _ in range(8)]
        consts, Bc = sb(es, [128, NCONST, 128], F32, "consts")
        ident_bf, Bidb = sb(es, [128, 128], BF16, "identbf")
        seqm, Bsm = sb(es, [128, NCH + 1], F32, "seqm")
        K.dma(consts[:].rearrange("p a b -> p (a b)"), consts_in[:, :], writes=[Bc])
        K.dma(seqm[:], seqmask_in[:, :], writes=[Bsm])
        K.op(dve, lambda: nc.vector.tensor_copy(out=ident_bf[:], in_=consts[:, C_ID, :]), [Bc], [Bidb])
        CN = lambda i: consts[:, i, :]

        def bank_bf(b):
            return psum[:, b, :].bitcast(BF16)

        def load_weight(es_, src2d, KT, C, name, stage, scale_engs, pre=None):
            wt, Bw = pre if pre is not None else sb(es_, [128, KT, C], BF16, name)
            CB = 1024
            i = 0
            for kt in range(KT):
                for c0 in range(0, C, CB):
                    c1 = min(C, c0 + CB)
                    st, Bs = stage[i % len(stage)]
                    K.dma(st[:, 0:c1 - c0], src2d[kt * 128:(kt + 1) * 128, c0:c1], writes=[Bs])
                    e = scale_engs[i % len(scale_engs)]
                    if e is act:
                        K.op(act, lambda: nc.scalar.activation(out=wt[:, kt, c0:c1], in_=st[:, 0:c1 - c0], func=AF.Copy), [Bs], [Bw])
                    elif e is dve:
                        K.op(dve, lambda: nc.vector.tensor_copy(out=wt[:, kt, c0:c1], in_=st[:, 0:c1 - c0]), [Bs], [Bw])
                    else:
                        K.op(pool, lambda: nc.gpsimd.tensor_copy(out=wt[:, kt, c0:c1], in_=st[:, 0:c1 - c0]), [Bs], [Bw])
                    i += 1
            return wt, Bw

        def bcast_row(es_, src1d, n, name):
            t, B = sb(es_, [128, n], F32, name)
            K.dma(t[:], src1d.partition_broadcast(128), writes=[B])
            return t, B

        def rms_rstd(xt, Bx, n, ss, Bss, junk, Bj):
            K.op(act, lambda: nc.scalar.activation(out=junk, in_=xt, func=AF.Square, accum_out=ss[:, 0:1]), [Bx], [Bj, Bss])
            K.op(dve, lambda: nc.vector.tensor_scalar(out=ss[:, 0:1], in0=ss[:, 0:1], scalar1=1.0 / n, scalar2=EPS, op0=ALU.mult, op1=ALU.add), [Bss], [Bss])
            K.op(act, lambda: nc.scalar.activation(out=ss[:, 0:1], in_=ss[:, 0:1], func=AF.Sqrt), [Bss], [Bss])
            K.op(dve, lambda: nc.vector.reciprocal(out=ss[:, 0:1], in_=ss[:, 0:1]), [Bss], [Bss])

        def transpose_to(dst3, Bd, src, Bs_, ntile, bank, dtype_bf=True, evac=None):
            done = 0
            while done < ntile:
                n = min(8, ntile - done)
                b = bank + (done // 8)
                pb = bank_bf(b)
                for j in range(n):
                    jj = done + j
                    K.op(pe, lambda: nc.tensor.transpose(pb[:, j * 128:(j + 1) * 128], src[:, jj * 128:(jj + 1) * 128], ident_bf[:]), [Bs_, Bidb], [PB[b]])
                e = evac or act
                if e is act:
                    K.op(act, lambda: nc.scalar.activation(out=dst3[:, done:done + n, :], in_=pb[:, 0:n * 128].rearrange("p (a b) -> p a b", b=128), func=AF.Copy), [PB[b]], [Bd])
                else:
                    K.op(dve, lambda: nc.vector.tensor_copy(out=dst3[:, done:done + n, :], in_=pb[:, 0:n * 128].rearrange("p (a b) -> p a b", b=128)), [PB[b]], [Bd])
                done += n

        def mm_tok(out_bank, lhsT3, Bl, KT, w3, Bw, c0, c1):
            for kt in range(KT):
                K.op(pe, lambda: nc.tensor.matmul(psum[:, out_bank, 0:c1 - c0], lhsT=lhsT3[:, kt, :], rhs=w3[:, kt, c0:c1], start=(kt == 0), stop=(kt == KT - 1)), [Bl, Bw], [PB[out_bank]])

        def run_gens(gens):
            while gens:
                for e_ in list(gens):
                    if e_[1] > 0:
                        e_[1] -= 1
                        continue
                    try:
                        next(e_[0])
                    except StopIteration:
                        gens.remove(e_)

        for L in range(DEPTH):
            xsrc, Bxsrc = (x_in, None) if L == 0 else (s_x2, B_x2)
            with ExitStack() as ph:
                stage = [sb(ph, [128, 1024], F32, "stg") for _ in range(3)]
                Win, BWin = load_weight(ph, P["w_in"][L], 8, DPROJ, "win", stage, [act, dve, pool])
                w1b, Bw1b = bcast_row(ph, P["norm1_w"][L], D, "w1b")
                dsb, Bdsb = bcast_row(ph, P["d_s5"][L], S5W, "dsb")
                NB = 2
                oy = [sb(ph, [128, S5W], F32, "oy") for _ in range(NB)]
                xt = [sb(ph, [128, D], F32, "x") for _ in range(NB)]
                junk = sb(ph, [128, D], BF16, "junk")
                ss = [sb(ph, [128, 1], F32, "ss") for _ in range(NB)]
                hb = [sb(ph, [128, D], BF16, "h") for _ in range(NB)]
                hT = [sb(ph, [128, 8, 128], BF16, "hT") for _ in range(NB)]
                ou = [sb(ph, [128, S5W], BF16, "ou") for _ in range(NB)]
                oz = [sb(ph, [128, M2I], BF16, "oz") for _ in range(NB)]
                odt = [sb(ph, [128, 48], F32, "odt") for _ in range(NB)]
                og = [sb(ph, [128, 2 * D], BF16, "og") for _ in range(NB)]
                ox = [sb(ph, [128, 20, 128], BF16, "ox") for _ in range(NB)]

                def p0_load(c):
                    t, B = xt[c % NB]
                    K.dma(t[:], xsrc[c], reads=([Bxsrc[c]] if Bxsrc else []), writes=[B])

                p0_load(0)
                bk = [0]

                def nb():
                    bk[0] = (bk[0] + 1) % 8
                    return bk[0]

                def p0_head(c):
                    i = c % NB
                    (x_, Bx), (ss_, Bss), (h_, Bh), (hT_, BhT) = xt[i], ss[i], hb[i], hT[i]
                    rms_rstd(x_[:], Bx, D, ss_, Bss, junk[0][:], junk[1])
                    K.op(dve, lambda: nc.vector.scalar_tensor_tensor(out=h_[:], in0=x_[:], scalar=ss_[:, 0:1], in1=w1b[:], op0=ALU.mult, op1=ALU.mult), [Bx, Bss, Bw1b], [Bh])
                    transpose_to(hT_, BhT, h_, Bh, 8, nb())

                p0_head(0)
                for c in range(NCH):
                    if c + 1 < NCH:
                        p0_load(c + 1)
                    i = c % NB
                    (x_, Bx), (ss_, Bss), (h_, Bh), (hT_, BhT) = xt[i], ss[i], hb[i], hT[i]
                    (u_, Bu_), (z_, Bz_), (d_, Bd_), (g_, Bg_), (xo_, Bxo_) = ou[i], oz[i], odt[i], og[i], ox[i]
                    blocks = [(OFF_U, 512, u_, 0, Bu_, "c"), (OFF_U + 512, 256, u_, 512, Bu_, "c"),
                              (OFF_Z, 512, z_, 0, Bz_, "c"), (OFF_Z + 512, 512, z_, 512, Bz_, "c"), (OFF_Z + 1024, 512, z_, 1024, Bz_, "c"),
                              (OFF_DT, 48, d_, 0, Bd_, "c")] + [(OFF_GATE + 512 * q, 512, g_, 512 * q, Bg_, "s") for q in range(4)]
                    for bi, (c0, n, dst, d0, Bdst, kind) in enumerate(blocks):
                        b = nb()
                        mm_tok(b, hT_, BhT, 8, Win, BWin, c0, c0 + n)
                        if False:
                            K.op(dve, lambda: nc.vector.tensor_tensor(out=oy[i][0][:, d0:d0 + n], in0=psum[:, b, 0:n], in1=dsb[:, d0:d0 + n], op=ALU.mult), [PB[b], Bdsb], [oy[i][1]])
                        if kind == "s":
                            K.op(act, lambda: nc.scalar.activation(out=dst[:, d0:d0 + n], in_=psum[:, b, 0:n], func=AF.Sigmoid), [PB[b]], [Bdst])
                        elif bi % 2 == 0:
                            K.op(dve, lambda: nc.vector.tensor_copy(out=dst[:, d0:d0 + n], in_=psum[:, b, 0:n]), [PB[b]], [Bdst])
                        else:
                            K.op(act, lambda: nc.scalar.activation(out=dst[:, d0:d0 + n], in_=psum[:, b, 0:n], func=AF.Copy), [PB[b]], [Bdst])
                    if c + 1 < NCH:
                        p0_head(c + 1)
                    for q in range(5):
                        b = nb()
                        for j in range(4):
                            col = OFF_XBC + (q * 4 + j) * 128
                            for kt in range(8):
                                K.op(pe, lambda: nc.tensor.matmul(psum[:, b, j * 128:(j + 1) * 128], lhsT=Win[:, kt, col:col + 128], rhs=hT_[:, kt, :], start=(kt == 0), stop=(kt == 7)), [BWin, BhT], [PB[b]])
                        if q % 2 == 0:
                            K.op(dve, lambda: nc.vector.tensor_copy(out=xo_[:, q * 4:q * 4 + 4, :], in_=psum[:, b, :].rearrange("p (a b) -> p a b", b=128)), [PB[b]], [Bxo_])
                        else:
                            K.op(act, lambda: nc.scalar.activation(out=xo_[:, q * 4:q * 4 + 4, :], in_=psum[:, b, :].rearrange("p (a b) -> p a b", b=128), func=AF.Copy), [PB[b]], [Bxo_])
                    K.dma(s_u[c], u_[:], reads=[Bu_], writes=[B_u[c]])
                    if False:
                        K.dma(s_ys5[c], oy[i][0][:], reads=[oy[i][1]], writes=[B_ys5[c]])
                    K.dma(s_z[c], z_[:], reads=[Bz_], writes=[B_z[c]])
                    K.dma(s_dt[c], d_[:], reads=[Bd_], writes=[B_dt[c]])
                    K.dma(s_gate[c], g_[:], reads=[Bg_], writes=[B_gate[c]])
                    K.dma(s_xbc[c], xo_[:], reads=[Bxo_], writes=[B_xbc[c]])
                K.barrier()

            for d in range(2):
                if LVL < 1 + d:
                    continue
                with ExitStack() as ph:
                    WY, BWY = sb(ph, [128, 48, 128], BF16, "WY")
                    WVr, BWVr = sb(ph, [128, 48, 64], BF16, "WVr")
                    WVi, BWVi = sb(ph, [128, 48, 64], BF16, "WVi")
                    WCr, BWCr = sb(ph, [128, 48, 128], BF16, "WCr")
                    WCi, BWCi = sb(ph, [128, 48, 128], BF16, "WCi")
                    K.op(dve, lambda: nc.vector.memset(WCr[:], 0.0), [], [BWCr])
                    K.op(dve, lambda: nc.vector.memset(WCi[:], 0.0), [], [BWCi])
                    A8, BA8 = sb(ph, [64, 2, 2, 48], F32, "A8")
                    with ExitStack() as tg:
                        Q = 64
                        def small(shape, name, dtype=F32):
                            return sb(tg, shape, dtype, name)
                        ln_, Bln = small([48, 2, 64], "lamn")
                        K.dma(ln_[:, 0, :], P["lam_re"][L, d], writes=[Bln])
                        K.dma(ln_[:, 1, :], P["lam_im"][L, d], writes=[Bln])
                        lam, Blam = small([Q, 2, 48], "lam")
                        for r_ in range(2):
                            K.op(pe, lambda: nc.tensor.transpose(psum[0:Q, 0, r_ * 48:(r_ + 1) * 48], ln_[:, r_, :], consts[0:48, C_ID, 0:48]), [Bln, Bc], [PB[0]])
                        K.op(dve, lambda: nc.vector.tensor_copy(out=lam[:].rearrange("p a b -> p (a b)"), in_=psum[0:Q, 0, 0:96]), [PB[0]], [Blam])
                        dl, Bdl = small([Q, 48], "dl")
                        K.dma(dl[:], P["log_dt"][L, d].partition_broadcast(Q), writes=[Bdl])
                        K.op(act, lambda: nc.scalar.activation(out=dl[:], in_=dl[:], func=AF.Exp), [Bdl], [Bdl])
                        lrd, Blrd = small([Q, 48], "lrd")
                        lid, Blid = small([Q, 48], "lid")
                        K.op(dve, lambda: nc.vector.tensor_tensor(out=lrd[:], in0=lam[:, 0, :], in1=dl[:], op=ALU.mult), [Blam, Bdl], [Blrd])
                        K.op(dve, lambda: nc.vector.tensor_tensor(out=lid[:], in0=lam[:, 1, :], in1=dl[:], op=ALU.mult), [Blam, Bdl], [Blid])
                        NE = 9
                        ea, Bea = small([Q, NE, 48], "ea")
                        eb, Beb = small([Q, NE, 48], "eb")
                        ec, Bec = small([Q, NE, 48], "ec")
                        ei, Bei = small([Q, NE, 48], "ei", mybir.dt.int32)
                        ef, Bef = small([Q, NE, 48], "ef")

                        def sin_of(dst, Bdst, n, shift):
                            K.op(dve, lambda: nc.vector.tensor_scalar(out=ec[:, 0:n, :], in0=eb[:, 0:n, :], scalar1=float(shift), scalar2=None, op0=ALU.add), [Beb], [Bec])
                            K.op(dve, lambda: nc.vector.tensor_copy(out=ei[:, 0:n, :], in_=ec[:, 0:n, :]), [Bec], [Bei])
                            K.op(dve, lambda: nc.vector.tensor_copy(out=ef[:, 0:n, :], in_=ei[:, 0:n, :]), [Bei], [Bef])
                            K.op(dve, lambda: nc.vector.tensor_tensor(out=ec[:, 0:n, :], in0=ec[:, 0:n, :], in1=ef[:, 0:n, :], op=ALU.subtract), [Bec, Bef], [Bec])
                            K.op(dve, lambda: nc.vector.tensor_scalar(out=ef[:, 0:n, :], in0=ec[:, 0:n, :], scalar1=0.5, scalar2=None, op0=ALU.is_gt), [Bec], [Bef])
                            K.op(dve, lambda: nc.vector.tensor_tensor(out=ec[:, 0:n, :], in0=ec[:, 0:n, :], in1=ef[:, 0:n, :], op=ALU.subtract), [Bec, Bef], [Bec])
                            K.op(dve, lambda: nc.vector.tensor_scalar(out=ef[:, 0:n, :], in0=ec[:, 0:n, :], scalar1=-0.5, scalar2=None, op0=ALU.is_lt), [Bec], [Bef])
                            K.op(dve, lambda: nc.vector.tensor_tensor(out=ec[:, 0:n, :], in0=ec[:, 0:n, :], in1=ef[:, 0:n, :], op=ALU.add), [Bec, Bef], [Bec])
                            K.op(act, lambda: nc.scalar.activation(out=dst[:, 0:n, :], in_=ec[:, 0:n, :], func=AF.Sin, scale=2.0 * PI), [Bec], [Bdst])

                        def powers(exps, sign, name):
                            n = len(exps)
                            pr, Bpr = small([Q, n, 48], name + "r")
                            pi_, Bpi = small([Q, n, 48], name + "i")
                            for j, e in enumerate(exps):
                                K.op(dve, lambda: nc.vector.tensor_scalar(out=ea[:, j, :], in0=lrd[:], scalar1=float(sign * e), scalar2=None, op0=ALU.mult), [Blrd], [Bea])
                                K.op(dve, lambda: nc.vector.tensor_scalar(out=eb[:, j, :], in0=lid[:], scalar1=float(e / (2.0 * PI)), scalar2=None, op0=ALU.mult), [Blid], [Beb])
                            K.op(act, lambda: nc.scalar.activation(out=ea[:, 0:n, :], in_=ea[:, 0:n, :], func=AF.Exp), [Bea], [Bea])
                            sin_of(pi_, Bpi, n, 0.0)
                            sin_of(pr, Bpr, n, 0.25)
                            K.op(dve, lambda: nc.vector.tensor_tensor(out=pr[:], in0=pr[:], in1=ea[:, 0:n, :], op=ALU.mult), [Bea, Bpr], [Bpr])
                            K.op(dve, lambda: nc.vector.scalar_tensor_tensor(out=pi_[:], in0=pi_[:], scalar=float(sign), in1=ea[:, 0:n, :], op0=ALU.mult, op1=ALU.mult), [Bea, Bpi], [Bpi])
                            return (pr, Bpr), (pi_, Bpi)

                        T8 = list(range(8))
                        if d == 0:
                            eX, eZ, eV, eC = T8, T8, [7 - t for t in T8], [t + 1 for t in T8]
                        else:
                            eX, eZ, eV, eC = [7 - t for t in T8], [7 - t for t in T8], T8, [8 - t for t in T8]
                        (a1r, Ba1r), (a1i, Ba1i) = powers([1, 8], 1, "a1")
                        K.op(dve, lambda: nc.vector.tensor_copy(out=A8[:, 0, 0, :], in_=a1r[:, 1, :]), [Ba1r], [BA8])
                        K.op(dve, lambda: nc.vector.tensor_copy(out=A8[:, 0, 1, :], in_=a1r[:, 1, :]), [Ba1r], [BA8])
                        K.op(dve, lambda: nc.vector.tensor_scalar(out=A8[:, 1, 0, :], in0=a1i[:, 1, :], scalar1=-1.0, scalar2=None, op0=ALU.mult), [Ba1i], [BA8])
                        K.op(dve, lambda: nc.vector.tensor_copy(out=A8[:, 1, 1, :], in_=a1i[:, 1, :]), [Ba1i], [BA8])
                        qt, Bqt = small([Q, 6, 48], "qt")
                        lr_, li_ = lam[:, 0, :], lam[:, 1, :]
                        K.op(dve, lambda: nc.vector.tensor_tensor(out=qt[:, 0, :], in0=lr_, in1=lr_, op=ALU.mult), [Blam], [Bqt])
                        K.op(dve, lambda: nc.vector.tensor_tensor(out=qt[:, 1, :], in0=li_, in1=li_, op=ALU.mult), [Blam], [Bqt])
                        K.op(dve, lambda: nc.vector.tensor_tensor(out=qt[:, 0, :], in0=qt[:, 0, :], in1=qt[:, 1, :], op=ALU.add), [Bqt], [Bqt])
                        K.op(dve, lambda: nc.vector.reciprocal(out=qt[:, 0, :], in_=qt[:, 0, :]), [Bqt], [Bqt])
                        K.op(dve, lambda: nc.vector.tensor_scalar(out=qt[:, 1, :], in0=a1r[:, 0, :], scalar1=-1.0, scalar2=None, op0=ALU.add), [Ba1r], [Bqt])
                        K.op(dve, lambda: nc.vector.tensor_tensor(out=qt[:, 2, :], in0=qt[:, 1, :], in1=lr_, op=ALU.mult), [Bqt, Blam], [Bqt])
                        K.op(dve, lambda: nc.vector.tensor_tensor(out=qt[:, 3, :], in0=a1i[:, 0, :], in1=li_, op=ALU.mult), [Ba1i, Blam], [Bqt])
                        K.op(dve, lambda: nc.vector.tensor_tensor(out=qt[:, 2, :], in0=qt[:, 2, :], in1=qt[:, 3, :], op=ALU.add), [Bqt], [Bqt])
                        K.op(dve, lambda: nc.vector.tensor_tensor(out=qt[:, 2, :], in0=qt[:, 2, :], in1=qt[:, 0, :], op=ALU.mult), [Bqt], [Bqt])
                        K.op(dve, lambda: nc.vector.tensor_tensor(out=qt[:, 3, :], in0=a1i[:, 0, :], in1=lr_, op=ALU.mult), [Ba1i, Blam], [Bqt])
                        K.op(dve, lambda: nc.vector.tensor_tensor(out=qt[:, 4, :], in0=qt[:, 1, :], in1=li_, op=ALU.mult), [Bqt, Blam], [Bqt])
                        K.op(dve, lambda: nc.vector.tensor_tensor(out=qt[:, 3, :], in0=qt[:, 3, :], in1=qt[:, 4, :], op=ALU.subtract), [Bqt], [Bqt])
                        K.op(dve, lambda: nc.vector.tensor_tensor(out=qt[:, 3, :], in0=qt[:, 3, :], in1=qt[:, 0, :], op=ALU.mult), [Bqt], [Bqt])
                        qr_b = qt[:, 2, :].unsqueeze(2).to_broadcast([Q, 48, 16])
                        qi_b = qt[:, 3, :].unsqueeze(2).to_broadcast([Q, 48, 16])
                        Bn, BBn = small([Q, 2, 48, 16], "Bn")
                        K.dma(Bn[:, 0], P["b_re"][L, d].rearrange("g p c -> p g c"), writes=[BBn])
                        K.dma(Bn[:, 1], P["b_im"][L, d].rearrange("g p c -> p g c"), writes=[BBn])
                        Bb, BBb = small([Q, 2, 48, 16], "Bbar")
                        tq, Btq = small([Q, 48, 16], "tq")
                        K.op(dve, lambda: nc.vector.tensor_tensor(out=Bb[:, 0], in0=Bn[:, 0], in1=qr_b, op=ALU.mult), [BBn, Bqt], [BBb])
                        K.op(dve, lambda: nc.vector.tensor_tensor(out=tq[:], in0=Bn[:, 1], in1=qi_b, op=ALU.mult), [BBn, Bqt], [Btq])
                        K.op(dve, lambda: nc.vector.tensor_tensor(out=Bb[:, 0], in0=Bb[:, 0], in1=tq[:], op=ALU.subtract), [BBb, Btq], [BBb])
                        K.op(dve, lambda: nc.vector.tensor_tensor(out=Bb[:, 1], in0=Bn[:, 1], in1=qr_b, op=ALU.mult), [BBn, Bqt], [BBb])
                        K.op(dve, lambda: nc.vector.tensor_tensor(out=tq[:], in0=Bn[:, 0], in1=qi_b, op=ALU.mult), [BBn, Bqt], [Btq])
                        K.op(dve, lambda: nc.vector.tensor_tensor(out=Bb[:, 1], in0=Bb[:, 1], in1=tq[:], op=ALU.add), [BBb, Btq], [BBb])
                        Cn, BCn = small([128, 2, 6, 64], "Cn")
                        K.dma(Cn[:, 0], P["c_re"][L, d].rearrange("(gt gi) c p -> (gi c) gt p", gi=8), writes=[BCn])
                        K.dma(Cn[:, 1], P["c_im"][L, d].rearrange("(gt gi) c p -> (gi c) gt p", gi=8), writes=[BCn])
                        Cc, BCc = small([Q, 2, 48, 16], "Cc")
                        for r_ in range(2):
                            for gt in range(6):
                                b = 1 + gt // 4
                                K.op(pe, lambda: nc.tensor.transpose(psum[0:Q, b, (gt % 4) * 128:(gt % 4 + 1) * 128], Cn[:, r_, gt, :], CN(C_ID)), [BCn, Bc], [PB[b]])
                            K.op(dve, lambda: nc.vector.tensor_copy(out=Cc[:, r_, 0:32, :].rearrange("p g c -> p (g c)"), in_=psum[0:Q, 1, :]), [PB[1]], [BCc])
                            K.op(dve, lambda: nc.vector.tensor_copy(out=Cc[:, r_, 32:48, :].rearrange("p g c -> p (g c)"), in_=psum[0:Q, 2, 0:256]), [PB[2]], [BCc])
                        (pXr, BpXr), (pXi, BpXi) = powers(eX, -1, "pX")
                        (pZr, BpZr), (pZi, BpZi) = powers(eZ, 1, "pZ")
                        (pVr, BpVr), (pVi, BpVi) = powers(eV, 1, "pV")
                        (pCr, BpCr), (pCi, BpCi) = powers(eC, 1, "pC")
                        GH = 24
                        tA, BtA = small([Q, GH, 8, 16], "tA")
                        tB, BtB = small([Q, GH, 8, 16], "tB")
                        Xr, BXr = small([Q, GH, 8, 16], "Xr")
                        Xi, BXi = small([Q, GH, 8, 16], "Xi")
                        Zr, BZr = small([Q, GH, 8, 16], "Zr")
                        Zi, BZi = small([Q, GH, 8, 16], "Zi")

                        def cmul(outr, Boutr, outi, Bouti, pw_r, Bpw_r, pw_i, Bpw_i, M, BM, g0, neg_im=False):
                            pr_b = pw_r[:, :, g0:g0 + GH].rearrange("p t g -> p g t").unsqueeze(3).to_broadcast([Q, GH, 8, 16])
                            pi_b = pw_i[:, :, g0:g0 + GH].rearrange("p t g -> p g t").unsqueeze(3).to_broadcast([Q, GH, 8, 16])
                            mr_b = M[:, 0, g0:g0 + GH, :].unsqueeze(2).to_broadcast([Q, GH, 8, 16])
                            mi_b = M[:, 1, g0:g0 + GH, :].unsqueeze(2).to_broadcast([Q, GH, 8, 16])
                            K.op(dve, lambda: nc.vector.tensor_tensor(out=tA[:], in0=pr_b, in1=mr_b, op=ALU.mult), [Bpw_r, BM], [BtA])
                            K.op(dve, lambda: nc.vector.tensor_tensor(out=tB[:], in0=pi_b, in1=mi_b, op=ALU.mult), [Bpw_i, BM], [BtB])
                            K.op(dve, lambda: nc.vector.tensor_tensor(out=outr, in0=tA[:], in1=tB[:], op=ALU.subtract), [BtA, BtB], [Boutr])
                            K.op(dve, lambda: nc.vector.tensor_tensor(out=tA[:], in0=pr_b, in1=mi_b, op=ALU.mult), [Bpw_r, BM], [BtA])
                            K.op(dve, lambda: nc.vector.tensor_tensor(out=tB[:], in0=pi_b, in1=mr_b, op=ALU.mult), [Bpw_i, BM], [BtB])
                            if neg_im:
                                K.op(dve, lambda: nc.vector.scalar_tensor_tensor(out=outi, in0=tA[:], scalar=-1.0, in1=tB[:], op0=ALU.mult, op1=ALU.subtract), [BtA, BtB], [Bouti])
                            else:
                                K.op(dve, lambda: nc.vector.tensor_tensor(out=outi, in0=tA[:], in1=tB[:], op=ALU.add), [BtA, BtB], [Bouti])

                        MK = CN(C_MF) if d == 0 else CN(C_MB)
                        for g0 in (0, GH):
                            cmul(Xr[:], BXr, Xi[:], BXi, pXr, BpXr, pXi, BpXi, Bb, BBb, g0)
                            cmul(Zr[:], BZr, Zi[:], BZi, pZr, BpZr, pZi, BpZi, Cc, BCc, g0, neg_im=True)
                            for q4 in range(GH // 4):
                                b = 3 + (q4 % 2)
                                for j in range(4):
                                    g = q4 * 4 + j
                                    K.op(pe, lambda: nc.tensor.matmul(psum[:, b, j * 128:(j + 1) * 128], lhsT=Xr[:, g].rearrange("p t c -> p (t c)"), rhs=Zr[:, g].rearrange("p t c -> p (t c)"), start=True, stop=False), [BXr, BZr], [PB[b]])
                                    K.op(pe, lambda: nc.tensor.matmul(psum[:, b, j * 128:(j + 1) * 128], lhsT=Xi[:, g].rearrange("p t c -> p (t c)"), rhs=Zi[:, g].rearrange("p t c -> p (t c)"), start=False, stop=True), [BXi, BZi], [PB[b]])
                                K.op(dve, lambda: nc.vector.tensor_tensor(out=WY[:, g0 + q4 * 4:g0 + q4 * 4 + 4, :], in0=psum[:, b, :].rearrange("p (a b) -> p a b", b=128), in1=MK.unsqueeze(1).to_broadcast([128, 4, 128]), op=ALU.mult), [PB[b], Bc], [BWY])
                            cmul(Xr[:], BXr, Xi[:], BXi, pVr, BpVr, pVi, BpVi, Bb, BBb, g0)
                            for (src, Bsrc, dstW, BdstW) in ((Xr, BXr, WVr, BWVr), (Xi, BXi, WVi, BWVi)):
                                for q8 in range(GH // 8):
                                    b = 5 + (q8 % 2)
                                    for j in range(8):
                                        g = q8 * 8 + j
                                        K.op(pe, lambda: nc.tensor.transpose(psum[:, b, j * 64:(j + 1) * 64], src[:, g].rearrange("p t c -> p (t c)"), consts[0:Q, C_ID, 0:Q]), [Bsrc, Bc], [PB[b]])
                                    K.op(act, lambda: nc.scalar.activation(out=dstW[:, g0 + q8 * 8:g0 + q8 * 8 + 8, :], in_=psum[:, b, :].rearrange("p (a b) -> p a b", b=64), func=AF.Copy), [PB[b]], [BdstW])
                            pr_b = pCr[:, :, g0:g0 + GH].rearrange("p t g -> p g t").unsqueeze(3).to_broadcast([Q, GH, 8, 16])
                            pi_b = pCi[:, :, g0:g0 + GH].rearrange("p t g -> p g t").unsqueeze(3).to_broadcast([Q, GH, 8, 16])
                            mr_b = Cc[:, 0, g0:g0 + GH, :].unsqueeze(2).to_broadcast([Q, GH, 8, 16])
                            mi_b = Cc[:, 1, g0:g0 + GH, :].unsqueeze(2).to_broadcast([Q, GH, 8, 16])
                            wcr_v = WCr[0:Q, g0:g0 + GH, :].rearrange("p g (t c) -> p g t c", c=16)
                            wci_v = WCi[0:Q, g0:g0 + GH, :].rearrange("p g (t c) -> p g t c", c=16)
                            K.op(dve, lambda: nc.vector.tensor_tensor(out=tA[:], in0=pr_b, in1=mr_b, op=ALU.mult), [BpCr, BCc], [BtA])
                            K.op(dve, lambda: nc.vector.tensor_tensor(out=tB[:], in0=pi_b, in1=mi_b, op=ALU.mult), [BpCi, BCc], [BtB])
                            K.op(dve, lambda: nc.vector.tensor_tensor(out=wcr_v, in0=tA[:], in1=tB[:], op=ALU.subtract), [BtA, BtB], [BWCr])
                            K.op(dve, lambda: nc.vector.tensor_tensor(out=tA[:], in0=pr_b, in1=mi_b, op=ALU.mult), [BpCr, BCc], [BtA])
                            K.op(dve, lambda: nc.vector.tensor_tensor(out=tB[:], in0=pi_b, in1=mr_b, op=ALU.mult), [BpCi, BCc], [BtB])
                            K.op(dve, lambda: nc.vector.scalar_tensor_tensor(out=wci_v, in0=tA[:], scalar=-1.0, in1=tB[:], op0=ALU.mult, op1=ALU.subtract), [BtA, BtB], [BWCi])
                        if d == 0:
                            dn, Bdn = small([48, 16], "dn")
                            dT, BdT = small([16, 48], "dT")
                            dsk, Bdsk = small([128, 48], "dsk")
                            K.dma(dn[:], P["d_s5"][L].rearrange("(g c) -> g c", c=16), writes=[Bdn])
                            K.op(pe, lambda: nc.tensor.transpose(psum[0:16, 7, 0:48], dn[:], consts[0:48, C_ID, 0:48]), [Bdn, Bc], [PB[7]])
                            K.op(dve, lambda: nc.vector.tensor_copy(out=dT[:], in_=psum[0:16, 7, 0:48]), [PB[7]], [BdT])
                            K.op(pe, lambda: nc.tensor.matmul(psum[:, 7, 64:112], lhsT=consts[0:16, C_E, :], rhs=dT[:], start=True, stop=True), [Bc, BdT], [PB[7]])
                            K.op(dve, lambda: nc.vector.tensor_copy(out=dsk[:], in_=psum[:, 7, 64:112]), [PB[7]], [Bdsk])
                            for g in range(48):
                                K.op(dve, lambda: nc.vector.scalar_tensor_tensor(out=WY[:, g, :], in0=CN(C_ID), scalar=dsk[:, g:g + 1], in1=WY[:, g, :], op0=ALU.mult, op1=ALU.add), [Bc, Bdsk, BWY], [BWY])
                        K.barrier()

                    cw, Bcw = sb(ph, [128, 20, 4], F32, "cw")
                    cbias, Bcb = sb(ph, [128, 20], F32, "cbias")
                    with ExitStack() as tc_:
                        cwn, Bcwn = sb(tc_, [4, 20, 128], F32, "cwn")
                        cbn, Bcbn = sb(tc_, [20, 128], F32, "cbn")
                        K.dma(cwn[:], P["conv_w"][L].rearrange("j (t p) -> j t p", p=128), writes=[Bcwn])
                        for t_ in range(20):
                            K.op(pe, lambda: nc.tensor.transpose(psum[:, 0, t_ * 4:(t_ + 1) * 4], cwn[:, t_, :], consts[0:4, C_ID, 0:4]), [Bcwn, Bc], [PB[0]])
                        K.op(dve, lambda: nc.vector.tensor_copy(out=cw[:].rearrange("p a b -> p (a b)"), in_=psum[:, 0, 0:80]), [PB[0]], [Bcw])
                        K.dma(cbn[:], P["conv_b"][L].rearrange("(t p) -> t p", p=128), writes=[Bcbn])
                        K.op(pe, lambda: nc.tensor.transpose(psum[:, 1, 0:20], cbn[:], consts[0:20, C_ID, 0:20]), [Bcbn, Bc], [PB[1]])
                        K.op(dve, lambda: nc.vector.tensor_copy(out=cbias[:], in_=psum[:, 1, 0:20]), [PB[1]], [Bcb])
                        K.barrier()
                    CD, BCD = sb(ph, [128, 20, 4, 128], BF16, "CD")
                    for t_ in range(20):
                        for j in range(4):
                            if (t_ * 4 + j) % 2 == 0:
                                K.op(dve, lambda: nc.vector.tensor_scalar(out=CD[:, t_, j, :], in0=CN(C_ID), scalar1=cw[:, t_, j:j + 1], scalar2=None, op0=ALU.mult), [Bc, Bcw], [BCD])
                            else:
                                K.op(act, lambda: nc.scalar.activation(out=CD[:, t_, j, :], in_=CN(C_ID), func=AF.Copy, scale=cw[:, t_, j:j + 1]), [Bc, Bcw], [BCD])
                    dtb, Bdtb = bcast_row(ph, P["dt_bias"][L, d], 24, "dtb")
                    Arow, BAr = bcast_row(ph, P["a_log"][L, d], 24, "Arow")
                    K.op(act, lambda: nc.scalar.activation(out=Arow[:], in_=Arow[:], func=AF.Exp), [BAr], [BAr])
                    K.op(dve, lambda: nc.vector.tensor_scalar(out=Arow[:], in0=Arow[:], scalar1=-1.0, scalar2=None, op0=ALU.mult), [BAr], [BAr])
                    Drow, BDr = bcast_row(ph, P["d_m2"][L], 24, "Drow")
                    TRI = CN(C_UT) if d == 0 else CN(C_LT)
                    MST = CN(C_SL) if d == 0 else CN(C_SU)
                    MCB = CN(C_UT) if d == 0 else CN(C_LT)
                    xe = [sb(ph, [128, 20, 131], BF16, "xe") for _ in range(3)]
                    dtr = [sb(ph, [128, 24], F32, "dtr") for _ in range(3)]
                    uk, Buk = sb(ph, [16, 8, S5W], BF16, "uk")
                    ukg, Bukg = sb(ph, [16, 48, 8, 16], BF16, "ukg")
                    Ut2 = [sb(ph, [128, 48, 16], BF16, "Ut") for _ in range(2)]
                    Vsb, BVsb = sb(ph, [64, 16, 2, 48], F32, "Vsb")
                    Hh, BHh = sb(ph, [64, 17, 4, 48], F32, "Hh")
                    Sbf2 = [sb(ph, [128, 2, 48, 16], BF16, "Sbf") for _ in range(2)]
                    for t_, Bt_ in Sbf2:
                        K.op(dve, lambda: nc.vector.memset(t_[:], 0.0), [], [Bt_])
                    Qt, BQt = sb(ph, [64, 2, 2, 48], F32, "Qt")
                    Rt, BRt = sb(ph, [64, 2, 2, 48], F32, "Rt")
                    part = [sb(ph, [16, 8, 128], F32, "part") for _ in range(2)]
                    ysc = [sb(ph, [16, 8, 128], F32, "ysc") for _ in range(2)]
                    xS, BxS = sb(ph, [128, 20, 128], BF16, "xS")
                    xdt, Bxdt = sb(ph, [128, 24, 64], BF16, "xdt")
                    xdin, Bxdin = sb(ph, [128, 24, 64], BF16, "xdin")
                    Btok, BBtok = sb(ph, [128, 4, 128], BF16, "Btok")
                    dts, Bdts = sb(ph, [128, 8, 24], F32, "dts")
                    LHh2 = [sb(ph, [128, 6, 128], BF16, "LHh") for _ in range(2)]
                    LHl2 = [sb(ph, [128, 6, 128], BF16, "LHl") for _ in range(2)]
                    cb16, Bcb16 = sb(ph, [128, 2, 128], BF16, "cb16")
                    K.op(dve, lambda: nc.vector.tensor_copy(out=cb16[:, 0, :], in_=TRI), [Bc], [Bcb16])
                    K.op(dve, lambda: nc.vector.tensor_copy(out=cb16[:, 1, :], in_=MST), [Bc], [Bcb16])
                    TRIb, MSTb = cb16[:, 0, :], cb16[:, 1, :]
                    dthl, Bdthl = sb(ph, [128, 2, 24], BF16, "dthl")
                    dtf, Bdtf = sb(ph, [128, 24], F32, "dtf")
                    dlo, Bdlo = sb(ph, [128, 24], F32, "dlo")
                    Dm2 = [sb(ph, [128, 6, 128], BF16, "Dm") for _ in range(2)]
                    MT2 = [sb(ph, [128, 6, 128], BF16, "MT") for _ in range(2)]
                    cbm, Bcbm = sb(ph, [128, 4, 128], BF16, "cbm")
                    Hst, BHst = sb(ph, [128, 24, 64], F32, "Hst")
                    Hb, BHb = sb(ph, [128, 24, 64], BF16, "Hb")
                    t1s = [sb(ph, [128, 6, 64], F32, "t1") for _ in range(2)]
                    pm2, Bpm2 = sb(ph, [128, 24, 64], F32, "pm2")
                    if debug and L == 0:
                        print('SBUF remaining in pass d=%d: %d bytes' % (d, nc.sbuf_bytes_remaining))
                    K.op(dve, lambda: nc.vector.memset(Hst[:], 0.0), [], [BHst])
                    K.op(dve, lambda: nc.vector.memset(Hh[:], 0.0), [], [BHh])
                    for t_, Bt_ in xe:
                        K.op(pool, lambda: nc.gpsimd.memset(t_[:], 0.0), [], [Bt_])
                    order = list(range(NCH)) if d == 0 else list(range(NCH - 1, -1, -1))
                    su_v = lambda c: s_u[c].rearrange("(k t) c -> k t c", t=8)
                    sy_v = lambda c: s_ys5[c].rearrange("(k t) c -> k t c", t=8)
                    A8a = A8[:, 0].unsqueeze(1).to_broadcast([64, 2, 2, 48])
                    A8b = A8[:, 1].unsqueeze(1).to_broadcast([64, 2, 2, 48])

                    def load_xe(c):
                        t_, Bt_ = xe[c % 3]
                        K.dma(t_[:, :, 1:129], s_xbc[c], reads=[B_xbc[c]], writes=[Bt_])
                        q_, Bq_ = dtr[c % 3]
                        K.dma(q_[:], s_dt[c][:, d * 24:(d + 1) * 24], reads=[B_dt[c]], writes=[Bq_])

                    def load_u(c):
                        K.dma(uk[:], su_v(c), reads=[B_u[c]], writes=[Buk])

                    def load_part(c, ct):
                        p_, Bp_ = part[ct % 2]
                        K.dma(p_[:], sy_v(c)[:, :, ct * 128:(ct + 1) * 128], reads=[B_ys5[c]], writes=[Bp_])

                    def gen_s5ab(it, c):
                        Ut, BUt = Ut2[it % 2]
                        Sbf, BSbf = Sbf2[it % 2]
                        m_in64 = seqm[0:64, c:c + 1] if d == 0 else seqm[0:64, c + 1:c + 2]
                        K.op(dve, lambda: nc.vector.tensor_copy(out=ukg[:, 0:24], in_=uk[:, :, 0:384].rearrange("k t (g c) -> k g t c", c=16)), [Buk], [Bukg])
                        K.op(dve, lambda: nc.vector.tensor_copy(out=ukg[:, 24:48], in_=uk[:, :, 384:768].rearrange("k t (g c) -> k g t c", c=16)), [Buk], [Bukg])
                        if it + 1 < NCH:
                            load_u(order[it + 1])
                        pb = bank_bf(0)
                        for g in range(48):
                            K.op(pe, lambda: nc.tensor.transpose(pb[:, g * 16:(g + 1) * 16], ukg[:, g].rearrange("k t c -> k (t c)"), ident_bf[0:16, 0:16]), [Bukg, Bidb], [PB[0]])
                        K.op(dve, lambda: nc.vector.tensor_copy(out=Ut[:], in_=pb[:, 0:768].rearrange("p (a b) -> p a b", b=16)), [PB[0]], [BUt])
                        yield
                        for r3 in range(3):
                            b = 1 if r3 % 2 == 0 else 0
                            for gi in range(16):
                                g = r3 * 16 + gi
                                K.op(pe, lambda: nc.tensor.matmul(psum[0:64, b, gi * 16:(gi + 1) * 16], lhsT=WVr[:, g, :], rhs=Ut[:, g, :], start=True, stop=True), [BWVr, BUt], [PB[b]])
                                K.op(pe, lambda: nc.tensor.matmul(psum[0:64, b, 256 + gi * 16:256 + (gi + 1) * 16], lhsT=WVi[:, g, :], rhs=Ut[:, g, :], start=True, stop=True), [BWVi, BUt], [PB[b]])
                            src = psum[0:64, b, :].rearrange("p (r g k) -> p k r g", r=2, g=16)
                            dst = Vsb[:, :, :, r3 * 16:(r3 + 1) * 16]
                            K.op(act, lambda: nc.scalar.activation(out=dst, in_=src, func=AF.Copy), [PB[b]], [BVsb])
                        yield
                        hin0 = 0 if d == 0 else 16
                        hprev_last = 16 if d == 0 else 0
                        if it == 0:
                            K.op(dve, lambda: nc.vector.memset(Hh[:, hin0], 0.0), [], [BHh])
                        else:
                            K.op(dve, lambda: nc.vector.tensor_scalar(out=Hh[:, hin0], in0=Hh[:, hprev_last], scalar1=m_in64, scalar2=None, op0=ALU.mult), [BHh, Bsm], [BHh])
                        ks = list(range(16)) if d == 0 else list(range(15, -1, -1))
                        for k_ in ks:
                            si, so = (k_, k_ + 1) if d == 0 else (k_ + 1, k_)
                            HA = Hh[:, si, 0:2, :].unsqueeze(1).to_broadcast([64, 2, 2, 48])
                            HB = Hh[:, si, 1:3, :].unsqueeze(1).to_broadcast([64, 2, 2, 48])
                            Vb = Vsb[:, k_].unsqueeze(1).to_broadcast([64, 2, 2, 48])
                            K.op(dve, lambda: nc.vector.tensor_tensor(out=Qt[:], in0=A8a, in1=HA, op=ALU.mult), [BA8, BHh], [BQt])
                            K.op(dve, lambda: nc.vector.tensor_tensor(out=Rt[:], in0=A8b, in1=HB, op=ALU.mult), [BA8, BHh], [BRt])
                            K.op(dve, lambda: nc.vector.tensor_tensor(out=Qt[:], in0=Qt[:], in1=Rt[:], op=ALU.add), [BQt, BRt], [BQt])
                            K.op(dve, lambda: nc.vector.tensor_tensor(out=Hh[:, so].rearrange("p (a b) g -> p a b g", a=2), in0=Qt[:], in1=Vb, op=ALU.add), [BQt, BVsb], [BHh])
                            yield
                        hs0 = 0 if d == 0 else 1
                        K.op(act, lambda: nc.scalar.activation(out=Sbf[0:64], in_=Hh[:, hs0:hs0 + 16, 0:2, :].rearrange("p k r g -> p r g k"), func=AF.Copy), [BHh], [BSbf])

                    def gen_s5c(it, c):
                        Ut, BUt = Ut2[it % 2]
                        Sbf, BSbf = Sbf2[it % 2]
                        if d == 1:
                            load_part(c, 0)
                            load_part(c, 1)
                        for ct in range(6):
                            y_, By_ = ysc[ct % 2]
                            for hf in range(2):
                                b = 2 + hf
                                for g4 in range(4):
                                    g = ct * 8 + hf * 4 + g4
                                    o_ = psum[0:16, b, g4 * 128:(g4 + 1) * 128]
                                    K.op(pe, lambda: nc.tensor.matmul(o_, lhsT=Ut[:, g, :], rhs=WY[:, g, :], start=True, stop=False), [BUt, BWY], [PB[b]])
                                    K.op(pe, lambda: nc.tensor.matmul(o_, lhsT=Sbf[:, 0, g, :], rhs=WCr[:, g, :], start=False, stop=False), [BSbf, BWCr], [PB[b]])
                                    K.op(pe, lambda: nc.tensor.matmul(o_, lhsT=Sbf[:, 1, g, :], rhs=WCi[:, g, :], start=False, stop=True), [BSbf, BWCi], [PB[b]])
                                ov = y_[:, :, hf * 64:(hf + 1) * 64].rearrange("k t (g c) -> k g t c", c=16)
                                pv = psum[0:16, b, :].rearrange("k (g t c) -> k g t c", g=4, t=8)
                                if d == 1:
                                    p_, Bp_ = part[ct % 2]
                                    K.op(dve, lambda: nc.vector.tensor_tensor(out=ov, in0=pv, in1=p_[:, :, hf * 64:(hf + 1) * 64].rearrange("k t (g c) -> k g t c", c=16), op=ALU.add), [PB[b], Bp_], [By_])
                                else:
                                    K.op(dve, lambda: nc.vector.tensor_copy(out=ov, in_=pv), [PB[b]], [By_])
                                yield
                            K.dma(sy_v(c)[:, :, ct * 128:(ct + 1) * 128], y_[:], reads=[By_], writes=[B_ys5[c]])
                            if d == 1 and ct + 2 < 6:
                                load_part(c, ct + 2)

                    def gen_ssd(it, c):
                        m_in = seqm[:, c:c + 1] if d == 0 else seqm[:, c + 1:c + 2]
                        xe_, Bxe_ = xe[c % 3]
                        if c - 1 >= 0:
                            xl, Bxl = xe[(c - 1) % 3]
                            K.op(dve, lambda: nc.vector.tensor_scalar(out=xe_[:, :, 0:1], in0=xl[:, :, 128:129], scalar1=seqm[:, c:c + 1], scalar2=None, op0=ALU.mult), [Bxl, Bsm], [Bxe_])
                        else:
                            K.op(dve, lambda: nc.vector.memset(xe_[:, :, 0:1], 0.0), [], [Bxe_])
                        if c + 1 < NCH:
                            xr, Bxr = xe[(c + 1) % 3]
                            K.op(dve, lambda: nc.vector.tensor_scalar(out=xe_[:, :, 129:131], in0=xr[:, :, 1:3], scalar1=seqm[:, c + 1:c + 2], scalar2=None, op0=ALU.mult), [Bxr, Bsm], [Bxe_])
                        else:
                            K.op(dve, lambda: nc.vector.memset(xe_[:, :, 129:131], 0.0), [], [Bxe_])
                        if it + 2 < NCH:
                            load_xe(order[it + 2])
                        q_, Bq_ = dtr[c % 3]
                        K.op(dve, lambda: nc.vector.tensor_tensor(out=dts[:, 7, :], in0=q_[:], in1=dtb[:], op=ALU.add), [Bq_, Bdtb], [Bdts])
                        K.op(act, lambda: nc.scalar.activation(out=dts[:, 7, :], in_=dts[:, 7, :], func=AF.Exp), [Bdts], [Bdts])
                        K.op(act, lambda: nc.scalar.activation(out=dts[:, 0, :], in_=dts[:, 7, :], func=AF.Ln, bias=1.0), [Bdts], [Bdts])
                        K.op(dve, lambda: nc.vector.tensor_tensor(out=dts[:, 1, :], in0=dts[:, 0, :], in1=Arow[:], op=ALU.mult), [Bdts, BAr], [Bdts])
                        K.op(dve, lambda: nc.vector.tensor_copy(out=dthl[:, 0, :], in_=dts[:, 1, :]), [Bdts], [Bdthl])
                        K.op(dve, lambda: nc.vector.tensor_copy(out=dtf[:], in_=dthl[:, 0, :]), [Bdthl], [Bdtf])
                        K.op(dve, lambda: nc.vector.tensor_tensor(out=dlo[:], in0=dts[:, 1, :], in1=dtf[:], op=ALU.subtract), [Bdts, Bdtf], [Bdlo])
                        K.op(pe, lambda: nc.tensor.matmul(psum[:, 6, 0:24], lhsT=TRI, rhs=dts[:, 1, :], start=True, stop=True), [Bc, Bdts], [PB[6]])
                        K.op(pe, lambda: nc.tensor.matmul(psum[:, 6, 32:56], lhsT=CN(C_ONES), rhs=dts[:, 1, :], start=True, stop=True), [Bc, Bdts], [PB[6]])
                        K.op(dve, lambda: nc.vector.tensor_copy(out=dts[:, 2, :], in_=psum[:, 6, 0:24]), [PB[6]], [Bdts])
                        K.op(act, lambda: nc.scalar.activation(out=dts[:, 3, :], in_=psum[:, 6, 0:24], func=AF.Exp), [PB[6]], [Bdts])
                        K.op(dve, lambda: nc.vector.tensor_tensor(out=dts[:, 7, :], in0=psum[:, 6, 32:56], in1=dts[:, 2, :], op=ALU.subtract), [PB[6], Bdts], [Bdts])
                        K.op(act, lambda: nc.scalar.activation(out=dts[:, 4, :], in_=dts[:, 7, :], func=AF.Exp), [Bdts], [Bdts])
                        K.op(act, lambda: nc.scalar.activation(out=dts[:, 5, :], in_=psum[:, 6, 32:56], func=AF.Exp), [PB[6]], [Bdts])
                        K.op(dve, lambda: nc.vector.tensor_scalar(out=dts[:, 6, :], in0=dts[:, 5, :], scalar1=m_in, scalar2=None, op0=ALU.mult), [Bdts, Bsm], [Bdts])
                        yield
                        for q in range(5):
                            b = 4 + (q % 2)
                            for j in range(4):
                                t_ = q * 4 + j
                                for tap in range(4):
                                    K.op(pe, lambda: nc.tensor.matmul(psum[:, b, j * 128:(j + 1) * 128], lhsT=CD[:, t_, tap, :], rhs=xe_[:, t_, tap:tap + 128], start=(tap == 0), stop=(tap == 3)), [BCD, Bxe_], [PB[b]])
                            for j in range(4):
                                t_ = q * 4 + j
                                K.op(act, lambda: nc.scalar.activation(out=xS[:, t_, :], in_=psum[:, b, j * 128:(j + 1) * 128], func=AF.Silu, bias=cbias[:, t_:t_ + 1]), [PB[b], Bcb], [BxS])
                            yield
                        for j in range(12):
                            b = 4 + j // 8
                            K.op(pe, lambda: nc.tensor.transpose(bank_bf(b)[:, (j % 8) * 128:(j % 8 + 1) * 128], xS[:, j, :], ident_bf[:]), [BxS, Bidb], [PB[b]])
                        for j in range(4):
                            K.op(pe, lambda: nc.tensor.transpose(bank_bf(5)[:, 512 + j * 128:512 + (j + 1) * 128], xS[:, 12 + j, :], ident_bf[:]), [BxS, Bidb], [PB[5]])
                        xtok = lambda h0, h1: (bank_bf(4)[:, h0 * 64:h1 * 64] if h1 <= 16 else bank_bf(5)[:, (h0 - 16) * 64:(h1 - 16) * 64]).rearrange("p (h q) -> p h q", q=64)
                        yield
                        for (h0, h1, b) in ((0, 16, 4), (16, 24, 5)):
                            dtv = dts[:, 0, h0:h1].unsqueeze(2).to_broadcast([128, h1 - h0, 64])
                            K.op(dve, lambda: nc.vector.tensor_tensor(out=xdt[:, h0:h1, :], in0=xtok(h0, h1), in1=dtv, op=ALU.mult), [PB[b], Bdts], [Bxdt])
                            if d == 0:
                                Dv = Drow[:, h0:h1].unsqueeze(2).to_broadcast([128, h1 - h0, 64])
                                K.op(dve, lambda: nc.vector.tensor_tensor(out=pm2[:, h0:h1, :], in0=xtok(h0, h1), in1=Dv, op=ALU.mult), [PB[b], BDr], [Bpm2])
                        K.op(act, lambda: nc.scalar.activation(out=Btok[:], in_=bank_bf(5)[:, 512:1024].rearrange("p (a b) -> p a b", b=128), func=AF.Copy), [PB[5]], [BBtok])
                        K.op(dve, lambda: nc.vector.tensor_tensor(out=xdin[:], in0=xdt[:], in1=dts[:, 4, :].unsqueeze(2).to_broadcast([128, 24, 64]), op=ALU.mult), [Bxdt, Bdts], [Bxdin])
                        K.op(act, lambda: nc.scalar.activation(out=Hb[:], in_=Hst[:], func=AF.Copy, scale=m_in), [BHst, Bsm], [BHb])
                        yield
                        for g in range(4):
                            K.op(pe, lambda: nc.tensor.matmul(psum[:, 7, g * 128:(g + 1) * 128], lhsT=xS[:, 12 + g, :], rhs=xS[:, 16 + g, :], start=True, stop=True), [BxS], [PB[7]])
                        K.op(dve, lambda: nc.vector.tensor_tensor(out=cbm[:], in0=psum[:, 7, :].rearrange("p (a b) -> p a b", b=128), in1=MCB.unsqueeze(1).to_broadcast([128, 4, 128]), op=ALU.mult), [PB[7], Bc], [Bcbm])
                        yield
                        def s1a(g):
                            hs = slice(g * 6, g * 6 + 6)
                            (LHh, BLHh), (LHl, BLHl) = LHh2[g % 2], LHl2[g % 2]
                            for j in range(6):
                                h = g * 6 + j
                                K.op(act, lambda: nc.scalar.activation(out=LHh[:, j, :], in_=MSTb, func=AF.Copy, scale=dtf[:, h:h + 1]), [Bcb16, Bdtf], [BLHh])
                                K.op(act, lambda: nc.scalar.activation(out=LHl[:, j, :], in_=MSTb, func=AF.Copy, scale=dlo[:, h:h + 1]), [Bcb16, Bdlo], [BLHl])
                            for j in range(6):
                                b = 4 + j // 4
                                o_ = psum[:, b, (j % 4) * 128:(j % 4 + 1) * 128]
                                K.op(pe, lambda: nc.tensor.matmul(o_, lhsT=LHh[:, j, :], rhs=TRIb, start=True, stop=False), [BLHh, Bcb16], [PB[b]])
                                K.op(pe, lambda: nc.tensor.matmul(o_, lhsT=LHl[:, j, :], rhs=TRIb, start=False, stop=True), [BLHl, Bcb16], [PB[b]])

                        def s1b(g):
                            (Dm, BDm), (MT, BMT) = Dm2[g % 2], MT2[g % 2]
                            K.op(act, lambda: nc.scalar.activation(out=Dm[:, 0:4, :], in_=psum[:, 4, :].rearrange("p (a b) -> p a b", b=128), func=AF.Exp), [PB[4]], [BDm])
                            K.op(act, lambda: nc.scalar.activation(out=Dm[:, 4:6, :], in_=psum[:, 5, 0:256].rearrange("p (a b) -> p a b", b=128), func=AF.Exp), [PB[5]], [BDm])
                            K.op(dve, lambda: nc.vector.tensor_tensor(out=MT[:], in0=Dm[:], in1=cbm[:, g:g + 1, :].to_broadcast([128, 6, 128]), op=ALU.mult), [BDm, Bcbm], [BMT])

                        def s2(g):
                            hs = slice(g * 6, g * 6 + 6)
                            MT, BMT = MT2[g % 2]
                            tq_, Btq_ = t1s[g % 2]
                            for j in range(6):
                                h = g * 6 + j
                                K.op(pe, lambda: nc.tensor.matmul(psum[:, 6, j * 64:(j + 1) * 64], lhsT=MT[:, j, :], rhs=xdt[:, h, :], start=True, stop=True), [BMT, Bxdt], [PB[6]])
                                K.op(pe, lambda: nc.tensor.matmul(psum[:, 7, j * 64:(j + 1) * 64], lhsT=xS[:, 16 + g, :], rhs=Hb[:, h, :], start=True, stop=True), [BxS, BHb], [PB[7]])
                            K.op(dve, lambda: nc.vector.tensor_tensor(out=tq_[:], in0=psum[:, 7, 0:384].rearrange("p (h q) -> p h q", q=64), in1=dts[:, 3, hs].unsqueeze(2).to_broadcast([128, 6, 64]), op=ALU.mult), [PB[7], Bdts], [Btq_])
                            K.op(dve, lambda: nc.vector.tensor_tensor(out=tq_[:], in0=tq_[:], in1=psum[:, 6, 0:384].rearrange("p (h q) -> p h q", q=64), op=ALU.add), [Btq_, PB[6]], [Btq_])
                            K.op(dve, lambda: nc.vector.tensor_tensor(out=pm2[:, hs, :], in0=pm2[:, hs, :], in1=tq_[:], op=ALU.add), [Bpm2, Btq_], [Bpm2])

                        for stg in (lambda: s1a(0), lambda: s1b(0), lambda: s1a(1), lambda: s2(0), lambda: s1b(1), lambda: s1a(2), lambda: s2(1), lambda: s1b(2), lambda: s1a(3), lambda: s2(2), lambda: s1b(3), lambda: s2(3)):
                            stg()
                            yield
                        K.dma(s_ym2[c], pm2[:].rearrange("p a b -> p (a b)"), reads=[Bpm2], writes=[B_ym2[c]])
                        if d == 1 and it + 1 < NCH:
                            cn = order[it + 1]
                            K.dma(pm2[:].rearrange("p a b -> p (a b)"), s_ym2[cn], reads=[B_ym2[cn]], writes=[Bpm2])
                        K.op(dve, lambda: nc.vector.tensor_tensor(out=Hst[:], in0=Hst[:], in1=dts[:, 6, :].unsqueeze(2).to_broadcast([128, 24, 64]), op=ALU.mult), [BHst, Bdts], [BHst])
                        for g in range(4):
                            b = 6 + (g % 2)
                            hs = slice(g * 6, g * 6 + 6)
                            for j in range(6):
                                h = g * 6 + j
                                K.op(pe, lambda: nc.tensor.matmul(psum[:, b, j * 64:(j + 1) * 64], lhsT=Btok[:, g, :], rhs=xdin[:, h, :], start=True, stop=True), [BBtok, Bxdin], [PB[b]])
                            K.op(dve, lambda: nc.vector.tensor_tensor(out=Hst[:, hs, :], in0=Hst[:, hs, :], in1=psum[:, b, 0:384].rearrange("p (h q) -> p h q", q=64), op=ALU.add), [BHst, PB[b]], [BHst])
                            yield

                    load_xe(order[0])
                    if NCH > 1:
                        load_xe(order[1])
                    load_u(order[0])
                    if d == 1:
                        K.dma(pm2[:].rearrange("p a b -> p (a b)"), s_ym2[order[0]], reads=[B_ym2[order[0]]], writes=[Bpm2])
                    for it, c in enumerate(order):
                        gens = [[gen_ssd(it, c), 0], [gen_s5ab(it, c), 0]]
                        if it > 0:
                            gens.append([gen_s5c(it - 1, order[it - 1]), 6])
                        run_gens(gens)
                    run_gens([[gen_s5c(NCH - 1, order[NCH - 1]), 0]])
                    K.barrier()
            if LVL < 3:
                break
            with ExitStack() as ph:
                pw_ = [sb(ph, [128, 6, S5W], BF16, "wglu"), sb(ph, [128, 6, D], BF16, "ws5o"), sb(ph, [128, 12, D], BF16, "wm2o"), sb(ph, [128, 8, D], BF16, "wo")]
                with ExitStack() as st_:
                    stage = [sb(st_, [128, 1024], F32, "stg") for _ in range(3)]
                    wglu, Bwglu = load_weight(ph, P["w_glu"][L], 6, S5W, "wglu", stage, [act, dve, pool], pre=pw_[0])
                    ws5o, Bws5o = load_weight(ph, P["w_s5_out"][L], 6, D, "ws5o", stage, [act, dve, pool], pre=pw_[1])
                    wm2o, Bwm2o = load_weight(ph, P["w_m2_out"][L], 12, D, "wm2o", stage, [act, dve, pool], pre=pw_[2])
                    wo, Bwo = load_weight(ph, P["w_o"][L], 8, D, "wo", stage, [act, dve, pool], pre=pw_[3])
                    K.barrier()
                nwb, Bnwb = bcast_row(ph, P["m2_norm_w"][L], M2I, "nwb")
                bgn, Bbgn = sb(ph, [6, 128], F32, "bgn")
                bg, Bbg = sb(ph, [128, 6], F32, "bg")
                K.dma(bgn[:], P["b_glu"][L].rearrange("(t p) -> t p", p=128), writes=[Bbgn])
                K.op(pe, lambda: nc.tensor.transpose(psum[:, 0, 0:6], bgn[:], consts[0:6, C_ID, 0:6]), [Bbgn, Bc], [PB[0]])
                K.op(dve, lambda: nc.vector.tensor_copy(out=bg[:], in_=psum[:, 0, 0:6]), [PB[0]], [Bbg])
                NB = 2
                ysk = [sb(ph, [16, 8, S5W], F32, "ysk") for _ in range(1)] * NB
                ym = [sb(ph, [128, M2I], F32, "ym") for _ in range(NB)]
                zt = [sb(ph, [128, M2I], BF16, "zt") for _ in range(NB)]
                gt_ = [sb(ph, [128, 2 * D], BF16, "gt") for _ in range(NB)]
                xt = [sb(ph, [128, D], F32, "x") for _ in range(3)]
                sq, Bsq = sb(ph, [128, 6, 128], F32, "sq")
                tt, Btt = sb(ph, [128, 6, 128], F32, "tt")
                hT, BhT = sb(ph, [128, 6, 128], BF16, "hT5")
                sg, Bsg = sb(ph, [128, 6, 128], BF16, "sg")
                s5a, Bs5a = sb(ph, [128, 6, 128], BF16, "s5a")
                mg, Bmg = sb(ph, [128, D], F32, "mg")
                mg2, Bmg2 = sb(ph, [128, D], F32, "mg2")
                mgb2 = [sb(ph, [128, D], BF16, "mgb") for _ in range(2)]
                mT2 = [sb(ph, [128, 8, 128], BF16, "mT") for _ in range(2)]
                sz, Bsz = sb(ph, [128, M2I], F32, "sz")
                gz, Bgz = sb(ph, [128, M2I], F32, "gz")
                junk2, Bjunk2 = sb(ph, [128, 384], BF16, "junk2")
                ss4, Bss4 = sb(ph, [128, 4], F32, "ss4")
                gnb, Bgnb = sb(ph, [128, M2I], BF16, "gnb")
                gT, BgT = sb(ph, [128, 12, 128], BF16, "gT")
                x1t = [sb(ph, [128, D], F32, "x1") for _ in range(NB)]

                def tl_load(c):
                    i = c % NB
                    K.dma(ym[i][0][:], s_ym2[c], reads=[B_ym2[c]], writes=[ym[i][1]])
                    K.dma(zt[i][0][:], s_z[c], reads=[B_z[c]], writes=[zt[i][1]])
                    K.dma(gt_[i][0][:], s_gate[c], reads=[B_gate[c]], writes=[gt_[i][1]])
                    K.dma(xt[c % 3][0][:], xsrc[c], reads=([Bxsrc[c]] if Bxsrc else []), writes=[xt[c % 3][1]])

                def ys_load(c):
                    K.dma(ysk[0][0][:], s_ys5[c].rearrange("(k t) c -> k t c", t=8), reads=[B_ys5[c]], writes=[ysk[0][1]])

                if debug and L == 0:
                    print('SBUF remaining in tail: %d bytes' % nc.sbuf_bytes_remaining)
                tl_load(0)
                ys_load(0)
                for c in range(NCH):
                    if c + 1 < NCH:
                        tl_load(c + 1)
                    i = c % NB
                    (ys_, Bys_), (ym_, Bym_), (z_, Bz_), (g_, Bg_) = ysk[i], ym[i], zt[i], gt_[i]
                    def gen_t5():
                        for ct in range(6):
                            b = ct // 3
                            for t_ in range(8):
                                col = (ct % 3) * 128 + t_ * 16
                                K.op(pe, lambda: nc.tensor.transpose(psum[:, b, col:col + 16], ys_[:, t_, ct * 128:(ct + 1) * 128], consts[0:16, C_ID, 0:16]), [Bys_, Bc], [PB[b]])
                        if c + 1 < NCH:
                            ys_load(c + 1)
                        yield
                        yT = lambda b: psum[:, b, 0:384].rearrange("p (c t k) -> p c k t", c=3, t=8)
                        for b in (0, 1):
                            cs_ = slice(b * 3, b * 3 + 3)
                            sqv = sq[:, cs_, :].rearrange("p c (k t) -> p c k t", t=8)
                            ttv = tt[:, cs_, :].rearrange("p c (k t) -> p c k t", t=8)
                            K.op(act, lambda: nc.scalar.activation(out=sqv, in_=yT(b), func=AF.Square), [PB[b]], [Bsq])
                            K.op(dve, lambda: nc.vector.tensor_scalar(out=sq[:, cs_, :], in0=sq[:, cs_, :], scalar1=0.044715, scalar2=1.0, op0=ALU.mult, op1=ALU.add), [Bsq], [Bsq])
                            K.op(dve, lambda: nc.vector.tensor_tensor(out=ttv, in0=sqv, in1=yT(b), op=ALU.mult), [Bsq, PB[b]], [Btt])
                            K.op(act, lambda: nc.scalar.activation(out=sq[:, cs_, :], in_=tt[:, cs_, :], func=AF.Sigmoid, scale=1.5957691216057308), [Btt], [Bsq])
                            K.op(dve, lambda: nc.vector.tensor_tensor(out=hT[:, cs_, :].rearrange("p c (k t) -> p c k t", t=8), in0=sqv, in1=yT(b), op=ALU.mult), [Bsq, PB[b]], [BhT])
                            yield
                        for co in range(6):
                            b = 2
                            for kt in range(6):
                                K.op(pe, lambda: nc.tensor.matmul(psum[:, b, (co % 4) * 128:(co % 4 + 1) * 128], lhsT=wglu[:, kt, co * 128:(co + 1) * 128], rhs=hT[:, kt, :], start=(kt == 0), stop=(kt == 5)), [Bwglu, BhT], [PB[b]])
                            K.op(act, lambda: nc.scalar.activation(out=sg[:, co, :], in_=psum[:, b, (co % 4) * 128:(co % 4 + 1) * 128], func=AF.Sigmoid, bias=bg[:, co:co + 1]), [PB[b], Bbg], [Bsg])
                            if co % 2 == 1:
                                yield
                        K.op(dve, lambda: nc.vector.tensor_tensor(out=s5a[:], in0=hT[:], in1=sg[:], op=ALU.mult), [BhT, Bsg], [Bs5a])
                        for hf in range(2):
                            b = hf
                            mm_tok(b, s5a, Bs5a, 6, ws5o, Bws5o, hf * 512, hf * 512 + 512)
                            K.op(dve, lambda: nc.vector.tensor_tensor(out=mg[:, hf * 512:(hf + 1) * 512], in0=psum[:, b, :], in1=g_[:, hf * 512:(hf + 1) * 512], op=ALU.mult), [PB[b], Bg_], [Bmg])
                            yield

                    def gen_tm():
                        K.op(act, lambda: nc.scalar.activation(out=sz[:], in_=z_[:], func=AF.Silu), [Bz_], [Bsz])
                        K.op(dve, lambda: nc.vector.tensor_tensor(out=gz[:], in0=ym_[:], in1=sz[:], op=ALU.mult), [Bym_, Bsz], [Bgz])
                        yield
                        for g in range(4):
                            K.op(act, lambda: nc.scalar.activation(out=junk2[:], in_=gz[:, g * 384:(g + 1) * 384], func=AF.Square, accum_out=ss4[:, g:g + 1]), [Bgz], [Bjunk2, Bss4])
                        K.op(dve, lambda: nc.vector.tensor_scalar(out=ss4[:], in0=ss4[:], scalar1=1.0 / 384, scalar2=EPS, op0=ALU.mult, op1=ALU.add), [Bss4], [Bss4])
                        K.op(act, lambda: nc.scalar.activation(out=ss4[:], in_=ss4[:], func=AF.Sqrt), [Bss4], [Bss4])
                        K.op(dve, lambda: nc.vector.reciprocal(out=ss4[:], in_=ss4[:]), [Bss4], [Bss4])
                        yield
                        K.op(dve, lambda: nc.vector.tensor_tensor(out=gz[:].rearrange("p (g q) -> p g q", g=4), in0=gz[:].rearrange("p (g q) -> p g q", g=4), in1=ss4[:].unsqueeze(2).to_broadcast([128, 4, 384]), op=ALU.mult), [Bgz, Bss4], [Bgz])
                        K.op(dve, lambda: nc.vector.tensor_tensor(out=gnb[:], in0=gz[:], in1=nwb[:], op=ALU.mult), [Bgz, Bnwb], [Bgnb])
                        yield
                        transpose_to(gT, BgT, gnb, Bgnb, 12, 4, evac=act)
                        yield
                        for hf in range(2):
                            b = 4 + hf
                            mm_tok(b, gT, BgT, 12, wm2o, Bwm2o, hf * 512, hf * 512 + 512)
                            K.op(dve, lambda: nc.vector.tensor_tensor(out=mg2[:, hf * 512:(hf + 1) * 512], in0=psum[:, b, :], in1=g_[:, D + hf * 512:D + (hf + 1) * 512], op=ALU.mult), [PB[b], Bg_], [Bmg2])
                            yield

                    def gen_wo(cp):
                        mgb, Bmgb = mgb2[cp % 2]
                        mT, BmT = mT2[cp % 2]
                        xp_, Bxp_ = xt[cp % 3]
                        x1_, Bx1_ = x1t[cp % 2]
                        transpose_to(mT, BmT, mgb, Bmgb, 8, 3)
                        yield
                        for hf in range(2):
                            b = 6 + hf
                            mm_tok(b, mT, BmT, 8, wo, Bwo, hf * 512, hf * 512 + 512)
                            K.op(dve, lambda: nc.vector.tensor_tensor(out=x1_[:, hf * 512:(hf + 1) * 512], in0=psum[:, b, :], in1=xp_[:, hf * 512:(hf + 1) * 512], op=ALU.add), [PB[b], Bxp_], [Bx1_])
                            yield
                        K.dma(s_x1[cp], x1_[:], reads=[Bx1_], writes=[B_x1[cp]])

                    gl = [[gen_tm(), 0], [gen_t5(), 0]]
                    if c > 0:
                        gl.append([gen_wo(c - 1), 1])
                    run_gens(gl)
                    if debug:
                        K.dma(s_dbg1[c], mg[:], reads=[Bmg], writes=[])
                        K.dma(s_dbg2[c], mg2[:], reads=[Bmg2], writes=[])
                    K.op(dve, lambda: nc.vector.tensor_tensor(out=mgb2[c % 2][0][:], in0=mg[:], in1=mg2[:], op=ALU.add), [Bmg, Bmg2], [mgb2[c % 2][1]])
                run_gens([[gen_wo(NCH - 1), 0]])
                K.barrier()
            if LVL < 4:
                break
            with ExitStack() as ph:
                pw_ = [sb(ph, [128, 8, DFF], BF16, "wup"), sb(ph, [128, 32, D], BF16, "wdn")]
                with ExitStack() as st_:
                    stage = [sb(st_, [128, 1024], F32, "stg") for _ in range(3)]
                    wup, Bwup = load_weight(ph, P["w_up"][L], 8, DFF, "wup", stage, [act, dve, pool], pre=pw_[0])
                    wdn, Bwdn = load_weight(ph, P["w_down"][L], 32, D, "wdn", stage, [act, dve, pool], pre=pw_[1])
                    K.barrier()
                w2b, Bw2b = bcast_row(ph, P["norm2_w"][L], D, "w2b")
                last = (L == DEPTH - 1)
                if last:
                    wfb, Bwfb = bcast_row(ph, P["final_norm_w"], D, "wfb")
                NB = 2
                xt = [sb(ph, [128, D], F32, "x") for _ in range(NB)]
                junk = sb(ph, [128, D], BF16, "junk")
                ss = [sb(ph, [128, 1], F32, "ss") for _ in range(NB)]
                hb = [sb(ph, [128, D], BF16, "h") for _ in range(NB)]
                hT = [sb(ph, [128, 8, 128], BF16, "hT") for _ in range(NB)]
                aT, BaT = sb(ph, [128, 32, 128], BF16, "aT")
                rl = [sb(ph, [128, 4, 128], F32, "rl") for _ in range(2)]
                x2t = [sb(ph, [128, D], F32, "x2") for _ in range(NB)]
                yo = [sb(ph, [128, D], F32, "yo") for _ in range(NB)]

                def p3_load(c):
                    t, B = xt[c % NB]
                    K.dma(t[:], s_x1[c], reads=[B_x1[c]], writes=[B])

                ssf = [sb(ph, [128, 1], F32, "ssf") for _ in range(NB)]

                def p3_head(c):
                    i = c % NB
                    (x_, Bx), (ss_, Bss), (h_, Bh), (hT_, BhT) = xt[i], ss[i], hb[i], hT[i]
                    rms_rstd(x_[:], Bx, D, ss_, Bss, junk[0][:], junk[1])
                    K.op(dve, lambda: nc.vector.scalar_tensor_tensor(out=h_[:], in0=x_[:], scalar=ss_[:, 0:1], in1=w2b[:], op0=ALU.mult, op1=ALU.mult), [Bx, Bss, Bw2b], [Bh])
                    transpose_to(hT_, BhT, h_, Bh, 8, 0)

                p3_load(0)
                p3_head(0)
                for c in range(NCH):
                    if c + 1 < NCH:
                        p3_load(c + 1)
                    i = c % NB
                    (x_, Bx), (ss_, Bss), (h_, Bh), (hT_, BhT) = xt[i], ss[i], hb[i], hT[i]
                    for q in range(8):
                        b = 1 + (q % 4)
                        for j in range(4):
                            ft = q * 4 + j
                            for kt in range(8):
                                K.op(pe, lambda: nc.tensor.matmul(psum[:, b, j * 128:(j + 1) * 128], lhsT=wup[:, kt, ft * 128:(ft + 1) * 128], rhs=hT_[:, kt, :], start=(kt == 0), stop=(kt == 7)), [Bwup, BhT], [PB[b]])
                        rl_, Brl_ = rl[q % 2]
                        K.op(act, lambda: nc.scalar.activation(out=rl_[:], in_=psum[:, b, :].rearrange("p (a b) -> p a b", b=128), func=AF.Relu), [PB[b]], [Brl_])
                        if q % 2 == 0:
                            K.op(dve, lambda: nc.vector.tensor_tensor(out=aT[:, q * 4:q * 4 + 4, :], in0=rl_[:], in1=rl_[:], op=ALU.mult), [Brl_], [BaT])
                        else:
                            K.op(dve, lambda: nc.vector.tensor_tensor(out=aT[:, q * 4:q * 4 + 4, :], in0=rl_[:], in1=rl_[:], op=ALU.mult), [Brl_], [BaT])
                    if c + 1 < NCH:
                        p3_head(c + 1)
                    x2_, Bx2_ = x2t[i]
                    for hf in range(2):
                        b = 5 + hf
                        mm_tok(b, aT, BaT, 32, wdn, Bwdn, hf * 512, hf * 512 + 512)
                        K.op(dve, lambda: nc.vector.tensor_tensor(out=x2_[:, hf * 512:(hf + 1) * 512], in0=psum[:, b, :], in1=x_[:, hf * 512:(hf + 1) * 512], op=ALU.add), [PB[b], Bx], [Bx2_])
                    if not last:
                        K.dma(s_x2[c], x2_[:], reads=[Bx2_], writes=[B_x2[c]])
                    else:
                        y_, By_ = yo[i]
                        sf_, Bsf_ = ssf[i]
                        rms_rstd(x2_[:], Bx2_, D, sf_, Bsf_, junk[0][:], junk[1])
                        K.op(dve, lambda: nc.vector.scalar_tensor_tensor(out=y_[:], in0=x2_[:], scalar=sf_[:, 0:1], in1=wfb[:], op0=ALU.mult, op1=ALU.mult), [Bx2_, Bsf_, Bwfb], [By_])
                        K.dma(y_out[c], y_[:], reads=[By_], writes=[])
                K.barrier()
            if LVL < 5:
                break

        K.barrier()
    return nc


PNAMES = ["norm1_w", "w_in", "lam_re", "lam_im", "log_dt", "b_re", "b_im", "c_re", "c_im", "d_s5", "w_glu", "b_glu",
          "w_s5_out", "conv_w", "conv_b", "dt_bias", "a_log", "d_m2", "m2_norm_w", "w_m2_out", "w_o", "norm2_w",
          "w_up", "w_down", "final_norm_w"]


def kernel(**inputs):
    xp = np.asarray(inputs["x_prompt"], np.float32)
    xs = np.asarray(inputs["x_sample"], np.float32)
    NCH = 64
    params = {k: np.ascontiguousarray(np.asarray(inputs[k], np.float32)) for k in PNAMES}
    consts = host_consts()
    in_maps = []
    for core in range(8):
        m = np.ones((128, NCH + 1), np.float32)
        m[:, 0] = 0.0
        m[:, NCH] = 0.0
        if core < 4:
            x = xs[core].reshape(NCH, 128, D)
        else:
            j = core - 4
            x = np.zeros((NCH, 128, D), np.float32)
            x[0:16] = xp[2 * j].reshape(16, 128, D)
            x[16:32] = xp[2 * j + 1].reshape(16, 128, D)
            m[:, 16] = 0.0
            m[:, 32] = 0.0
            m[:, 48] = 0.0
        im = dict(params)
        im.update(x=np.ascontiguousarray(x), seqmask=m, consts=consts)
        in_maps.append(im)
    nc = build_program(4, 16)
    res = run_bass_kernel_spmd(nc, in_maps, core_ids=list(range(8)))
    yp = np.zeros((8, 2048, D), np.float32)
    ysm = np.zeros((4, 8192, D), np.float32)
    for core in range(8):
        y = np.asarray(res.results[core]["y"], np.float32).reshape(NCH * 128, D)
        if core < 4:
            ysm[core] = y
        else:
            j = core - 4
            yp[2 * j] = y[0:2048]
            yp[2 * j + 1] = y[2048:4096]
    return (yp, ysm)
```

```python
import numpy as np
from contextlib import ExitStack
import concourse.bass as bass
import concourse.mybir as mybir
from concourse.bass_utils import run_bass_kernel_spmd

F32 = mybir.dt.float32
BF16 = mybir.dt.bfloat16
AF = mybir.ActivationFunctionType
ALU = mybir.AluOpType
AX = mybir.AxisListType

D = 1024
DEPTH = 2
EPS = 1e-6
S5W, S5G, S5P = 768, 48, 64
M2I, M2H, M2P, M2G, M2N = 1536, 24, 64, 4, 128
XBC = 2560
OFF_U, OFF_Z, OFF_XBC, OFF_DT, OFF_GATE, DPROJ = 0, 768, 2304, 4864, 4912, 6960
DFF = 4096
PI = float(np.pi)
SUB = ""

C_ID, C_UT, C_LT, C_SU, C_SL, C_ONES, C_MF, C_MB, C_E = range(9)
NCONST = 9


def host_consts():
    r = np.arange(128)[:, None]
    t = np.arange(128)[None, :]
    cs = np.zeros((NCONST, 128, 128), np.float32)
    cs[C_ID] = (r == t)
    cs[C_UT] = (r <= t)
    cs[C_LT] = (r >= t)
    cs[C_SU] = (r < t)
    cs[C_SL] = (r > t)
    cs[C_ONES] = 1.0
    cs[C_MF] = ((t // 16) >= (r // 16))
    cs[C_MB] = ((t // 16) <= (r // 16))
    cs[C_E] = ((t % 16) == r)
    return np.ascontiguousarray(cs.transpose(1, 0, 2).reshape(128, NCONST * 128))


class Buf:
    __slots__ = ("w", "r", "excl")

    def __init__(self, excl=False):
        self.w = None
        self.r = {}
        self.excl = excl


class Eng:
    def __init__(self, K, name, h, self_sync):
        self.K, self.name, self.h, self.self_sync = K, name, h, self_sync
        self.sem = K.es.enter_context(K.nc.semaphore("e_" + name))
        self.sid = "e_" + name
        self.count = 0
        self.known = {}
        self.last = None
        self.dsems = []
        self.dcnt = []
        self.dpos = 0

    def wait(self, tok):
        if tok is None:
            return
        sid, sem, val = tok
        if self.known.get(sid, 0) >= val:
            return
        self.h.wait_ge(sem, val)
        self.known[sid] = val

    def issue(self, ins):
        self.count += 1
        ins.then_inc(self.sem, 1)
        tok = (self.sid, self.sem, self.count)
        if not self.self_sync:
            self.known[self.sid] = self.count
        self.last = tok
        return tok


class Kern:
    def __init__(self, nc, es, ndma=12):
        self.nc, self.es = nc, es
        self.pe = Eng(self, "pe", nc.tensor, False)
        self.act = Eng(self, "act", nc.scalar, True)
        self.dve = Eng(self, "dve", nc.vector, True)
        self.pool = Eng(self, "pool", nc.gpsimd, True)
        self.sp = Eng(self, "sp", nc.sync, False)
        self.engs = [self.pe, self.act, self.dve, self.pool, self.sp]
        for i in range(ndma):
            self.sp.dsems.append(es.enter_context(nc.semaphore("d_%d" % i)))
            self.sp.dcnt.append(0)
        self.dma_toks = [None] * ndma

    def _deps(self, eng, reads, writes):
        for b in reads:
            eng.wait(b.w)
        for b in writes:
            eng.wait(b.w)
            for t in b.r.values():
                eng.wait(t)

    def _mark(self, tok, reads, writes):
        for b in writes:
            b.w = tok
            b.r = {}
        for b in reads:
            b.r[tok[0]] = tok

    def op(self, eng, emit, reads=(), writes=()):
        ex = [b for b in reads if b.excl]
        if ex:
            reads = [b for b in reads if not b.excl]
            writes = list(writes) + ex
        self._deps(eng, reads, writes)
        tok = eng.issue(emit())
        self._mark(tok, reads, writes)
        return tok

    def dma(self, out, in_, reads=(), writes=(), **kw):
        eng = self.sp
        self._deps(eng, reads, writes)
        j = eng.dpos
        eng.dpos = (eng.dpos + 1) % len(eng.dsems)
        eng.wait(self.dma_toks[j])
        eng.dcnt[j] += 16
        eng.h.dma_start(out=out, in_=in_, **kw).then_inc(eng.dsems[j], 16)
        tok = ("d_%d" % j, eng.dsems[j], eng.dcnt[j])
        self.dma_toks[j] = tok
        self._mark(tok, reads, writes)
        return tok

    def barrier(self):
        toks = [e.last for e in self.engs if e.last is not None] + [t for t in self.dma_toks if t is not None]
        for e in self.engs:
            for t in toks:
                if t[0] == e.sid and not e.self_sync:
                    continue
                e.wait(t)


def build_program(NU, CPU, debug=False, LVL=9):
    NCH = NU * CPU
    nc = bass.Bass("TRN2", target_bir_lowering=False)
    dt_in = lambda n, s, d=F32: nc.dram_tensor(n, list(s), d, kind="ExternalInput").ap()
    kind_dbg = "ExternalOutput" if debug else "Internal"
    dt_sc = lambda n, s, d=F32: nc.dram_tensor(n, list(s), d, kind=kind_dbg).ap()

    x_in = dt_in("x", [NCH, 128, D])
    seqmask_in = dt_in("seqmask", [128, NCH + 1])
    consts_in = dt_in("consts", [128, NCONST * 128])
    P = {}
    for n, s in [("norm1_w", (DEPTH, D)), ("w_in", (DEPTH, D, DPROJ)), ("lam_re", (DEPTH, 2, S5G, S5P)),
                 ("lam_im", (DEPTH, 2, S5G, S5P)), ("log_dt", (DEPTH, 2, S5G)),
                 ("b_re", (DEPTH, 2, S5G, S5P, 16)), ("b_im", (DEPTH, 2, S5G, S5P, 16)),
                 ("c_re", (DEPTH, 2, S5G, 16, S5P)), ("c_im", (DEPTH, 2, S5G, 16, S5P)),
                 ("d_s5", (DEPTH, S5W)), ("w_glu", (DEPTH, S5W, S5W)), ("b_glu", (DEPTH, S5W)),
                 ("w_s5_out", (DEPTH, S5W, D)), ("conv_w", (DEPTH, 4, XBC)), ("conv_b", (DEPTH, XBC)),
                 ("dt_bias", (DEPTH, 2, M2H)), ("a_log", (DEPTH, 2, M2H)), ("d_m2", (DEPTH, M2H)),
                 ("m2_norm_w", (DEPTH, M2I)), ("w_m2_out", (DEPTH, M2I, D)), ("w_o", (DEPTH, D, D)),
                 ("norm2_w", (DEPTH, D)), ("w_up", (DEPTH, D, DFF)), ("w_down", (DEPTH, DFF, D)),
                 ("final_norm_w", (D,))]:
        P[n] = dt_in(n, s)
    y_out = nc.dram_tensor("y", [NCH, 128, D], F32, kind="ExternalOutput").ap()

    s_u = dt_sc("s_u", [NCH, 128, S5W], BF16)
    s_z = dt_sc("s_z", [NCH, 128, M2I], BF16)
    s_dt = dt_sc("s_dt", [NCH, 128, 48], F32)
    s_gate = dt_sc("s_gate", [NCH, 128, 2 * D], BF16)
    s_xbc = dt_sc("s_xbc", [NCH, 128, 20, 128], BF16)
    s_ys5 = dt_sc("s_ys5", [NCH, 128, S5W], F32)
    s_ym2 = dt_sc("s_ym2", [NCH, 128, M2I], F32)
    s_x1 = dt_sc("s_x1", [NCH, 128, D], F32)
    s_x2 = dt_sc("s_x2", [NCH, 128, D], F32)
    s_dbg1 = dt_sc("s_dbg1", [NCH, 128, D], F32)
    s_dbg2 = dt_sc("s_dbg2", [NCH, 128, D], F32)
    B_u = [Buf() for _ in range(NCH)]
    B_z = [Buf() for _ in range(NCH)]
    B_dt = [Buf() for _ in range(NCH)]
    B_gate = [Buf() for _ in range(NCH)]
    B_xbc = [Buf() for _ in range(NCH)]
    B_ys5 = [Buf() for _ in range(NCH)]
    B_ym2 = [Buf() for _ in range(NCH)]
    B_x1 = [Buf() for _ in range(NCH)]
    B_x2 = [Buf() for _ in range(NCH)]

    with ExitStack() as es:
        K = Kern(nc, es)
        pe, act, dve, pool = K.pe, K.act, K.dve, K.pool
        uid = [0]

        def sb(es_, shape, dtype=F32, name=None):
            uid[0] += 1
            t = es_.enter_context(nc.sbuf_tensor("%s_%d" % (name or "t", uid[0]), list(shape), dtype))
            return t, Buf()

        psum = es.enter_context(nc.psum_tensor("psum", [128, 8, 512], F32))
        PB = [Buf(excl=True) for _ in range(8)]
        consts, Bc = sb(es, [128, NCONST, 128], F32, "consts")
        ident_bf, Bidb = sb(es, [128, 128], BF16, "identbf")
        seqm, Bsm = sb(es, [128, NCH + 1], F32, "seqm")
        K.dma(consts[:].rearrange("p a b -> p (a b)"), consts_in[:, :], writes=[Bc])
        K.dma(seqm[:], seqmask_in[:, :], writes=[Bsm])
        K.op(dve, lambda: nc.vector.tensor_copy(out=ident_bf[:], in_=consts[:, C_ID, :]), [Bc], [Bidb])
        CN = lambda i: consts[:, i, :]

        def bank_bf(b):
            return psum[:, b, :].bitcast(BF16)

        def load_weight(es_, src2d, KT, C, name, stage, scale_engs, pre=None):
            wt, Bw = pre if pre is not None else sb(es_, [128, KT, C], BF16, name)
            CB = 1024
            i = 0
            for kt in range(KT):
                for c0 in range(0, C, CB):
                    c1 = min(C, c0 + CB)
                    st, Bs = stage[i % len(stage)]
                    K.dma(st[:, 0:c1 - c0], src2d[kt * 128:(kt + 1) * 128, c0:c1], writes=[Bs])
                    e = scale_engs[i % len(scale_engs)]
                    if e is act:
                        K.op(act, lambda: nc.scalar.activation(out=wt[:, kt, c0:c1], in_=st[:, 0:c1 - c0], func=AF.Copy), [Bs], [Bw])
                    elif e is dve:
                        K.op(dve, lambda: nc.vector.tensor_copy(out=wt[:, kt, c0:c1], in_=st[:, 0:c1 - c0]), [Bs], [Bw])
                    else:
                        K.op(pool, lambda: nc.gpsimd.tensor_copy(out=wt[:, kt, c0:c1], in_=st[:, 0:c1 - c0]), [Bs], [Bw])
                    i += 1
            return wt, Bw

        def bcast_row(es_, src1d, n, name):
            t, B = sb(es_, [128, n], F32, name)
            K.dma(t[:], src1d.partition_broadcast(128), writes=[B])
            return t, B

        def rms_rstd(xt, Bx, n, ss, Bss, junk, Bj):
            K.op(act, lambda: nc.scalar.activation(out=junk, in_=xt, func=AF.Square, accum_out=ss[:, 0:1]), [Bx], [Bj, Bss])
            K.op(dve, lambda: nc.vector.tensor_scalar(out=ss[:, 0:1], in0=ss[:, 0:1], scalar1=1.0 / n, scalar2=EPS, op0=ALU.mult, op1=ALU.add), [Bss], [Bss])
            K.op(act, lambda: nc.scalar.activation(out=ss[:, 0:1], in_=ss[:, 0:1], func=AF.Sqrt), [Bss], [Bss])
            K.op(dve, lambda: nc.vector.reciprocal(out=ss[:, 0:1], in_=ss[:, 0:1]), [Bss], [Bss])

        def transpose_to(dst3, Bd, src, Bs_, ntile, bank, dtype_bf=True, evac=None):
            done = 0
            while done < ntile:
                n = min(8, ntile - done)
                b = bank + (done // 8)
                pb = bank_bf(b)
                for j in range(n):
                    jj = done + j
                    K.op(pe, lambda: nc.tensor.transpose(pb[:, j * 128:(j + 1) * 128], src[:, jj * 128:(jj + 1) * 128], ident_bf[:]), [Bs_, Bidb], [PB[b]])
                e = evac or act
                if e is act:
                    K.op(act, lambda: nc.scalar.activation(out=dst3[:, done:done + n, :], in_=pb[:, 0:n * 128].rearrange("p (a b) -> p a b", b=128), func=AF.Copy), [PB[b]], [Bd])
                else:
                    K.op(dve, lambda: nc.vector.tensor_copy(out=dst3[:, done:done + n, :], in_=pb[:, 0:n * 128].rearrange("p (a b) -> p a b", b=128)), [PB[b]], [Bd])
                done += n

        def mm_tok(out_bank, lhsT3, Bl, KT, w3, Bw, c0, c1):
            for kt in range(KT):
                K.op(pe, lambda: nc.tensor.matmul(psum[:, out_bank, 0:c1 - c0], lhsT=lhsT3[:, kt, :], rhs=w3[:, kt, c0:c1], start=(kt == 0), stop=(kt == KT - 1)), [Bl, Bw], [PB[out_bank]])

        def run_gens(gens):
            while gens:
                for e_ in list(gens):
                    if e_[1] > 0:
                        e_[1] -= 1
                        continue
                    try:
                        next(e_[0])
                    except StopIteration:
                        gens.remove(e_)

        for L in range(DEPTH):
            xsrc, Bxsrc = (x_in, None) if L == 0 else (s_x2, B_x2)
            with ExitStack() as ph:
                stage = [sb(ph, [128, 1024], F32, "stg") for _ in range(3)]
                Win, BWin = load_weight(ph, P["w_in"][L], 8, DPROJ, "win", stage, [act, dve])
                w1b, Bw1b = bcast_row(ph, P["norm1_w"][L], D, "w1b")
                dsb, Bdsb = bcast_row(ph, P["d_s5"][L], S5W, "dsb")
                NB = 2
                oy = [sb(ph, [128, S5W], F32, "oy") for _ in range(NB)]
                xt = [sb(ph, [128, D], F32, "x") for _ in range(NB)]
                junk = sb(ph, [128, D], BF16, "junk")
                ss = [sb(ph, [128, 1], F32, "ss") for _ in range(NB)]
                hb = [sb(ph, [128, D], BF16, "h") for _ in range(NB)]
                hT = [sb(ph, [128, 8, 128], BF16, "hT") for _ in range(NB)]
                ou = [sb(ph, [128, S5W], BF16, "ou") for _ in range(NB)]
                oz = [sb(ph, [128, M2I], BF16, "oz") for _ in range(NB)]
                odt = [sb(ph, [128, 48], F32, "odt") for _ in range(NB)]
                og = [sb(ph, [128, 2 * D], BF16, "og") for _ in range(NB)]
                ox = [sb(ph, [128, 20, 128], BF16, "ox") for _ in range(NB)]

                def p0_load(c):
                    t, B = xt[c % NB]
                    K.dma(t[:], xsrc[c], reads=([Bxsrc[c]] if Bxsrc else []), writes=[B])

                p0_load(0)
                bk = [0]

                def nb():
                    bk[0] = (bk[0] + 1) % 8
                    return bk[0]

                def p0_head(c):
                    i = c % NB
                    (x_, Bx), (ss_, Bss), (h_, Bh), (hT_, BhT) = xt[i], ss[i], hb[i], hT[i]
                    rms_rstd(x_[:], Bx, D, ss_, Bss, junk[0][:], junk[1])
                    K.op(dve, lambda: nc.vector.scalar_tensor_tensor(out=h_[:], in0=x_[:], scalar=ss_[:, 0:1], in1=w1b[:], op0=ALU.mult, op1=ALU.mult), [Bx, Bss, Bw1b], [Bh])
                    transpose_to(hT_, BhT, h_, Bh, 8, nb())

                p0_head(0)
                for c in range(NCH):
                    if c + 1 < NCH:
                        p0_load(c + 1)
                    i = c % NB
                    (x_, Bx), (ss_, Bss), (h_, Bh), (hT_, BhT) = xt[i], ss[i], hb[i], hT[i]
                    (u_, Bu_), (z_, Bz_), (d_, Bd_), (g_, Bg_), (xo_, Bxo_) = ou[i], oz[i], odt[i], og[i], ox[i]
                    blocks = [(OFF_U, 512, u_, 0, Bu_, "c"), (OFF_U + 512, 256, u_, 512, Bu_, "c"),
                              (OFF_Z, 512, z_, 0, Bz_, "c"), (OFF_Z + 512, 512, z_, 512, Bz_, "c"), (OFF_Z + 1024, 512, z_, 1024, Bz_, "c"),
                              (OFF_DT, 48, d_, 0, Bd_, "c")] + [(OFF_GATE + 512 * q, 512, g_, 512 * q, Bg_, "s") for q in range(4)]
                    for bi, (c0, n, dst, d0, Bdst, kind) in enumerate(blocks):
                        b = nb()
                        mm_tok(b, hT_, BhT, 8, Win, BWin, c0, c0 + n)
                        if False:
                            K.op(dve, lambda: nc.vector.tensor_tensor(out=oy[i][0][:, d0:d0 + n], in0=psum[:, b, 0:n], in1=dsb[:, d0:d0 + n], op=ALU.mult), [PB[b], Bdsb], [oy[i][1]])
                        if kind == "s":
                            K.op(act, lambda: nc.scalar.activation(out=dst[:, d0:d0 + n], in_=psum[:, b, 0:n], func=AF.Sigmoid), [PB[b]], [Bdst])
                        elif bi % 2 == 0:
                            K.op(dve, lambda: nc.vector.tensor_copy(out=dst[:, d0:d0 + n], in_=psum[:, b, 0:n]), [PB[b]], [Bdst])
                        else:
                            K.op(act, lambda: nc.scalar.activation(out=dst[:, d0:d0 + n], in_=psum[:, b, 0:n], func=AF.Copy), [PB[b]], [Bdst])
                    if c + 1 < NCH:
                        p0_head(c + 1)
                    for q in range(5):
                        b = nb()
                        for j in range(4):
                            col = OFF_XBC + (q * 4 + j) * 128
                            for kt in range(8):
                                K.op(pe, lambda: nc.tensor.matmul(psum[:, b, j * 128:(j + 1) * 128], lhsT=Win[:, kt, col:col + 128], rhs=hT_[:, kt, :], start=(kt == 0), stop=(kt == 7)), [BWin, BhT], [PB[b]])
                        if q % 2 == 0:
                            K.op(dve, lambda: nc.vector.tensor_copy(out=xo_[:, q * 4:q * 4 + 4, :], in_=psum[:, b, :].rearrange("p (a b) -> p a b", b=128)), [PB[b]], [Bxo_])
                        else:
                            K.op(act, lambda: nc.scalar.activation(out=xo_[:, q * 4:q * 4 + 4, :], in_=psum[:, b, :].rearrange("p (a b) -> p a b", b=128), func=AF.Copy), [PB[b]], [Bxo_])
                    K.dma(s_u[c], u_[:], reads=[Bu_], writes=[B_u[c]])
                    if False:
                        K.dma(s_ys5[c], oy[i][0][:], reads=[oy[i][1]], writes=[B_ys5[c]])
                    K.dma(s_z[c], z_[:], reads=[Bz_], writes=[B_z[c]])
                    K.dma(s_dt[c], d_[:], reads=[Bd_], writes=[B_dt[c]])
                    K.dma(s_gate[c], g_[:], reads=[Bg_], writes=[B_gate[c]])
                    K.dma(s_xbc[c], xo_[:], reads=[Bxo_], writes=[B_xbc[c]])
                K.barrier()

            for d in range(2):
                if LVL < 1 + d:
                    continue
                with ExitStack() as ph:
                    WY, BWY = sb(ph, [128, 48, 128], BF16, "WY")
                    WVr, BWVr = sb(ph, [128, 48, 64], BF16, "WVr")
                    WVi, BWVi = sb(ph, [128, 48, 64], BF16, "WVi")
                    WCr, BWCr = sb(ph, [128, 48, 128], BF16, "WCr")
                    WCi, BWCi = sb(ph, [128, 48, 128], BF16, "WCi")
                    K.op(dve, lambda: nc.vector.memset(WCr[:], 0.0), [], [BWCr])
                    K.op(dve, lambda: nc.vector.memset(WCi[:], 0.0), [], [BWCi])
                    A8, BA8 = sb(ph, [64, 2, 2, 48], F32, "A8")
                    with ExitStack() as tg:
                        Q = 64
                        def small(shape, name, dtype=F32):
                            return sb(tg, shape, dtype, name)
                        ln_, Bln = small([48, 2, 64], "lamn")
                        K.dma(ln_[:, 0, :], P["lam_re"][L, d], writes=[Bln])
                        K.dma(ln_[:, 1, :], P["lam_im"][L, d], writes=[Bln])
                        lam, Blam = small([Q, 2, 48], "lam")
                        for r_ in range(2):
                            K.op(pe, lambda: nc.tensor.transpose(psum[0:Q, 0, r_ * 48:(r_ + 1) * 48], ln_[:, r_, :], consts[0:48, C_ID, 0:48]), [Bln, Bc], [PB[0]])
                        K.op(dve, lambda: nc.vector.tensor_copy(out=lam[:].rearrange("p a b -> p (a b)"), in_=psum[0:Q, 0, 0:96]), [PB[0]], [Blam])
                        dl, Bdl = small([Q, 48], "dl")
                        K.dma(dl[:], P["log_dt"][L, d].partition_broadcast(Q), writes=[Bdl])
                        K.op(act, lambda: nc.scalar.activation(out=dl[:], in_=dl[:], func=AF.Exp), [Bdl], [Bdl])
                        lrd, Blrd = small([Q, 48], "lrd")
                        lid, Blid = small([Q, 48], "lid")
                        K.op(dve, lambda: nc.vector.tensor_tensor(out=lrd[:], in0=lam[:, 0, :], in1=dl[:], op=ALU.mult), [Blam, Bdl], [Blrd])
                        K.op(dve, lambda: nc.vector.tensor_tensor(out=lid[:], in0=lam[:, 1, :], in1=dl[:], op=ALU.mult), [Blam, Bdl], [Blid])
                        NE = 9
                        ea, Bea = small([Q, NE, 48], "ea")
                        eb, Beb = small([Q, NE, 48], "eb")
                        ec, Bec = small([Q, NE, 48], "ec")
                        ei, Bei = small([Q, NE, 48], "ei", mybir.dt.int32)
                        ef, Bef = small([Q, NE, 48], "ef")

                        def sin_of(dst, Bdst, n, shift):
                            K.op(dve, lambda: nc.vector.tensor_scalar(out=ec[:, 0:n, :], in0=eb[:, 0:n, :], scalar1=float(shift), scalar2=None, op0=ALU.add), [Beb], [Bec])
                            K.op(dve, lambda: nc.vector.tensor_copy(out=ei[:, 0:n, :], in_=ec[:, 0:n, :]), [Bec], [Bei])
                            K.op(dve, lambda: nc.vector.tensor_copy(out=ef[:, 0:n, :], in_=ei[:, 0:n, :]), [Bei], [Bef])
                            K.op(dve, lambda: nc.vector.tensor_tensor(out=ec[:, 0:n, :], in0=ec[:, 0:n, :], in1=ef[:, 0:n, :], op=ALU.subtract), [Bec, Bef], [Bec])
                            K.op(dve, lambda: nc.vector.tensor_scalar(out=ef[:, 0:n, :], in0=ec[:, 0:n, :], scalar1=0.5, scalar2=None, op0=ALU.is_gt), [Bec], [Bef])
                            K.op(dve, lambda: nc.vector.tensor_tensor(out=ec[:, 0:n, :], in0=ec[:, 0:n, :], in1=ef[:, 0:n, :], op=ALU.subtract), [Bec, Bef], [Bec])
                            K.op(dve, lambda: nc.vector.tensor_scalar(out=ef[:, 0:n, :], in0=ec[:, 0:n, :], scalar1=-0.5, scalar2=None, op0=ALU.is_lt), [Bec], [Bef])
                            K.op(dve, lambda: nc.vector.tensor_tensor(out=ec[:, 0:n, :], in0=ec[:, 0:n, :], in1=ef[:, 0:n, :], op=ALU.add), [Bec, Bef], [Bec])
                            K.op(act, lambda: nc.scalar.activation(out=dst[:, 0:n, :], in_=ec[:, 0:n, :], func=AF.Sin, scale=2.0 * PI), [Bec], [Bdst])

                        def powers(exps, sign, name):
                            n = len(exps)
                            pr, Bpr = small([Q, n, 48], name + "r")
                            pi_, Bpi = small([Q, n, 48], name + "i")
                            for j, e in enumerate(exps):
                                K.op(dve, lambda: nc.vector.tensor_scalar(out=ea[:, j, :], in0=lrd[:], scalar1=float(sign * e), scalar2=None, op0=ALU.mult), [Blrd], [Bea])
                                K.op(dve, lambda: nc.vector.tensor_scalar(out=eb[:, j, :], in0=lid[:], scalar1=float(e / (2.0 * PI)), scalar2=None, op0=ALU.mult), [Blid], [Beb])
                            K.op(act, lambda: nc.scalar.activation(out=ea[:, 0:n, :], in_=ea[:, 0:n, :], func=AF.Exp), [Bea], [Bea])
                            sin_of(pi_, Bpi, n, 0.0)
                            sin_of(pr, Bpr, n, 0.25)
                            K.op(dve, lambda: nc.vector.tensor_tensor(out=pr[:], in0=pr[:], in1=ea[:, 0:n, :], op=ALU.mult), [Bea, Bpr], [Bpr])
                            K.op(dve, lambda: nc.vector.scalar_tensor_tensor(out=pi_[:], in0=pi_[:], scalar=float(sign), in1=ea[:, 0:n, :], op0=ALU.mult, op1=ALU.mult), [Bea, Bpi], [Bpi])
                            return (pr, Bpr), (pi_, Bpi)

                        T8 = list(range(8))
                        if d == 0:
                            eX, eZ, eV, eC = T8, T8, [7 - t for t in T8], [t + 1 for t in T8]
                        else:
                            eX, eZ, eV, eC = [7 - t for t in T8], [7 - t for t in T8], T8, [8 - t for t in T8]
                        (a1r, Ba1r), (a1i, Ba1i) = powers([1, 8], 1, "a1")
                        K.op(dve, lambda: nc.vector.tensor_copy(out=A8[:, 0, 0, :], in_=a1r[:, 1, :]), [Ba1r], [BA8])
                        K.op(dve, lambda: nc.vector.tensor_copy(out=A8[:, 0, 1, :], in_=a1r[:, 1, :]), [Ba1r], [BA8])
                        K.op(dve, lambda: nc.vector.tensor_scalar(out=A8[:, 1, 0, :], in0=a1i[:, 1, :], scalar1=-1.0, scalar2=None, op0=ALU.mult), [Ba1i], [BA8])
                        K.op(dve, lambda: nc.vector.tensor_copy(out=A8[:, 1, 1, :], in_=a1i[:, 1, :]), [Ba1i], [BA8])
                        qt, Bqt = small([Q, 6, 48], "qt")
                        lr_, li_ = lam[:, 0, :], lam[:, 1, :]
                        K.op(dve, lambda: nc.vector.tensor_tensor(out=qt[:, 0, :], in0=lr_, in1=lr_, op=ALU.mult), [Blam], [Bqt])
                        K.op(dve, lambda: nc.vector.tensor_tensor(out=qt[:, 1, :], in0=li_, in1=li_, op=ALU.mult), [Blam], [Bqt])
                        K.op(dve, lambda: nc.vector.tensor_tensor(out=qt[:, 0, :], in0=qt[:, 0, :], in1=qt[:, 1, :], op=ALU.add), [Bqt], [Bqt])
                        K.op(dve, lambda: nc.vector.reciprocal(out=qt[:, 0, :], in_=qt[:, 0, :]), [Bqt], [Bqt])
                        K.op(dve, lambda: nc.vector.tensor_scalar(out=qt[:, 1, :], in0=a1r[:, 0, :], scalar1=-1.0, scalar2=None, op0=ALU.add), [Ba1r], [Bqt])
                        K.op(dve, lambda: nc.vector.tensor_tensor(out=qt[:, 2, :], in0=qt[:, 1, :], in1=lr_, op=ALU.mult), [Bqt, Blam], [Bqt])
                        K.op(dve, lambda: nc.vector.tensor_tensor(out=qt[:, 3, :], in0=a1i[:, 0, :], in1=li_, op=ALU.mult), [Ba1i, Blam], [Bqt])
                        K.op(dve, lambda: nc.vector.tensor_tensor(out=qt[:, 2, :], in0=qt[:, 2, :], in1=qt[:, 3, :], op=ALU.add), [Bqt], [Bqt])
                        K.op(dve, lambda: nc.vector.tensor_tensor(out=qt[:, 2, :], in0=qt[:, 2, :], in1=qt[:, 0, :], op=ALU.mult), [Bqt], [Bqt])
                        K.op(dve, lambda: nc.vector.tensor_tensor(out=qt[:, 3, :], in0=a1i[:, 0, :], in1=lr_, op=ALU.mult), [Ba1i, Blam], [Bqt])
                        K.op(dve, lambda: nc.vector.tensor_tensor(out=qt[:, 4, :], in0=qt[:, 1, :], in1=li_, op=ALU.mult), [Bqt, Blam], [Bqt])
                        K.op(dve, lambda: nc.vector.tensor_tensor(out=qt[:, 3, :], in0=qt[:, 3, :], in1=qt[:, 4, :], op=ALU.subtract), [Bqt], [Bqt])
                        K.op(dve, lambda: nc.vector.tensor_tensor(out=qt[:, 3, :], in0=qt[:, 3, :], in1=qt[:, 0, :], op=ALU.mult), [Bqt], [Bqt])
                        qr_b = qt[:, 2, :].unsqueeze(2).to_broadcast([Q, 48, 16])
                        qi_b = qt[:, 3, :].unsqueeze(2).to_broadcast([Q, 48, 16])
                        Bn, BBn = small([Q, 2, 48, 16], "Bn")
                        K.dma(Bn[:, 0], P["b_re"][L, d].rearrange("g p c -> p g c"), writes=[BBn])
                        K.dma(Bn[:, 1], P["b_im"][L, d].rearrange("g p c -> p g c"), writes=[BBn])
                        Bb, BBb = small([Q, 2, 48, 16], "Bbar")
                        tq, Btq = small([Q, 48, 16], "tq")
                        K.op(dve, lambda: nc.vector.tensor_tensor(out=Bb[:, 0], in0=Bn[:, 0], in1=qr_b, op=ALU.mult), [BBn, Bqt], [BBb])
                        K.op(dve, lambda: nc.vector.tensor_tensor(out=tq[:], in0=Bn[:, 1], in1=qi_b, op=ALU.mult), [BBn, Bqt], [Btq])
                        K.op(dve, lambda: nc.vector.tensor_tensor(out=Bb[:, 0], in0=Bb[:, 0], in1=tq[:], op=ALU.subtract), [BBb, Btq], [BBb])
                        K.op(dve, lambda: nc.vector.tensor_tensor(out=Bb[:, 1], in0=Bn[:, 1], in1=qr_b, op=ALU.mult), [BBn, Bqt], [BBb])
                        K.op(dve, lambda: nc.vector.tensor_tensor(out=tq[:], in0=Bn[:, 0], in1=qi_b, op=ALU.mult), [BBn, Bqt], [Btq])
                        K.op(dve, lambda: nc.vector.tensor_tensor(out=Bb[:, 1], in0=Bb[:, 1], in1=tq[:], op=ALU.add), [BBb, Btq], [BBb])
                        Cn, BCn = small([128, 2, 6, 64], "Cn")
                        K.dma(Cn[:, 0], P["c_re"][L, d].rearrange("(gt gi) c p -> (gi c) gt p", gi=8), writes=[BCn])
                        K.dma(Cn[:, 1], P["c_im"][L, d].rearrange("(gt gi) c p -> (gi c) gt p", gi=8), writes=[BCn])
                        Cc, BCc = small([Q, 2, 48, 16], "Cc")
                        for r_ in range(2):
                            for gt in range(6):
                                b = 1 + gt // 4
                                K.op(pe, lambda: nc.tensor.transpose(psum[0:Q, b, (gt % 4) * 128:(gt % 4 + 1) * 128], Cn[:, r_, gt, :], CN(C_ID)), [BCn, Bc], [PB[b]])
                            K.op(dve, lambda: nc.vector.tensor_copy(out=Cc[:, r_, 0:32, :].rearrange("p g c -> p (g c)"), in_=psum[0:Q, 1, :]), [PB[1]], [BCc])
                            K.op(dve, lambda: nc.vector.tensor_copy(out=Cc[:, r_, 32:48, :].rearrange("p g c -> p (g c)"), in_=psum[0:Q, 2, 0:256]), [PB[2]], [BCc])
                        (pXr, BpXr), (pXi, BpXi) = powers(eX, -1, "pX")
                        (pZr, BpZr), (pZi, BpZi) = powers(eZ, 1, "pZ")
                        (pVr, BpVr), (pVi, BpVi) = powers(eV, 1, "pV")
                        (pCr, BpCr), (pCi, BpCi) = powers(eC, 1, "pC")
                        GH = 24
                        tA, BtA = small([Q, GH, 8, 16], "tA")
                        tB, BtB = small([Q, GH, 8, 16], "tB")
                        Xr, BXr = small([Q, GH, 8, 16], "Xr")
                        Xi, BXi = small([Q, GH, 8, 16], "Xi")
                        Zr, BZr = small([Q, GH, 8, 16], "Zr")
                        Zi, BZi = small([Q, GH, 8, 16], "Zi")

                        def cmul(outr, Boutr, outi, Bouti, pw_r, Bpw_r, pw_i, Bpw_i, M, BM, g0, neg_im=False):
                            pr_b = pw_r[:, :, g0:g0 + GH].rearrange("p t g -> p g t").unsqueeze(3).to_broadcast([Q, GH, 8, 16])
                            pi_b = pw_i[:, :, g0:g0 + GH].rearrange("p t g -> p g t").unsqueeze(3).to_broadcast([Q, GH, 8, 16])
                            mr_b = M[:, 0, g0:g0 + GH, :].unsqueeze(2).to_broadcast([Q, GH, 8, 16])
                            mi_b = M[:, 1, g0:g0 + GH, :].unsqueeze(2).to_broadcast([Q, GH, 8, 16])
                            K.op(dve, lambda: nc.vector.tensor_tensor(out=tA[:], in0=pr_b, in1=mr_b, op=ALU.mult), [Bpw_r, BM], [BtA])
                            K.op(dve, lambda: nc.vector.tensor_tensor(out=tB[:], in0=pi_b, in1=mi_b, op=ALU.mult), [Bpw_i, BM], [BtB])
                            K.op(dve, lambda: nc.vector.tensor_tensor(out=outr, in0=tA[:], in1=tB[:], op=ALU.subtract), [BtA, BtB], [Boutr])
                            K.op(dve, lambda: nc.vector.tensor_tensor(out=tA[:], in0=pr_b, in1=mi_b, op=ALU.mult), [Bpw_r, BM], [BtA])
                            K.op(dve, lambda: nc.vector.tensor_tensor(out=tB[:], in0=pi_b, in1=mr_b, op=ALU.mult), [Bpw_i, BM], [BtB])
                            if neg_im:
                                K.op(dve, lambda: nc.vector.scalar_tensor_tensor(out=outi, in0=tA[:], scalar=-1.0, in1=tB[:], op0=ALU.mult, op1=ALU.subtract), [BtA, BtB], [Bouti])
                            else:
                                K.op(dve, lambda: nc.vector.tensor_tensor(out=outi, in0=tA[:], in1=tB[:], op=ALU.add), [BtA, BtB], [Bouti])

                        MK = CN(C_MF) if d == 0 else CN(C_MB)
                        for g0 in (0, GH):
                            cmul(Xr[:], BXr, Xi[:], BXi, pXr, BpXr, pXi, BpXi, Bb, BBb, g0)
                            cmul(Zr[:], BZr, Zi[:], BZi, pZr, BpZr, pZi, BpZi, Cc, BCc, g0, neg_im=True)
                            for q4 in range(GH // 4):
                                b = 3 + (q4 % 2)
                                for j in range(4):
                                    g = q4 * 4 + j
                                    K.op(pe, lambda: nc.tensor.matmul(psum[:, b, j * 128:(j + 1) * 128], lhsT=Xr[:, g].rearrange("p t c -> p (t c)"), rhs=Zr[:, g].rearrange("p t c -> p (t c)"), start=True, stop=False), [BXr, BZr], [PB[b]])
                                    K.op(pe, lambda: nc.tensor.matmul(psum[:, b, j * 128:(j + 1) * 128], lhsT=Xi[:, g].rearrange("p t c -> p (t c)"), rhs=Zi[:, g].rearrange("p t c -> p (t c)"), start=False, stop=True), [BXi, BZi], [PB[b]])
                                K.op(dve, lambda: nc.vector.tensor_tensor(out=WY[:, g0 + q4 * 4:g0 + q4 * 4 + 4, :], in0=psum[:, b, :].rearrange("p (a b) -> p a b", b=128), in1=MK.unsqueeze(1).to_broadcast([128, 4, 128]), op=ALU.mult), [PB[b], Bc], [BWY])
                            cmul(Xr[:], BXr, Xi[:], BXi, pVr, BpVr, pVi, BpVi, Bb, BBb, g0)
                            for (src, Bsrc, dstW, BdstW) in ((Xr, BXr, WVr, BWVr), (Xi, BXi, WVi, BWVi)):
                                for q8 in range(GH // 8):
                                    b = 5 + (q8 % 2)
                                    for j in range(8):
                                        g = q8 * 8 + j
                                        K.op(pe, lambda: nc.tensor.transpose(psum[:, b, j * 64:(j + 1) * 64], src[:, g].rearrange("p t c -> p (t c)"), consts[0:Q, C_ID, 0:Q]), [Bsrc, Bc], [PB[b]])
                                    K.op(act, lambda: nc.scalar.activation(out=dstW[:, g0 + q8 * 8:g0 + q8 * 8 + 8, :], in_=psum[:, b, :].rearrange("p (a b) -> p a b", b=64), func=AF.Copy), [PB[b]], [BdstW])
                            pr_b = pCr[:, :, g0:g0 + GH].rearrange("p t g -> p g t").unsqueeze(3).to_broadcast([Q, GH, 8, 16])
                            pi_b = pCi[:, :, g0:g0 + GH].rearrange("p t g -> p g t").unsqueeze(3).to_broadcast([Q, GH, 8, 16])
                            mr_b = Cc[:, 0, g0:g0 + GH, :].unsqueeze(2).to_broadcast([Q, GH, 8, 16])
                            mi_b = Cc[:, 1, g0:g0 + GH, :].unsqueeze(2).to_broadcast([Q, GH, 8, 16])
                            wcr_v = WCr[0:Q, g0:g0 + GH, :].rearrange("p g (t c) -> p g t c", c=16)
                            wci_v = WCi[0:Q, g0:g0 + GH, :].rearrange("p g (t c) -> p g t c", c=16)
                            K.op(dve, lambda: nc.vector.tensor_tensor(out=tA[:], in0=pr_b, in1=mr_b, op=ALU.mult), [BpCr, BCc], [BtA])
                            K.op(dve, lambda: nc.vector.tensor_tensor(out=tB[:], in0=pi_b, in1=mi_b, op=ALU.mult), [BpCi, BCc], [BtB])
                            K.op(dve, lambda: nc.vector.tensor_tensor(out=wcr_v, in0=tA[:], in1=tB[:], op=ALU.subtract), [BtA, BtB], [BWCr])
                            K.op(dve, lambda: nc.vector.tensor_tensor(out=tA[:], in0=pr_b, in1=mi_b, op=ALU.mult), [BpCr, BCc], [BtA])
                            K.op(dve, lambda: nc.vector.tensor_tensor(out=tB[:], in0=pi_b, in1=mr_b, op=ALU.mult), [BpCi, BCc], [BtB])
                            K.op(dve, lambda: nc.vector.scalar_tensor_tensor(out=wci_v, in0=tA[:], scalar=-1.0, in1=tB[:], op0=ALU.mult, op1=ALU.subtract), [BtA, BtB], [BWCi])
                        if d == 0:
                            dn, Bdn = small([48, 16], "dn")
                            dT, BdT = small([16, 48], "dT")
                            dsk, Bdsk = small([128, 48], "dsk")
                            K.dma(dn[:], P["d_s5"][L].rearrange("(g c) -> g c", c=16), writes=[Bdn])
                            K.op(pe, lambda: nc.tensor.transpose(psum[0:16, 7, 0:48], dn[:], consts[0:48, C_ID, 0:48]), [Bdn, Bc], [PB[7]])
                            K.op(dve, lambda: nc.vector.tensor_copy(out=dT[:], in_=psum[0:16, 7, 0:48]), [PB[7]], [BdT])
                            K.op(pe, lambda: nc.tensor.matmul(psum[:, 7, 64:112], lhsT=consts[0:16, C_E, :], rhs=dT[:], start=True, stop=True), [Bc, BdT], [PB[7]])
                            K.op(dve, lambda: nc.vector.tensor_copy(out=dsk[:], in_=psum[:, 7, 64:112]), [PB[7]], [Bdsk])
                            for g in range(48):
                                K.op(dve, lambda: nc.vector.scalar_tensor_tensor(out=WY[:, g, :], in0=CN(C_ID), scalar=dsk[:, g:g + 1], in1=WY[:, g, :], op0=ALU.mult, op1=ALU.add), [Bc, Bdsk, BWY], [BWY])
                        K.barrier()

                    cw, Bcw = sb(ph, [128, 20, 4], F32, "cw")
                    cbias, Bcb = sb(ph, [128, 20], F32, "cbias")
                    with ExitStack() as tc_:
                        cwn, Bcwn = sb(tc_, [4, 20, 128], F32, "cwn")
                        cbn, Bcbn = sb(tc_, [20, 128], F32, "cbn")
                        K.dma(cwn[:], P["conv_w"][L].rearrange("j (t p) -> j t p", p=128), writes=[Bcwn])
                        for t_ in range(20):
                            K.op(pe, lambda: nc.tensor.transpose(psum[:, 0, t_ * 4:(t_ + 1) * 4], cwn[:, t_, :], consts[0:4, C_ID, 0:4]), [Bcwn, Bc], [PB[0]])
                        K.op(dve, lambda: nc.vector.tensor_copy(out=cw[:].rearrange("p a b -> p (a b)"), in_=psum[:, 0, 0:80]), [PB[0]], [Bcw])
                        K.dma(cbn[:], P["conv_b"][L].rearrange("(t p) -> t p", p=128), writes=[Bcbn])
                        K.op(pe, lambda: nc.tensor.transpose(psum[:, 1, 0:20], cbn[:], consts[0:20, C_ID, 0:20]), [Bcbn, Bc], [PB[1]])
                        K.op(dve, lambda: nc.vector.tensor_copy(out=cbias[:], in_=psum[:, 1, 0:20]), [PB[1]], [Bcb])
                        K.barrier()
                    CD, BCD = sb(ph, [128, 20, 4, 128], BF16, "CD")
                    for t_ in range(20):
                        for j in range(4):
                            if (t_ * 4 + j) % 2 == 0:
                                K.op(dve, lambda: nc.vector.tensor_scalar(out=CD[:, t_, j, :], in0=CN(C_ID), scalar1=cw[:, t_, j:j + 1], scalar2=None, op0=ALU.mult), [Bc, Bcw], [BCD])
                            else:
                                K.op(act, lambda: nc.scalar.activation(out=CD[:, t_, j, :], in_=CN(C_ID), func=AF.Copy, scale=cw[:, t_, j:j + 1]), [Bc, Bcw], [BCD])
                    dtb, Bdtb = bcast_row(ph, P["dt_bias"][L, d], 24, "dtb")
                    Arow, BAr = bcast_row(ph, P["a_log"][L, d], 24, "Arow")
                    K.op(act, lambda: nc.scalar.activation(out=Arow[:], in_=Arow[:], func=AF.Exp), [BAr], [BAr])
                    K.op(dve, lambda: nc.vector.tensor_scalar(out=Arow[:], in0=Arow[:], scalar1=-1.0, scalar2=None, op0=ALU.mult), [BAr], [BAr])
                    Drow, BDr = bcast_row(ph, P["d_m2"][L], 24, "Drow")
                    TRI = CN(C_UT) if d == 0 else CN(C_LT)
                    MST = CN(C_SL) if d == 0 else CN(C_SU)
                    MCB = CN(C_UT) if d == 0 else CN(C_LT)
                    xe = [sb(ph, [128, 20, 131], BF16, "xe") for _ in range(3)]
                    dtr = [sb(ph, [128, 24], F32, "dtr") for _ in range(3)]
                    uk, Buk = sb(ph, [16, 8, S5W], BF16, "uk")
                    ukg, Bukg = sb(ph, [16, 48, 8, 16], BF16, "ukg")
                    Ut2 = [sb(ph, [128, 48, 16], BF16, "Ut") for _ in range(2)]
                    Vsb, BVsb = sb(ph, [64, 16, 2, 48], F32, "Vsb")
                    Hh, BHh = sb(ph, [64, 17, 4, 48], F32, "Hh")
                    Sbf2 = [sb(ph, [128, 2, 48, 16], BF16, "Sbf") for _ in range(2)]
                    for t_, Bt_ in Sbf2:
                        K.op(dve, lambda: nc.vector.memset(t_[:], 0.0), [], [Bt_])
                    Qt, BQt = sb(ph, [64, 2, 2, 48], F32, "Qt")
                    Rt, BRt = sb(ph, [64, 2, 2, 48], F32, "Rt")
                    part = [sb(ph, [16, 8, 128], F32, "part") for _ in range(2)]
                    ysc = [sb(ph, [16, 8, 128], F32, "ysc") for _ in range(2)]
                    xS, BxS = sb(ph, [128, 20, 128], BF16, "xS")
                    xdt, Bxdt = sb(ph, [128, 24, 64], BF16, "xdt")
                    xdin, Bxdin = sb(ph, [128, 24, 64], BF16, "xdin")
                    Btok, BBtok = sb(ph, [128, 4, 128], BF16, "Btok")
                    dts, Bdts = sb(ph, [128, 8, 24], F32, "dts")
                    LHh2 = [sb(ph, [128, 6, 128], BF16, "LHh") for _ in range(2)]
                    LHl2 = [sb(ph, [128, 6, 128], BF16, "LHl") for _ in range(2)]
                    cb16, Bcb16 = sb(ph, [128, 2, 128], BF16, "cb16")
                    K.op(dve, lambda: nc.vector.tensor_copy(out=cb16[:, 0, :], in_=TRI), [Bc], [Bcb16])
                    K.op(dve, lambda: nc.vector.tensor_copy(out=cb16[:, 1, :], in_=MST), [Bc], [Bcb16])
                    TRIb, MSTb = cb16[:, 0, :], cb16[:, 1, :]
                    dthl, Bdthl = sb(ph, [128, 2, 24], BF16, "dthl")
                    dtf, Bdtf = sb(ph, [128, 24], F32, "dtf")
                    dlo, Bdlo = sb(ph, [128, 24], F32, "dlo")
                    Dm2 = [sb(ph, [128, 6, 128], BF16, "Dm") for _ in range(2)]
                    MT2 = [sb(ph, [128, 6, 128], BF16, "MT") for _ in range(2)]
                    cbm, Bcbm = sb(ph, [128, 4, 128], BF16, "cbm")
                    Hst, BHst = sb(ph, [128, 24, 64], F32, "Hst")
                    Hb, BHb = sb(ph, [128, 24, 64], BF16, "Hb")
                    t1s = [sb(ph, [128, 6, 64], F32, "t1") for _ in range(2)]
                    pm2, Bpm2 = sb(ph, [128, 24, 64], F32, "pm2")
                    if debug and L == 0:
                        print('SBUF remaining in pass d=%d: %d bytes' % (d, nc.sbuf_bytes_remaining))
                    K.op(dve, lambda: nc.vector.memset(Hst[:], 0.0), [], [BHst])
                    K.op(dve, lambda: nc.vector.memset(Hh[:], 0.0), [], [BHh])
                    for t_, Bt_ in xe:
                        K.op(pool, lambda: nc.gpsimd.memset(t_[:], 0.0), [], [Bt_])
                    order = list(range(NCH)) if d == 0 else list(range(NCH - 1, -1, -1))
                    su_v = lambda c: s_u[c].rearrange("(k t) c -> k t c", t=8)
                    sy_v = lambda c: s_ys5[c].rearrange("(k t) c -> k t c", t=8)
                    A8a = A8[:, 0].unsqueeze(1).to_broadcast([64, 2, 2, 48])
                    A8b = A8[:, 1].unsqueeze(1).to_broadcast([64, 2, 2, 48])

                    def load_xe(c):
                        t_, Bt_ = xe[c % 3]
                        K.dma(t_[:, :, 1:129], s_xbc[c], reads=[B_xbc[c]], writes=[Bt_])
                        q_, Bq_ = dtr[c % 3]
                        K.dma(q_[:], s_dt[c][:, d * 24:(d + 1) * 24], reads=[B_dt[c]], writes=[Bq_])

                    def load_u(c):
                        K.dma(uk[:], su_v(c), reads=[B_u[c]], writes=[Buk])

                    def load_part(c, ct):
                        p_, Bp_ = part[ct % 2]
                        K.dma(p_[:], sy_v(c)[:, :, ct * 128:(ct + 1) * 128], reads=[B_ys5[c]], writes=[Bp_])

                    def gen_s5ab(it, c):
                        Ut, BUt = Ut2[it % 2]
                        Sbf, BSbf = Sbf2[it % 2]
                        m_in64 = seqm[0:64, c:c + 1] if d == 0 else seqm[0:64, c + 1:c + 2]
                        K.op(dve, lambda: nc.vector.tensor_copy(out=ukg[:, 0:24], in_=uk[:, :, 0:384].rearrange("k t (g c) -> k g t c", c=16)), [Buk], [Bukg])
                        K.op(dve, lambda: nc.vector.tensor_copy(out=ukg[:, 24:48], in_=uk[:, :, 384:768].rearrange("k t (g c) -> k g t c", c=16)), [Buk], [Bukg])
                        if it + 1 < NCH:
                            load_u(order[it + 1])
                        pb = bank_bf(0)
                        for g in range(48):
                            K.op(pe, lambda: nc.tensor.transpose(pb[:, g * 16:(g + 1) * 16], ukg[:, g].rearrange("k t c -> k (t c)"), ident_bf[0:16, 0:16]), [Bukg, Bidb], [PB[0]])
                        K.op(dve, lambda: nc.vector.tensor_copy(out=Ut[:], in_=pb[:, 0:768].rearrange("p (a b) -> p a b", b=16)), [PB[0]], [BUt])
                        yield
                        for r3 in range(3):
                            b = 1 if r3 % 2 == 0 else 0
                            for gi in range(16):
                                g = r3 * 16 + gi
                                K.op(pe, lambda: nc.tensor.matmul(psum[0:64, b, gi * 16:(gi + 1) * 16], lhsT=WVr[:, g, :], rhs=Ut[:, g, :], start=True, stop=True), [BWVr, BUt], [PB[b]])
                                K.op(pe, lambda: nc.tensor.matmul(psum[0:64, b, 256 + gi * 16:256 + (gi + 1) * 16], lhsT=WVi[:, g, :], rhs=Ut[:, g, :], start=True, stop=True), [BWVi, BUt], [PB[b]])
                            src = psum[0:64, b, :].rearrange("p (r g k) -> p k r g", r=2, g=16)
                            dst = Vsb[:, :, :, r3 * 16:(r3 + 1) * 16]
                            K.op(act, lambda: nc.scalar.activation(out=dst, in_=src, func=AF.Copy), [PB[b]], [BVsb])
                        yield
                        hin0 = 0 if d == 0 else 16
                        hprev_last = 16 if d == 0 else 0
                        if it == 0:
                            K.op(dve, lambda: nc.vector.memset(Hh[:, hin0], 0.0), [], [BHh])
                        else:
                            K.op(dve, lambda: nc.vector.tensor_scalar(out=Hh[:, hin0], in0=Hh[:, hprev_last], scalar1=m_in64, scalar2=None, op0=ALU.mult), [BHh, Bsm], [BHh])
                        ks = list(range(16)) if d == 0 else list(range(15, -1, -1))
                        for k_ in ks:
                            si, so = (k_, k_ + 1) if d == 0 else (k_ + 1, k_)
                            HA = Hh[:, si, 0:2, :].unsqueeze(1).to_broadcast([64, 2, 2, 48])
                            HB = Hh[:, si, 1:3, :].unsqueeze(1).to_broadcast([64, 2, 2, 48])
                            Vb = Vsb[:, k_].unsqueeze(1).to_broadcast([64, 2, 2, 48])
                            K.op(dve, lambda: nc.vector.tensor_tensor(out=Qt[:], in0=A8a, in1=HA, op=ALU.mult), [BA8, BHh], [BQt])
                            K.op(dve, lambda: nc.vector.tensor_tensor(out=Rt[:], in0=A8b, in1=HB, op=ALU.mult), [BA8, BHh], [BRt])
                            K.op(dve, lambda: nc.vector.tensor_tensor(out=Qt[:], in0=Qt[:], in1=Rt[:], op=ALU.add), [BQt, BRt], [BQt])
                            K.op(dve, lambda: nc.vector.tensor_tensor(out=Hh[:, so].rearrange("p (a b) g -> p a b g", a=2), in0=Qt[:], in1=Vb, op=ALU.add), [BQt, BVsb], [BHh])
                            yield
                        hs0 = 0 if d == 0 else 1
                        K.op(act, lambda: nc.scalar.activation(out=Sbf[0:64], in_=Hh[:, hs0:hs0 + 16, 0:2, :].rearrange("p k r g -> p r g k"), func=AF.Copy), [BHh], [BSbf])

                    def gen_s5c(it, c):
                        Ut, BUt = Ut2[it % 2]
                        Sbf, BSbf = Sbf2[it % 2]
                        if d == 1:
                            load_part(c, 0)
                            load_part(c, 1)
                        for ct in range(6):
                            y_, By_ = ysc[ct % 2]
                            for hf in range(2):
                                b = 2 + hf
                                for g4 in range(4):
                                    g = ct * 8 + hf * 4 + g4
                                    o_ = psum[0:16, b, g4 * 128:(g4 + 1) * 128]
                                    K.op(pe, lambda: nc.tensor.matmul(o_, lhsT=Ut[:, g, :], rhs=WY[:, g, :], start=True, stop=False), [BUt, BWY], [PB[b]])
                                    K.op(pe, lambda: nc.tensor.matmul(o_, lhsT=Sbf[:, 0, g, :], rhs=WCr[:, g, :], start=False, stop=False), [BSbf, BWCr], [PB[b]])
                                    K.op(pe, lambda: nc.tensor.matmul(o_, lhsT=Sbf[:, 1, g, :], rhs=WCi[:, g, :], start=False, stop=True), [BSbf, BWCi], [PB[b]])
                                ov = y_[:, :, hf * 64:(hf + 1) * 64].rearrange("k t (g c) -> k g t c", c=16)
                                pv = psum[0:16, b, :].rearrange("k (g t c) -> k g t c", g=4, t=8)
                                if d == 1:
                                    p_, Bp_ = part[ct % 2]
                                    K.op(dve, lambda: nc.vector.tensor_tensor(out=ov, in0=pv, in1=p_[:, :, hf * 64:(hf + 1) * 64].rearrange("k t (g c) -> k g t c", c=16), op=ALU.add), [PB[b], Bp_], [By_])
                                else:
                                    K.op(dve, lambda: nc.vector.tensor_copy(out=ov, in_=pv), [PB[b]], [By_])
                                yield
                            K.dma(sy_v(c)[:, :, ct * 128:(ct + 1) * 128], y_[:], reads=[By_], writes=[B_ys5[c]])
                            if d == 1 and ct + 2 < 6:
                                load_part(c, ct + 2)

                    def gen_ssd(it, c):
                        m_in = seqm[:, c:c + 1] if d == 0 else seqm[:, c + 1:c + 2]
                        xe_, Bxe_ = xe[c % 3]
                        if c - 1 >= 0:
                            xl, Bxl = xe[(c - 1) % 3]
                            K.op(dve, lambda: nc.vector.tensor_scalar(out=xe_[:, :, 0:1], in0=xl[:, :, 128:129], scalar1=seqm[:, c:c + 1], scalar2=None, op0=ALU.mult), [Bxl, Bsm], [Bxe_])
                        else:
                            K.op(dve, lambda: nc.vector.memset(xe_[:, :, 0:1], 0.0), [], [Bxe_])
                        if c + 1 < NCH:
                            xr, Bxr = xe[(c + 1) % 3]
                            K.op(dve, lambda: nc.vector.tensor_scalar(out=xe_[:, :, 129:131], in0=xr[:, :, 1:3], scalar1=seqm[:, c + 1:c + 2], scalar2=None, op0=ALU.mult), [Bxr, Bsm], [Bxe_])
                        else:
                            K.op(dve, lambda: nc.vector.memset(xe_[:, :, 129:131], 0.0), [], [Bxe_])
                        if it + 2 < NCH:
                            load_xe(order[it + 2])
                        q_, Bq_ = dtr[c % 3]
                        K.op(dve, lambda: nc.vector.tensor_tensor(out=dts[:, 7, :], in0=q_[:], in1=dtb[:], op=ALU.add), [Bq_, Bdtb], [Bdts])
                        K.op(act, lambda: nc.scalar.activation(out=dts[:, 7, :], in_=dts[:, 7, :], func=AF.Exp), [Bdts], [Bdts])
                        K.op(act, lambda: nc.scalar.activation(out=dts[:, 0, :], in_=dts[:, 7, :], func=AF.Ln, bias=1.0), [Bdts], [Bdts])
                        K.op(dve, lambda: nc.vector.tensor_tensor(out=dts[:, 1, :], in0=dts[:, 0, :], in1=Arow[:], op=ALU.mult), [Bdts, BAr], [Bdts])
                        K.op(dve, lambda: nc.vector.tensor_copy(out=dthl[:, 0, :], in_=dts[:, 1, :]), [Bdts], [Bdthl])
                        K.op(dve, lambda: nc.vector.tensor_copy(out=dtf[:], in_=dthl[:, 0, :]), [Bdthl], [Bdtf])
                        K.op(dve, lambda: nc.vector.tensor_tensor(out=dlo[:], in0=dts[:, 1, :], in1=dtf[:], op=ALU.subtract), [Bdts, Bdtf], [Bdlo])
                        K.op(pe, lambda: nc.tensor.matmul(psum[:, 6, 0:24], lhsT=TRI, rhs=dts[:, 1, :], start=True, stop=True), [Bc, Bdts], [PB[6]])
                        K.op(pe, lambda: nc.tensor.matmul(psum[:, 6, 32:56], lhsT=CN(C_ONES), rhs=dts[:, 1, :], start=True, stop=True), [Bc, Bdts], [PB[6]])
                        K.op(dve, lambda: nc.vector.tensor_copy(out=dts[:, 2, :], in_=psum[:, 6, 0:24]), [PB[6]], [Bdts])
                        K.op(act, lambda: nc.scalar.activation(out=dts[:, 3, :], in_=psum[:, 6, 0:24], func=AF.Exp), [PB[6]], [Bdts])
                        K.op(dve, lambda: nc.vector.tensor_tensor(out=dts[:, 7, :], in0=psum[:, 6, 32:56], in1=dts[:, 2, :], op=ALU.subtract), [PB[6], Bdts], [Bdts])
                        K.op(act, lambda: nc.scalar.activation(out=dts[:, 4, :], in_=dts[:, 7, :], func=AF.Exp), [Bdts], [Bdts])
                        K.op(act, lambda: nc.scalar.activation(out=dts[:, 5, :], in_=psum[:, 6, 32:56], func=AF.Exp), [PB[6]], [Bdts])
                        K.op(dve, lambda: nc.vector.tensor_scalar(out=dts[:, 6, :], in0=dts[:, 5, :], scalar1=m_in, scalar2=None, op0=ALU.mult), [Bdts, Bsm], [Bdts])
                        yield
                        for q in range(5):
                            b = 4 + (q % 2)
                            for j in range(4):
                                t_ = q * 4 + j
                                for tap in range(4):
                                    K.op(pe, lambda: nc.tensor.matmul(psum[:, b, j * 128:(j + 1) * 128], lhsT=CD[:, t_, tap, :], rhs=xe_[:, t_, tap:tap + 128], start=(tap == 0), stop=(tap == 3)), [BCD, Bxe_], [PB[b]])
                            for j in range(4):
                                t_ = q * 4 + j
                                K.op(act, lambda: nc.scalar.activation(out=xS[:, t_, :], in_=psum[:, b, j * 128:(j + 1) * 128], func=AF.Silu, bias=cbias[:, t_:t_ + 1]), [PB[b], Bcb], [BxS])
                            yield
                        for j in range(12):
                            b = 4 + j // 8
                            K.op(pe, lambda: nc.tensor.transpose(bank_bf(b)[:, (j % 8) * 128:(j % 8 + 1) * 128], xS[:, j, :], ident_bf[:]), [BxS, Bidb], [PB[b]])
                        for j in range(4):
                            K.op(pe, lambda: nc.tensor.transpose(bank_bf(5)[:, 512 + j * 128:512 + (j + 1) * 128], xS[:, 12 + j, :], ident_bf[:]), [BxS, Bidb], [PB[5]])
                        xtok = lambda h0, h1: (bank_bf(4)[:, h0 * 64:h1 * 64] if h1 <= 16 else bank_bf(5)[:, (h0 - 16) * 64:(h1 - 16) * 64]).rearrange("p (h q) -> p h q", q=64)
                        yield
                        for (h0, h1, b) in ((0, 16, 4), (16, 24, 5)):
                            dtv = dts[:, 0, h0:h1].unsqueeze(2).to_broadcast([128, h1 - h0, 64])
                            K.op(dve, lambda: nc.vector.tensor_tensor(out=xdt[:, h0:h1, :], in0=xtok(h0, h1), in1=dtv, op=ALU.mult), [PB[b], Bdts], [Bxdt])
                            if d == 0:
                                Dv = Drow[:, h0:h1].unsqueeze(2).to_broadcast([128, h1 - h0, 64])
                                K.op(dve, lambda: nc.vector.tensor_tensor(out=pm2[:, h0:h1, :], in0=xtok(h0, h1), in1=Dv, op=ALU.mult), [PB[b], BDr], [Bpm2])
                        K.op(act, lambda: nc.scalar.activation(out=Btok[:], in_=bank_bf(5)[:, 512:1024].rearrange("p (a b) -> p a b", b=128), func=AF.Copy), [PB[5]], [BBtok])
                        K.op(dve, lambda: nc.vector.tensor_tensor(out=xdin[:], in0=xdt[:], in1=dts[:, 4, :].unsqueeze(2).to_broadcast([128, 24, 64]), op=ALU.mult), [Bxdt, Bdts], [Bxdin])
                        K.op(act, lambda: nc.scalar.activation(out=Hb[:], in_=Hst[:], func=AF.Copy, scale=m_in), [BHst, Bsm], [BHb])
                        yield
                        for g in range(4):
                            K.op(pe, lambda: nc.tensor.matmul(psum[:, 7, g * 128:(g + 1) * 128], lhsT=xS[:, 12 + g, :], rhs=xS[:, 16 + g, :], start=True, stop=True), [BxS], [PB[7]])
                        K.op(dve, lambda: nc.vector.tensor_tensor(out=cbm[:], in0=psum[:, 7, :].rearrange("p (a b) -> p a b", b=128), in1=MCB.unsqueeze(1).to_broadcast([128, 4, 128]), op=ALU.mult), [PB[7], Bc], [Bcbm])
                        yield
                        def s1a(g):
                            hs = slice(g * 6, g * 6 + 6)
                            (LHh, BLHh), (LHl, BLHl) = LHh2[g % 2], LHl2[g % 2]
                            for j in range(6):
                                h = g * 6 + j
                                K.op(act, lambda: nc.scalar.activation(out=LHh[:, j, :], in_=MSTb, func=AF.Copy, scale=dtf[:, h:h + 1]), [Bcb16, Bdtf], [BLHh])
                                K.op(act, lambda: nc.scalar.activation(out=LHl[:, j, :], in_=MSTb, func=AF.Copy, scale=dlo[:, h:h + 1]), [Bcb16, Bdlo], [BLHl])
                            for j in range(6):
                                b = 4 + j // 4
                                o_ = psum[:, b, (j % 4) * 128:(j % 4 + 1) * 128]
                                K.op(pe, lambda: nc.tensor.matmul(o_, lhsT=LHh[:, j, :], rhs=TRIb, start=True, stop=False), [BLHh, Bcb16], [PB[b]])
                                K.op(pe, lambda: nc.tensor.matmul(o_, lhsT=LHl[:, j, :], rhs=TRIb, start=False, stop=True), [BLHl, Bcb16], [PB[b]])

                        def s1b(g):
                            (Dm, BDm), (MT, BMT) = Dm2[g % 2], MT2[g % 2]
                            K.op(act, lambda: nc.scalar.activation(out=Dm[:, 0:4, :], in_=psum[:, 4, :].rearrange("p (a b) -> p a b", b=128), func=AF.Exp), [PB[4]], [BDm])
                            K.op(act, lambda: nc.scalar.activation(out=Dm[:, 4:6, :], in_=psum[:, 5, 0:256].rearrange("p (a b) -> p a b", b=128), func=AF.Exp), [PB[5]], [BDm])
                            K.op(dve, lambda: nc.vector.tensor_tensor(out=MT[:], in0=Dm[:], in1=cbm[:, g:g + 1, :].to_broadcast([128, 6, 128]), op=ALU.mult), [BDm, Bcbm], [BMT])

                        def s2(g):
                            hs = slice(g * 6, g * 6 + 6)
                            MT, BMT = MT2[g % 2]
                            tq_, Btq_ = t1s[g % 2]
                            for j in range(6):
                                h = g * 6 + j
                                K.op(pe, lambda: nc.tensor.matmul(psum[:, 6, j * 64:(j + 1) * 64], lhsT=MT[:, j, :], rhs=xdt[:, h, :], start=True, stop=True), [BMT, Bxdt], [PB[6]])
                                K.op(pe, lambda: nc.tensor.matmul(psum[:, 7, j * 64:(j + 1) * 64], lhsT=xS[:, 16 + g, :], rhs=Hb[:, h, :], start=True, stop=True), [BxS, BHb], [PB[7]])
                            K.op(dve, lambda: nc.vector.tensor_tensor(out=tq_[:], in0=psum[:, 7, 0:384].rearrange("p (h q) -> p h q", q=64), in1=dts[:, 3, hs].unsqueeze(2).to_broadcast([128, 6, 64]), op=ALU.mult), [PB[7], Bdts], [Btq_])
                            K.op(dve, lambda: nc.vector.tensor_tensor(out=tq_[:], in0=tq_[:], in1=psum[:, 6, 0:384].rearrange("p (h q) -> p h q", q=64), op=ALU.add), [Btq_, PB[6]], [Btq_])
                            K.op(dve, lambda: nc.vector.tensor_tensor(out=pm2[:, hs, :], in0=pm2[:, hs, :], in1=tq_[:], op=ALU.add), [Bpm2, Btq_], [Bpm2])

                        for stg in (lambda: s1a(0), lambda: s1b(0), lambda: s1a(1), lambda: s2(0), lambda: s1b(1), lambda: s1a(2), lambda: s2(1), lambda: s1b(2), lambda: s1a(3), lambda: s2(2), lambda: s1b(3), lambda: s2(3)):
                            stg()
                            yield
                        K.dma(s_ym2[c], pm2[:].rearrange("p a b -> p (a b)"), reads=[Bpm2], writes=[B_ym2[c]])
                        if d == 1 and it + 1 < NCH:
                            cn = order[it + 1]
                            K.dma(pm2[:].rearrange("p a b -> p (a b)"), s_ym2[cn], reads=[B_ym2[cn]], writes=[Bpm2])
                        K.op(dve, lambda: nc.vector.tensor_tensor(out=Hst[:], in0=Hst[:], in1=dts[:, 6, :].unsqueeze(2).to_broadcast([128, 24, 64]), op=ALU.mult), [BHst, Bdts], [BHst])
                        for g in range(4):
                            b = 6 + (g % 2)
                            hs = slice(g * 6, g * 6 + 6)
                            for j in range(6):
                                h = g * 6 + j
                                K.op(pe, lambda: nc.tensor.matmul(psum[:, b, j * 64:(j + 1) * 64], lhsT=Btok[:, g, :], rhs=xdin[:, h, :], start=True, stop=True), [BBtok, Bxdin], [PB[b]])
                            K.op(dve, lambda: nc.vector.tensor_tensor(out=Hst[:, hs, :], in0=Hst[:, hs, :], in1=psum[:, b, 0:384].rearrange("p (h q) -> p h q", q=64), op=ALU.add), [BHst, PB[b]], [BHst])
                            yield

                    load_xe(order[0])
                    if NCH > 1:
                        load_xe(order[1])
                    load_u(order[0])
                    if d == 1:
                        K.dma(pm2[:].rearrange("p a b -> p (a b)"), s_ym2[order[0]], reads=[B_ym2[order[0]]], writes=[Bpm2])
                    for it, c in enumerate(order):
                        gens = [[gen_s5ab(it, c), 0], [gen_ssd(it, c), 0]]
                        if it > 0:
                            gens.append([gen_s5c(it - 1, order[it - 1]), 9])
                        run_gens(gens)
                    run_gens([[gen_s5c(NCH - 1, order[NCH - 1]), 0]])
                    K.barrier()
            if LVL < 3:
                break
            with ExitStack() as ph:
                pw_ = [sb(ph, [128, 6, S5W], BF16, "wglu"), sb(ph, [128, 6, D], BF16, "ws5o"), sb(ph, [128, 12, D], BF16, "wm2o"), sb(ph, [128, 8, D], BF16, "wo")]
                with ExitStack() as st_:
                    stage = [sb(st_, [128, 1024], F32, "stg") for _ in range(3)]
                    wglu, Bwglu = load_weight(ph, P["w_glu"][L], 6, S5W, "wglu", stage, [act, dve], pre=pw_[0])
                    ws5o, Bws5o = load_weight(ph, P["w_s5_out"][L], 6, D, "ws5o", stage, [act, dve], pre=pw_[1])
                    wm2o, Bwm2o = load_weight(ph, P["w_m2_out"][L], 12, D, "wm2o", stage, [act, dve], pre=pw_[2])
                    wo, Bwo = load_weight(ph, P["w_o"][L], 8, D, "wo", stage, [act, dve], pre=pw_[3])
                    K.barrier()
                nwb, Bnwb = bcast_row(ph, P["m2_norm_w"][L], M2I, "nwb")
                bgn, Bbgn = sb(ph, [6, 128], F32, "bgn")
                bg, Bbg = sb(ph, [128, 6], F32, "bg")
                K.dma(bgn[:], P["b_glu"][L].rearrange("(t p) -> t p", p=128), writes=[Bbgn])
                K.op(pe, lambda: nc.tensor.transpose(psum[:, 0, 0:6], bgn[:], consts[0:6, C_ID, 0:6]), [Bbgn, Bc], [PB[0]])
                K.op(dve, lambda: nc.vector.tensor_copy(out=bg[:], in_=psum[:, 0, 0:6]), [PB[0]], [Bbg])
                NB = 2
                ysk = [sb(ph, [16, 8, S5W], F32, "ysk") for _ in range(1)] * NB
                ym = [sb(ph, [128, M2I], F32, "ym") for _ in range(NB)]
                zt = [sb(ph, [128, M2I], BF16, "zt") for _ in range(NB)]
                gt_ = [sb(ph, [128, 2 * D], BF16, "gt") for _ in range(NB)]
                xt = [sb(ph, [128, D], F32, "x") for _ in range(3)]
                sq, Bsq = sb(ph, [128, 6, 128], F32, "sq")
                tt, Btt = sb(ph, [128, 6, 128], F32, "tt")
                hT, BhT = sb(ph, [128, 6, 128], BF16, "hT5")
                sg, Bsg = sb(ph, [128, 6, 128], BF16, "sg")
                s5a, Bs5a = sb(ph, [128, 6, 128], BF16, "s5a")
                mg, Bmg = sb(ph, [128, D], F32, "mg")
                mg2, Bmg2 = sb(ph, [128, D], F32, "mg2")
                mgb2 = [sb(ph, [128, D], BF16, "mgb") for _ in range(2)]
                mT2 = [sb(ph, [128, 8, 128], BF16, "mT") for _ in range(2)]
                sz, Bsz = sb(ph, [128, M2I], F32, "sz")
                gz, Bgz = sb(ph, [128, M2I], F32, "gz")
                junk2, Bjunk2 = sb(ph, [128, 384], BF16, "junk2")
                ss4, Bss4 = sb(ph, [128, 4], F32, "ss4")
                gnb, Bgnb = sb(ph, [128, M2I], BF16, "gnb")
                gT, BgT = sb(ph, [128, 12, 128], BF16, "gT")
                x1t = [sb(ph, [128, D], F32, "x1") for _ in range(NB)]

                def tl_load(c):
                    i = c % NB
                    K.dma(ym[i][0][:], s_ym2[c], reads=[B_ym2[c]], writes=[ym[i][1]])
                    K.dma(zt[i][0][:], s_z[c], reads=[B_z[c]], writes=[zt[i][1]])
                    K.dma(gt_[i][0][:], s_gate[c], reads=[B_gate[c]], writes=[gt_[i][1]])
                    K.dma(xt[c % 3][0][:], xsrc[c], reads=([Bxsrc[c]] if Bxsrc else []), writes=[xt[c % 3][1]])

                def ys_load(c):
                    K.dma(ysk[0][0][:], s_ys5[c].rearrange("(k t) c -> k t c", t=8), reads=[B_ys5[c]], writes=[ysk[0][1]])

                if debug and L == 0:
                    print('SBUF remaining in tail: %d bytes' % nc.sbuf_bytes_remaining)
                tl_load(0)
                ys_load(0)
                for c in range(NCH):
                    if c + 1 < NCH:
                        tl_load(c + 1)
                    i = c % NB
                    (ys_, Bys_), (ym_, Bym_), (z_, Bz_), (g_, Bg_) = ysk[i], ym[i], zt[i], gt_[i]
                    def gen_t5():
                        for ct in range(6):
                            b = ct // 3
                            for t_ in range(8):
                                col = (ct % 3) * 128 + t_ * 16
                                K.op(pe, lambda: nc.tensor.transpose(psum[:, b, col:col + 16], ys_[:, t_, ct * 128:(ct + 1) * 128], consts[0:16, C_ID, 0:16]), [Bys_, Bc], [PB[b]])
                        if c + 1 < NCH:
                            ys_load(c + 1)
                        yield
                        yT = lambda b: psum[:, b, 0:384].rearrange("p (c t k) -> p c k t", c=3, t=8)
                        for b in (0, 1):
                            cs_ = slice(b * 3, b * 3 + 3)
                            sqv = sq[:, cs_, :].rearrange("p c (k t) -> p c k t", t=8)
                            ttv = tt[:, cs_, :].rearrange("p c (k t) -> p c k t", t=8)
                            K.op(act, lambda: nc.scalar.activation(out=sqv, in_=yT(b), func=AF.Square), [PB[b]], [Bsq])
                            K.op(dve, lambda: nc.vector.tensor_scalar(out=sq[:, cs_, :], in0=sq[:, cs_, :], scalar1=0.044715, scalar2=1.0, op0=ALU.mult, op1=ALU.add), [Bsq], [Bsq])
                            K.op(dve, lambda: nc.vector.tensor_tensor(out=ttv, in0=sqv, in1=yT(b), op=ALU.mult), [Bsq, PB[b]], [Btt])
                            K.op(act, lambda: nc.scalar.activation(out=sq[:, cs_, :], in_=tt[:, cs_, :], func=AF.Sigmoid, scale=1.5957691216057308), [Btt], [Bsq])
                            K.op(dve, lambda: nc.vector.tensor_tensor(out=hT[:, cs_, :].rearrange("p c (k t) -> p c k t", t=8), in0=sqv, in1=yT(b), op=ALU.mult), [Bsq, PB[b]], [BhT])
                            yield
                        for co in range(6):
                            b = 2
                            for kt in range(6):
                                K.op(pe, lambda: nc.tensor.matmul(psum[:, b, (co % 4) * 128:(co % 4 + 1) * 128], lhsT=wglu[:, kt, co * 128:(co + 1) * 128], rhs=hT[:, kt, :], start=(kt == 0), stop=(kt == 5)), [Bwglu, BhT], [PB[b]])
                            K.op(act, lambda: nc.scalar.activation(out=sg[:, co, :], in_=psum[:, b, (co % 4) * 128:(co % 4 + 1) * 128], func=AF.Sigmoid, bias=bg[:, co:co + 1]), [PB[b], Bbg], [Bsg])
                            if co % 2 == 1:
                                yield
                        K.op(dve, lambda: nc.vector.tensor_tensor(out=s5a[:], in0=hT[:], in1=sg[:], op=ALU.mult), [BhT, Bsg], [Bs5a])
                        for hf in range(2):
                            b = hf
                            mm_tok(b, s5a, Bs5a, 6, ws5o, Bws5o, hf * 512, hf * 512 + 512)
                            K.op(dve, lambda: nc.vector.tensor_tensor(out=mg[:, hf * 512:(hf + 1) * 512], in0=psum[:, b, :], in1=g_[:, hf * 512:(hf + 1) * 512], op=ALU.mult), [PB[b], Bg_], [Bmg])
                            yield

                    def gen_tm():
                        K.op(act, lambda: nc.scalar.activation(out=sz[:], in_=z_[:], func=AF.Silu), [Bz_], [Bsz])
                        K.op(dve, lambda: nc.vector.tensor_tensor(out=gz[:], in0=ym_[:], in1=sz[:], op=ALU.mult), [Bym_, Bsz], [Bgz])
                        yield
                        for g in range(4):
                            K.op(act, lambda: nc.scalar.activation(out=junk2[:], in_=gz[:, g * 384:(g + 1) * 384], func=AF.Square, accum_out=ss4[:, g:g + 1]), [Bgz], [Bjunk2, Bss4])
                        K.op(dve, lambda: nc.vector.tensor_scalar(out=ss4[:], in0=ss4[:], scalar1=1.0 / 384, scalar2=EPS, op0=ALU.mult, op1=ALU.add), [Bss4], [Bss4])
                        K.op(act, lambda: nc.scalar.activation(out=ss4[:], in_=ss4[:], func=AF.Sqrt), [Bss4], [Bss4])
                        K.op(dve, lambda: nc.vector.reciprocal(out=ss4[:], in_=ss4[:]), [Bss4], [Bss4])
                        yield
                        K.op(dve, lambda: nc.vector.tensor_tensor(out=gz[:].rearrange("p (g q) -> p g q", g=4), in0=gz[:].rearrange("p (g q) -> p g q", g=4), in1=ss4[:].unsqueeze(2).to_broadcast([128, 4, 384]), op=ALU.mult), [Bgz, Bss4], [Bgz])
                        K.op(dve, lambda: nc.vector.tensor_tensor(out=gnb[:], in0=gz[:], in1=nwb[:], op=ALU.mult), [Bgz, Bnwb], [Bgnb])
                        yield
                        transpose_to(gT, BgT, gnb, Bgnb, 12, 4, evac=act)
                        yield
                        for hf in range(2):
                            b = 4 + hf
                            mm_tok(b, gT, BgT, 12, wm2o, Bwm2o, hf * 512, hf * 512 + 512)
                            K.op(dve, lambda: nc.vector.tensor_tensor(out=mg2[:, hf * 512:(hf + 1) * 512], in0=psum[:, b, :], in1=g_[:, D + hf * 512:D + (hf + 1) * 512], op=ALU.mult), [PB[b], Bg_], [Bmg2])
                            yield

                    def gen_wo(cp):
                        mgb, Bmgb = mgb2[cp % 2]
                        mT, BmT = mT2[cp % 2]
                        xp_, Bxp_ = xt[cp % 3]
                        x1_, Bx1_ = x1t[cp % 2]
                        transpose_to(mT, BmT, mgb, Bmgb, 8, 3)
                        yield
                        for hf in range(2):
                            b = 6 + hf
                            mm_tok(b, mT, BmT, 8, wo, Bwo, hf * 512, hf * 512 + 512)
                            K.op(dve, lambda: nc.vector.tensor_tensor(out=x1_[:, hf * 512:(hf + 1) * 512], in0=psum[:, b, :], in1=xp_[:, hf * 512:(hf + 1) * 512], op=ALU.add), [PB[b], Bxp_], [Bx1_])
                            yield
                        K.dma(s_x1[cp], x1_[:], reads=[Bx1_], writes=[B_x1[cp]])

                    gl = [[gen_tm(), 0], [gen_t5(), 0]]
                    if c > 0:
                        gl.append([gen_wo(c - 1), 1])
                    run_gens(gl)
                    if debug:
                        K.dma(s_dbg1[c], mg[:], reads=[Bmg], writes=[])
                        K.dma(s_dbg2[c], mg2[:], reads=[Bmg2], writes=[])
                    K.op(dve, lambda: nc.vector.tensor_tensor(out=mgb2[c % 2][0][:], in0=mg[:], in1=mg2[:], op=ALU.add), [Bmg, Bmg2], [mgb2[c % 2][1]])
                run_gens([[gen_wo(NCH - 1), 0]])
                K.barrier()
            if LVL < 4:
                break
            with ExitStack() as ph:
                pw_ = [sb(ph, [128, 8, DFF], BF16, "wup"), sb(ph, [128, 32, D], BF16, "wdn")]
                with ExitStack() as st_:
                    stage = [sb(st_, [128, 1024], F32, "stg") for _ in range(3)]
                    wup, Bwup = load_weight(ph, P["w_up"][L], 8, DFF, "wup", stage, [act, dve], pre=pw_[0])
                    wdn, Bwdn = load_weight(ph, P["w_down"][L], 32, D, "wdn", stage, [act, dve], pre=pw_[1])
                    K.barrier()
                w2b, Bw2b = bcast_row(ph, P["norm2_w"][L], D, "w2b")
                last = (L == DEPTH - 1)
                if last:
                    wfb, Bwfb = bcast_row(ph, P["final_norm_w"], D, "wfb")
                NB = 2
                xt = [sb(ph, [128, D], F32, "x") for _ in range(NB)]
                junk = sb(ph, [128, D], BF16, "junk")
                ss = [sb(ph, [128, 1], F32, "ss") for _ in range(NB)]
                hb = [sb(ph, [128, D], BF16, "h") for _ in range(NB)]
                hT = [sb(ph, [128, 8, 128], BF16, "hT") for _ in range(NB)]
                aT, BaT = sb(ph, [128, 32, 128], BF16, "aT")
                rl = [sb(ph, [128, 4, 128], F32, "rl") for _ in range(2)]
                x2t = [sb(ph, [128, D], F32, "x2") for _ in range(NB)]
                yo = [sb(ph, [128, D], F32, "yo") for _ in range(NB)]

                def p3_load(c):
                    t, B = xt[c % NB]
                    K.dma(t[:], s_x1[c], reads=[B_x1[c]], writes=[B])

                ssf = [sb(ph, [128, 1], F32, "ssf") for _ in range(NB)]

                def p3_head(c):
                    i = c % NB
                    (x_, Bx), (ss_, Bss), (h_, Bh), (hT_, BhT) = xt[i], ss[i], hb[i], hT[i]
                    rms_rstd(x_[:], Bx, D, ss_, Bss, junk[0][:], junk[1])
                    K.op(dve, lambda: nc.vector.scalar_tensor_tensor(out=h_[:], in0=x_[:], scalar=ss_[:, 0:1], in1=w2b[:], op0=ALU.mult, op1=ALU.mult), [Bx, Bss, Bw2b], [Bh])
                    transpose_to(hT_, BhT, h_, Bh, 8, 0)

                p3_load(0)
                p3_head(0)
                for c in range(NCH):
                    if c + 1 < NCH:
                        p3_load(c + 1)
                    i = c % NB
                    (x_, Bx), (ss_, Bss), (h_, Bh), (hT_, BhT) = xt[i], ss[i], hb[i], hT[i]
                    for q in range(8):
                        b = 1 + (q % 4)
                        for j in range(4):
                            ft = q * 4 + j
                            for kt in range(8):
                                K.op(pe, lambda: nc.tensor.matmul(psum[:, b, j * 128:(j + 1) * 128], lhsT=wup[:, kt, ft * 128:(ft + 1) * 128], rhs=hT_[:, kt, :], start=(kt == 0), stop=(kt == 7)), [Bwup, BhT], [PB[b]])
                        rl_, Brl_ = rl[q % 2]
                        K.op(act, lambda: nc.scalar.activation(out=rl_[:], in_=psum[:, b, :].rearrange("p (a b) -> p a b", b=128), func=AF.Relu), [PB[b]], [Brl_])
                        if q % 2 == 0:
                            K.op(dve, lambda: nc.vector.tensor_tensor(out=aT[:, q * 4:q * 4 + 4, :], in0=rl_[:], in1=rl_[:], op=ALU.mult), [Brl_], [BaT])
                        else:
                            K.op(dve, lambda: nc.vector.tensor_tensor(out=aT[:, q * 4:q * 4 + 4, :], in0=rl_[:], in1=rl_[:], op=ALU.mult), [Brl_], [BaT])
                    if c + 1 < NCH:
                        p3_head(c + 1)
                    x2_, Bx2_ = x2t[i]
                    for hf in range(2):
                        b = 5 + hf
                        mm_tok(b, aT, BaT, 32, wdn, Bwdn, hf * 512, hf * 512 + 512)
                        K.op(dve, lambda: nc.vector.tensor_tensor(out=x2_[:, hf * 512:(hf + 1) * 512], in0=psum[:, b, :], in1=x_[:, hf * 512:(hf + 1) * 512], op=ALU.add), [PB[b], Bx], [Bx2_])
                    if not last:
                        K.dma(s_x2[c], x2_[:], reads=[Bx2_], writes=[B_x2[c]])
                    else:
                        y_, By_ = yo[i]
                        sf_, Bsf_ = ssf[i]
                        rms_rstd(x2_[:], Bx2_, D, sf_, Bsf_, junk[0][:], junk[1])
                        K.op(dve, lambda: nc.vector.scalar_tensor_tensor(out=y_[:], in0=x2_[:], scalar=sf_[:, 0:1], in1=wfb[:], op0=ALU.mult, op1=ALU.mult), [Bx2_, Bsf_, Bwfb], [By_])
                        K.dma(y_out[c], y_[:], reads=[By_], writes=[])
                K.barrier()
            if LVL < 5:
                break

        K.barrier()
    return nc


PNAMES = ["norm1_w", "w_in", "lam_re", "lam_im", "log_dt", "b_re", "b_im", "c_re", "c_im", "d_s5", "w_glu", "b_glu",
          "w_s5_out", "conv_w", "conv_b", "dt_bias", "a_log", "d_m2", "m2_norm_w", "w_m2_out", "w_o", "norm2_w",
          "w_up", "w_down", "final_norm_w"]


def kernel(**inputs):
    xp = np.asarray(inputs["x_prompt"], np.float32)
    xs = np.asarray(inputs["x_sample"], np.float32)
    NCH = 64
    params = {k: np.ascontiguousarray(np.asarray(inputs[k], np.float32)) for k in PNAMES}
    consts = host_consts()
    in_maps = []
    for core in range(8):
        m = np.ones((128, NCH + 1), np.float32)
        m[:, 0] = 0.0
        m[:, NCH] = 0.0
        if core < 4:
            x = xs[core].reshape(NCH, 128, D)
        else:
            j = core - 4
            x = np.zeros((NCH, 128, D), np.float32)
            x[0:16] = xp[2 * j].reshape(16, 128, D)
            x[16:32] = xp[2 * j + 1].reshape(16, 128, D)
            m[:, 16] = 0.0
            m[:, 32] = 0.0
            m[:, 48] = 0.0
        im = dict(params)
        im.update(x=np.ascontiguousarray(x), seqmask=m, consts=consts)
        in_maps.append(im)
    nc = build_program(4, 16)
    res = run_bass_kernel_spmd(nc, in_maps, core_ids=list(range(8)))
    yp = np.zeros((8, 2048, D), np.float32)
    ysm = np.zeros((4, 8192, D), np.float32)
    for core in range(8):
        y = np.asarray(res.results[core]["y"], np.float32).reshape(NCH * 128, D)
        if core < 4:
            ysm[core] = y
        else:
            j = core - 4
            yp[2 * j] = y[0:2048]
            yp[2 * j + 1] = y[2048:4096]
    return (yp, ysm)
```

```python
import numpy as np
from contextlib import ExitStack
import concourse.bass as bass
import concourse.mybir as mybir
from concourse.bass_utils import run_bass_kernel_spmd

F32 = mybir.dt.float32
BF16 = mybir.dt.bfloat16
AF = mybir.ActivationFunctionType
ALU = mybir.AluOpType
AX = mybir.AxisListType

D = 1024
DEPTH = 2
EPS = 1e-6
S5W, S5G, S5P = 768, 48, 64
M2I, M2H, M2P, M2G, M2N = 1536, 24, 64, 4, 128
XBC = 2560
OFF_U, OFF_Z, OFF_XBC, OFF_DT, OFF_GATE, DPROJ = 0, 768, 2304, 4864, 4912, 6960
DFF = 4096
PI = float(np.pi)
SUB = ""

C_ID, C_UT, C_LT, C_SU, C_SL, C_ONES, C_MF, C_MB, C_E = range(9)
NCONST = 9


def host_consts():
    r = np.arange(128)[:, None]
    t = np.arange(128)[None, :]
    cs = np.zeros((NCONST, 128, 128), np.float32)
    cs[C_ID] = (r == t)
    cs[C_UT] = (r <= t)
    cs[C_LT] = (r >= t)
    cs[C_SU] = (r < t)
    cs[C_SL] = (r > t)
    cs[C_ONES] = 1.0
    cs[C_MF] = ((t // 16) >= (r // 16))
    cs[C_MB] = ((t // 16) <= (r // 16))
    cs[C_E] = ((t % 16) == r)
    return np.ascontiguousarray(cs.transpose(1, 0, 2).reshape(128, NCONST * 128))


class Buf:
    __slots__ = ("w", "r", "excl")

    def __init__(self, excl=False):
        self.w = None
        self.r = {}
        self.excl = excl


class Eng:
    def __init__(self, K, name, h, self_sync):
        self.K, self.name, self.h, self.self_sync = K, name, h, self_sync
        self.sem = K.es.enter_context(K.nc.semaphore("e_" + name))
        self.sid = "e_" + name
        self.count = 0
        self.known = {}
        self.last = None
        self.dsems = []
        self.dcnt = []
        self.dpos = 0

    def wait(self, tok):
        if tok is None:
            return
        sid, sem, val = tok
        if self.known.get(sid, 0) >= val:
            return
        self.h.wait_ge(sem, val)
        self.known[sid] = val

    def issue(self, ins):
        self.count += 1
        ins.then_inc(self.sem, 1)
        tok = (self.sid, self.sem, self.count)
        if not self.self_sync:
            self.known[self.sid] = self.count
        self.last = tok
        return tok


class Kern:
    def __init__(self, nc, es, ndma=12):
        self.nc, self.es = nc, es
        self.pe = Eng(self, "pe", nc.tensor, False)
        self.act = Eng(self, "act", nc.scalar, True)
        self.dve = Eng(self, "dve", nc.vector, True)
        self.pool = Eng(self, "pool", nc.gpsimd, True)
        self.sp = Eng(self, "sp", nc.sync, False)
        self.engs = [self.pe, self.act, self.dve, self.pool, self.sp]
        for i in range(ndma):
            self.sp.dsems.append(es.enter_context(nc.semaphore("d_%d" % i)))
            self.sp.dcnt.append(0)
        self.dma_toks = [None] * ndma

    def _deps(self, eng, reads, writes):
        for b in reads:
            eng.wait(b.w)
        for b in writes:
            eng.wait(b.w)
            for t in b.r.values():
                eng.wait(t)

    def _mark(self, tok, reads, writes):
        for b in writes:
            b.w = tok
            b.r = {}
        for b in reads:
            b.r[tok[0]] = tok

    def op(self, eng, emit, reads=(), writes=()):
        ex = [b for b in reads if b.excl]
        if ex:
            reads = [b for b in reads if not b.excl]
            writes = list(writes) + ex
        self._deps(eng, reads, writes)
        tok = eng.issue(emit())
        self._mark(tok, reads, writes)
        return tok

    def dma(self, out, in_, reads=(), writes=(), **kw):
        eng = self.sp
        self._deps(eng, reads, writes)
        j = eng.dpos
        eng.dpos = (eng.dpos + 1) % len(eng.dsems)
        eng.wait(self.dma_toks[j])
        eng.dcnt[j] += 16
        eng.h.dma_start(out=out, in_=in_, **kw).then_inc(eng.dsems[j], 16)
        tok = ("d_%d" % j, eng.dsems[j], eng.dcnt[j])
        self.dma_toks[j] = tok
        self._mark(tok, reads, writes)
        return tok

    def barrier(self):
        toks = [e.last for e in self.engs if e.last is not None] + [t for t in self.dma_toks if t is not None]
        for e in self.engs:
            for t in toks:
                if t[0] == e.sid and not e.self_sync:
                    continue
                e.wait(t)


def build_program(NU, CPU, debug=False, LVL=9):
    NCH = NU * CPU
    nc = bass.Bass("TRN2", target_bir_lowering=False)
    dt_in = lambda n, s, d=F32: nc.dram_tensor(n, list(s), d, kind="ExternalInput").ap()
    kind_dbg = "ExternalOutput" if debug else "Internal"
    dt_sc = lambda n, s, d=F32: nc.dram_tensor(n, list(s), d, kind=kind_dbg).ap()

    x_in = dt_in("x", [NCH, 128, D])
    seqmask_in = dt_in("seqmask", [128, NCH + 1])
    consts_in = dt_in("consts", [128, NCONST * 128])
    P = {}
    for n, s in [("norm1_w", (DEPTH, D)), ("w_in", (DEPTH, D, DPROJ)), ("lam_re", (DEPTH, 2, S5G, S5P)),
                 ("lam_im", (DEPTH, 2, S5G, S5P)), ("log_dt", (DEPTH, 2, S5G)),
                 ("b_re", (DEPTH, 2, S5G, S5P, 16)), ("b_im", (DEPTH, 2, S5G, S5P, 16)),
                 ("c_re", (DEPTH, 2, S5G, 16, S5P)), ("c_im", (DEPTH, 2, S5G, 16, S5P)),
                 ("d_s5", (DEPTH, S5W)), ("w_glu", (DEPTH, S5W, S5W)), ("b_glu", (DEPTH, S5W)),
                 ("w_s5_out", (DEPTH, S5W, D)), ("conv_w", (DEPTH, 4, XBC)), ("conv_b", (DEPTH, XBC)),
                 ("dt_bias", (DEPTH, 2, M2H)), ("a_log", (DEPTH, 2, M2H)), ("d_m2", (DEPTH, M2H)),
                 ("m2_norm_w", (DEPTH, M2I)), ("w_m2_out", (DEPTH, M2I, D)), ("w_o", (DEPTH, D, D)),
                 ("norm2_w", (DEPTH, D)), ("w_up", (DEPTH, D, DFF)), ("w_down", (DEPTH, DFF, D)),
                 ("final_norm_w", (D,))]:
        P[n] = dt_in(n, s)
    y_out = nc.dram_tensor("y", [NCH, 128, D], F32, kind="ExternalOutput").ap()

    s_u = dt_sc("s_u", [NCH, 128, S5W], BF16)
    s_z = dt_sc("s_z", [NCH, 128, M2I], BF16)
    s_dt = dt_sc("s_dt", [NCH, 128, 48], F32)
    s_gate = dt_sc("s_gate", [NCH, 128, 2 * D], BF16)
    s_xbc = dt_sc("s_xbc", [NCH, 128, 20, 128], BF16)
    s_ys5 = dt_sc("s_ys5", [NCH, 128, S5W], F32)
    s_ym2 = dt_sc("s_ym2", [NCH, 128, M2I], F32)
    s_x1 = dt_sc("s_x1", [NCH, 128, D], F32)
    s_x2 = dt_sc("s_x2", [NCH, 128, D], F32)
    s_dbg1 = dt_sc("s_dbg1", [NCH, 128, D], F32)
    s_dbg2 = dt_sc("s_dbg2", [NCH, 128, D], F32)
    B_u = [Buf() for _ in range(NCH)]
    B_z = [Buf() for _ in range(NCH)]
    B_dt = [Buf() for _ in range(NCH)]
    B_gate = [Buf() for _ in range(NCH)]
    B_xbc = [Buf() for _ in range(NCH)]
    B_ys5 = [Buf() for _ in range(NCH)]
    B_ym2 = [Buf() for _ in range(NCH)]
    B_x1 = [Buf() for _ in range(NCH)]
    B_x2 = [Buf() for _ in range(NCH)]

    with ExitStack() as es:
        K = Kern(nc, es)
        pe, act, dve, pool = K.pe, K.act, K.dve, K.pool
        uid = [0]

        def sb(es_, shape, dtype=F32, name=None):
            uid[0] += 1
            t = es_.enter_context(nc.sbuf_tensor("%s_%d" % (name or "t", uid[0]), list(shape), dtype))
            return t, Buf()

        psum = es.enter_context(nc.psum_tensor("psum", [128, 8, 512], F32))
        PB = [Buf(excl=True) for _ in range(8)]
        consts, Bc = sb(es, [128, NCONST, 128], F32, "consts")
        ident_bf, Bidb = sb(es, [128, 128], BF16, "identbf")
        seqm, Bsm = sb(es, [128, NCH + 1], F32, "seqm")
        K.dma(consts[:].rearrange("p a b -> p (a b)"), consts_in[:, :], writes=[Bc])
        K.dma(seqm[:], seqmask_in[:, :], writes=[Bsm])
        K.op(dve, lambda: nc.vector.tensor_copy(out=ident_bf[:], in_=consts[:, C_ID, :]), [Bc], [Bidb])
        CN = lambda i: consts[:, i, :]

        def bank_bf(b):
            return psum[:, b, :].bitcast(BF16)

        def load_weight(es_, src2d, KT, C, name, stage, scale_engs, pre=None):
            wt, Bw = pre if pre is not None else sb(es_, [128, KT, C], BF16, name)
            CB = 1024
            i = 0
            for kt in range(KT):
                for c0 in range(0, C, CB):
                    c1 = min(C, c0 + CB)
                    st, Bs = stage[i % len(stage)]
                    K.dma(st[:, 0:c1 - c0], src2d[kt * 128:(kt + 1) * 128, c0:c1], writes=[Bs])
                    e = scale_engs[i % len(scale_engs)]
                    if e is act:
                        K.op(act, lambda: nc.scalar.activation(out=wt[:, kt, c0:c1], in_=st[:, 0:c1 - c0], func=AF.Copy), [Bs], [Bw])
                    elif e is dve:
                        K.op(dve, lambda: nc.vector.tensor_copy(out=wt[:, kt, c0:c1], in_=st[:, 0:c1 - c0]), [Bs], [Bw])
                    else:
                        K.op(pool, lambda: nc.gpsimd.tensor_copy(out=wt[:, kt, c0:c1], in_=st[:, 0:c1 - c0]), [Bs], [Bw])
                    i += 1
            return wt, Bw

        def bcast_row(es_, src1d, n, name):
            t, B = sb(es_, [128, n], F32, name)
            K.dma(t[:], src1d.partition_broadcast(128), writes=[B])
            return t, B

        def rms_rstd(xt, Bx, n, ss, Bss, junk, Bj):
            K.op(act, lambda: nc.scalar.activation(out=junk, in_=xt, func=AF.Square, accum_out=ss[:, 0:1]), [Bx], [Bj, Bss])
            K.op(dve, lambda: nc.vector.tensor_scalar(out=ss[:, 0:1], in0=ss[:, 0:1], scalar1=1.0 / n, scalar2=EPS, op0=ALU.mult, op1=ALU.add), [Bss], [Bss])
            K.op(act, lambda: nc.scalar.activation(out=ss[:, 0:1], in_=ss[:, 0:1], func=AF.Sqrt), [Bss], [Bss])
            K.op(dve, lambda: nc.vector.reciprocal(out=ss[:, 0:1], in_=ss[:, 0:1]), [Bss], [Bss])

        def transpose_to(dst3, Bd, src, Bs_, ntile, bank, dtype_bf=True, evac=None):
            done = 0
            while done < ntile:
                n = min(8, ntile - done)
                b = bank + (done // 8)
                pb = bank_bf(b)
                for j in range(n):
                    jj = done + j
                    K.op(pe, lambda: nc.tensor.transpose(pb[:, j * 128:(j + 1) * 128], src[:, jj * 128:(jj + 1) * 128], ident_bf[:]), [Bs_, Bidb], [PB[b]])
                e = evac or act
                if e is act:
                    K.op(act, lambda: nc.scalar.activation(out=dst3[:, done:done + n, :], in_=pb[:, 0:n * 128].rearrange("p (a b) -> p a b", b=128), func=AF.Copy), [PB[b]], [Bd])
                else:
                    K.op(dve, lambda: nc.vector.tensor_copy(out=dst3[:, done:done + n, :], in_=pb[:, 0:n * 128].rearrange("p (a b) -> p a b", b=128)), [PB[b]], [Bd])
                done += n

        def mm_tok(out_bank, lhsT3, Bl, KT, w3, Bw, c0, c1):
            for kt in range(KT):
                K.op(pe, lambda: nc.tensor.matmul(psum[:, out_bank, 0:c1 - c0], lhsT=lhsT3[:, kt, :], rhs=w3[:, kt, c0:c1], start=(kt == 0), stop=(kt == KT - 1)), [Bl, Bw], [PB[out_bank]])

        def run_gens(gens):
            while gens:
                for e_ in list(gens):
                    if e_[1] > 0:
                        e_[1] -= 1
                        continue
                    try:
                        next(e_[0])
                    except StopIteration:
                        gens.remove(e_)

        for L in range(DEPTH):
            xsrc, Bxsrc = (x_in, None) if L == 0 else (s_x2, B_x2)
            with ExitStack() as ph:
                stage = [sb(ph, [128, 1024], F32, "stg") for _ in range(3)]
                Win, BWin = load_weight(ph, P["w_in"][L], 8, DPROJ, "win", stage, [act, dve])
                w1b, Bw1b = bcast_row(ph, P["norm1_w"][L], D, "w1b")
                dsb, Bdsb = bcast_row(ph, P["d_s5"][L], S5W, "dsb")
                NB = 2
                oy = [sb(ph, [128, S5W], F32, "oy") for _ in range(NB)]
                xt = [sb(ph, [128, D], F32, "x") for _ in range(NB)]
                junk = sb(ph, [128, D], BF16, "junk")
                ss = [sb(ph, [128, 1], F32, "ss") for _ in range(NB)]
                hb = [sb(ph, [128, D], BF16, "h") for _ in range(NB)]
                hT = [sb(ph, [128, 8, 128], BF16, "hT") for _ in range(NB)]
                ou = [sb(ph, [128, S5W], BF16, "ou") for _ in range(NB)]
                oz = [sb(ph, [128, M2I], BF16, "oz") for _ in range(NB)]
                odt = [sb(ph, [128, 48], F32, "odt") for _ in range(NB)]
                og = [sb(ph, [128, 2 * D], BF16, "og") for _ in range(NB)]
                ox = [sb(ph, [128, 20, 128], BF16, "ox") for _ in range(NB)]

                def p0_load(c):
                    t, B = xt[c % NB]
                    K.dma(t[:], xsrc[c], reads=([Bxsrc[c]] if Bxsrc else []), writes=[B])

                p0_load(0)
                bk = [0]

                def nb():
                    bk[0] = (bk[0] + 1) % 8
                    return bk[0]

                def p0_head(c):
                    i = c % NB
                    (x_, Bx), (ss_, Bss), (h_, Bh), (hT_, BhT) = xt[i], ss[i], hb[i], hT[i]
                    rms_rstd(x_[:], Bx, D, ss_, Bss, junk[0][:], junk[1])
                    K.op(dve, lambda: nc.vector.scalar_tensor_tensor(out=h_[:], in0=x_[:], scalar=ss_[:, 0:1], in1=w1b[:], op0=ALU.mult, op1=ALU.mult), [Bx, Bss, Bw1b], [Bh])
                    transpose_to(hT_, BhT, h_, Bh, 8, nb())

                p0_head(0)
                for c in range(NCH):
                    if c + 1 < NCH:
                        p0_load(c + 1)
                    i = c % NB
                    (x_, Bx), (ss_, Bss), (h_, Bh), (hT_, BhT) = xt[i], ss[i], hb[i], hT[i]
                    (u_, Bu_), (z_, Bz_), (d_, Bd_), (g_, Bg_), (xo_, Bxo_) = ou[i], oz[i], odt[i], og[i], ox[i]
                    blocks = [(OFF_U, 512, u_, 0, Bu_, "c"), (OFF_U + 512, 256, u_, 512, Bu_, "c"),
                              (OFF_Z, 512, z_, 0, Bz_, "c"), (OFF_Z + 512, 512, z_, 512, Bz_, "c"), (OFF_Z + 1024, 512, z_, 1024, Bz_, "c"),
                              (OFF_DT, 48, d_, 0, Bd_, "c")] + [(OFF_GATE + 512 * q, 512, g_, 512 * q, Bg_, "s") for q in range(4)]
                    for bi, (c0, n, dst, d0, Bdst, kind) in enumerate(blocks):
                        b = nb()
                        mm_tok(b, hT_, BhT, 8, Win, BWin, c0, c0 + n)
                        if False:
                            K.op(dve, lambda: nc.vector.tensor_tensor(out=oy[i][0][:, d0:d0 + n], in0=psum[:, b, 0:n], in1=dsb[:, d0:d0 + n], op=ALU.mult), [PB[b], Bdsb], [oy[i][1]])
                        if kind == "s":
                            K.op(act, lambda: nc.scalar.activation(out=dst[:, d0:d0 + n], in_=psum[:, b, 0:n], func=AF.Sigmoid), [PB[b]], [Bdst])
                        elif bi % 2 == 0:
                            K.op(dve, lambda: nc.vector.tensor_copy(out=dst[:, d0:d0 + n], in_=psum[:, b, 0:n]), [PB[b]], [Bdst])
                        else:
                            K.op(act, lambda: nc.scalar.activation(out=dst[:, d0:d0 + n], in_=psum[:, b, 0:n], func=AF.Copy), [PB[b]], [Bdst])
                    if c + 1 < NCH:
                        p0_head(c + 1)
                    for q in range(5):
                        b = nb()
                        for j in range(4):
                            col = OFF_XBC + (q * 4 + j) * 128
                            for kt in range(8):
                                K.op(pe, lambda: nc.tensor.matmul(psum[:, b, j * 128:(j + 1) * 128], lhsT=Win[:, kt, col:col + 128], rhs=hT_[:, kt, :], start=(kt == 0), stop=(kt == 7)), [BWin, BhT], [PB[b]])
                        if q % 2 == 0:
                            K.op(dve, lambda: nc.vector.tensor_copy(out=xo_[:, q * 4:q * 4 + 4, :], in_=psum[:, b, :].rearrange("p (a b) -> p a b", b=128)), [PB[b]], [Bxo_])
                        else:
                            K.op(act, lambda: nc.scalar.activation(out=xo_[:, q * 4:q * 4 + 4, :], in_=psum[:, b, :].rearrange("p (a b) -> p a b", b=128), func=AF.Copy), [PB[b]], [Bxo_])
                    K.dma(s_u[c], u_[:], reads=[Bu_], writes=[B_u[c]])
                    if False:
                        K.dma(s_ys5[c], oy[i][0][:], reads=[oy[i][1]], writes=[B_ys5[c]])
                    K.dma(s_z[c], z_[:], reads=[Bz_], writes=[B_z[c]])
                    K.dma(s_dt[c], d_[:], reads=[Bd_], writes=[B_dt[c]])
                    K.dma(s_gate[c], g_[:], reads=[Bg_], writes=[B_gate[c]])
                    K.dma(s_xbc[c], xo_[:], reads=[Bxo_], writes=[B_xbc[c]])
                K.barrier()

            for d in range(2):
                if LVL < 1 + d:
                    continue
                with ExitStack() as ph:
                    WY, BWY = sb(ph, [128, 48, 128], BF16, "WY")
                    WVr, BWVr = sb(ph, [128, 48, 64], BF16, "WVr")
                    WVi, BWVi = sb(ph, [128, 48, 64], BF16, "WVi")
                    WCr, BWCr = sb(ph, [128, 48, 128], BF16, "WCr")
                    WCi, BWCi = sb(ph, [128, 48, 128], BF16, "WCi")
                    K.op(dve, lambda: nc.vector.memset(WCr[:], 0.0), [], [BWCr])
                    K.op(dve, lambda: nc.vector.memset(WCi[:], 0.0), [], [BWCi])
                    A8, BA8 = sb(ph, [64, 2, 2, 48], F32, "A8")
                    with ExitStack() as tg:
                        Q = 64
                        def small(shape, name, dtype=F32):
                            return sb(tg, shape, dtype, name)
                        ln_, Bln = small([48, 2, 64], "lamn")
                        K.dma(ln_[:, 0, :], P["lam_re"][L, d], writes=[Bln])
                        K.dma(ln_[:, 1, :], P["lam_im"][L, d], writes=[Bln])
                        lam, Blam = small([Q, 2, 48], "lam")
                        for r_ in range(2):
                            K.op(pe, lambda: nc.tensor.transpose(psum[0:Q, 0, r_ * 48:(r_ + 1) * 48], ln_[:, r_, :], consts[0:48, C_ID, 0:48]), [Bln, Bc], [PB[0]])
                        K.op(dve, lambda: nc.vector.tensor_copy(out=lam[:].rearrange("p a b -> p (a b)"), in_=psum[0:Q, 0, 0:96]), [PB[0]], [Blam])
                        dl, Bdl = small([Q, 48], "dl")
                        K.dma(dl[:], P["log_dt"][L, d].partition_broadcast(Q), writes=[Bdl])
                        K.op(act, lambda: nc.scalar.activation(out=dl[:], in_=dl[:], func=AF.Exp), [Bdl], [Bdl])
                        lrd, Blrd = small([Q, 48], "lrd")
                        lid, Blid = small([Q, 48], "lid")
                        K.op(dve, lambda: nc.vector.tensor_tensor(out=lrd[:], in0=lam[:, 0, :], in1=dl[:], op=ALU.mult), [Blam, Bdl], [Blrd])
                        K.op(dve, lambda: nc.vector.tensor_tensor(out=lid[:], in0=lam[:, 1, :], in1=dl[:], op=ALU.mult), [Blam, Bdl], [Blid])
                        NE = 9
                        ea, Bea = small([Q, NE, 48], "ea")
                        eb, Beb = small([Q, NE, 48], "eb")
                        ec, Bec = small([Q, NE, 48], "ec")
                        ei, Bei = small([Q, NE, 48], "ei", mybir.dt.int32)
                        ef, Bef = small([Q, NE, 48], "ef")

                        def sin_of(dst, Bdst, n, shift):
                            K.op(dve, lambda: nc.vector.tensor_scalar(out=ec[:, 0:n, :], in0=eb[:, 0:n, :], scalar1=float(shift), scalar2=None, op0=ALU.add), [Beb], [Bec])
                            K.op(dve, lambda: nc.vector.tensor_copy(out=ei[:, 0:n, :], in_=ec[:, 0:n, :]), [Bec], [Bei])
                            K.op(dve, lambda: nc.vector.tensor_copy(out=ef[:, 0:n, :], in_=ei[:, 0:n, :]), [Bei], [Bef])
                            K.op(dve, lambda: nc.vector.tensor_tensor(out=ec[:, 0:n, :], in0=ec[:, 0:n, :], in1=ef[:, 0:n, :], op=ALU.subtract), [Bec, Bef], [Bec])
                            K.op(dve, lambda: nc.vector.tensor_scalar(out=ef[:, 0:n, :], in0=ec[:, 0:n, :], scalar1=0.5, scalar2=None, op0=ALU.is_gt), [Bec], [Bef])
                            K.op(dve, lambda: nc.vector.tensor_tensor(out=ec[:, 0:n, :], in0=ec[:, 0:n, :], in1=ef[:, 0:n, :], op=ALU.subtract), [Bec, Bef], [Bec])
                            K.op(dve, lambda: nc.vector.tensor_scalar(out=ef[:, 0:n, :], in0=ec[:, 0:n, :], scalar1=-0.5, scalar2=None, op0=ALU.is_lt), [Bec], [Bef])
                            K.op(dve, lambda: nc.vector.tensor_tensor(out=ec[:, 0:n, :], in0=ec[:, 0:n, :], in1=ef[:, 0:n, :], op=ALU.add), [Bec, Bef], [Bec])
                            K.op(act, lambda: nc.scalar.activation(out=dst[:, 0:n, :], in_=ec[:, 0:n, :], func=AF.Sin, scale=2.0 * PI), [Bec], [Bdst])

                        def powers(exps, sign, name):
                            n = len(exps)
                            pr, Bpr = small([Q, n, 48], name + "r")
                            pi_, Bpi = small([Q, n, 48], name + "i")
                            for j, e in enumerate(exps):
                                K.op(dve, lambda: nc.vector.tensor_scalar(out=ea[:, j, :], in0=lrd[:], scalar1=float(sign * e), scalar2=None, op0=ALU.mult), [Blrd], [Bea])
                                K.op(dve, lambda: nc.vector.tensor_scalar(out=eb[:, j, :], in0=lid[:], scalar1=float(e / (2.0 * PI)), scalar2=None, op0=ALU.mult), [Blid], [Beb])
                            K.op(act, lambda: nc.scalar.activation(out=ea[:, 0:n, :], in_=ea[:, 0:n, :], func=AF.Exp), [Bea], [Bea])
                            sin_of(pi_, Bpi, n, 0.0)
                            sin_of(pr, Bpr, n, 0.25)
                            K.op(dve, lambda: nc.vector.tensor_tensor(out=pr[:], in0=pr[:], in1=ea[:, 0:n, :], op=ALU.mult), [Bea, Bpr], [Bpr])
                            K.op(dve, lambda: nc.vector.scalar_tensor_tensor(out=pi_[:], in0=pi_[:], scalar=float(sign), in1=ea[:, 0:n, :], op0=ALU.mult, op1=ALU.mult), [Bea, Bpi], [Bpi])
                            return (pr, Bpr), (pi_, Bpi)

                        T8 = list(range(8))
                        if d == 0:
                            eX, eZ, eV, eC = T8, T8, [7 - t for t in T8], [t + 1 for t in T8]
                        else:
                            eX, eZ, eV, eC = [7 - t for t in T8], [7 - t for t in T8], T8, [8 - t for t in T8]
                        (a1r, Ba1r), (a1i, Ba1i) = powers([1, 8], 1, "a1")
                        K.op(dve, lambda: nc.vector.tensor_copy(out=A8[:, 0, 0, :], in_=a1r[:, 1, :]), [Ba1r], [BA8])
                        K.op(dve, lambda: nc.vector.tensor_copy(out=A8[:, 0, 1, :], in_=a1r[:, 1, :]), [Ba1r], [BA8])
                        K.op(dve, lambda: nc.vector.tensor_scalar(out=A8[:, 1, 0, :], in0=a1i[:, 1, :], scalar1=-1.0, scalar2=None, op0=ALU.mult), [Ba1i], [BA8])
                        K.op(dve, lambda: nc.vector.tensor_copy(out=A8[:, 1, 1, :], in_=a1i[:, 1, :]), [Ba1i], [BA8])
                        qt, Bqt = small([Q, 6, 48], "qt")
                        lr_, li_ = lam[:, 0, :], lam[:, 1, :]
                        K.op(dve, lambda: nc.vector.tensor_tensor(out=qt[:, 0, :], in0=lr_, in1=lr_, op=ALU.mult), [Blam], [Bqt])
                        K.op(dve, lambda: nc.vector.tensor_tensor(out=qt[:, 1, :], in0=li_, in1=li_, op=ALU.mult), [Blam], [Bqt])
                        K.op(dve, lambda: nc.vector.tensor_tensor(out=qt[:, 0, :], in0=qt[:, 0, :], in1=qt[:, 1, :], op=ALU.add), [Bqt], [Bqt])
                        K.op(dve, lambda: nc.vector.reciprocal(out=qt[:, 0, :], in_=qt[:, 0, :]), [Bqt], [Bqt])
                        K.op(dve, lambda: nc.vector.tensor_scalar(out=qt[:, 1, :], in0=a1r[:, 0, :], scalar1=-1.0, scalar2=None, op0=ALU.add), [Ba1r], [Bqt])
                        K.op(dve, lambda: nc.vector.tensor_tensor(out=qt[:, 2, :], in0=qt[:, 1, :], in1=lr_, op=ALU.mult), [Bqt, Blam], [Bqt])
                        K.op(dve, lambda: nc.vector.tensor_tensor(out=qt[:, 3, :], in0=a1i[:, 0, :], in1=li_, op=ALU.mult), [Ba1i, Blam], [Bqt])
                        K.op(dve, lambda: nc.vector.tensor_tensor(out=qt[:, 2, :], in0=qt[:, 2, :], in1=qt[:, 3, :], op=ALU.add), [Bqt], [Bqt])
                        K.op(dve, lambda: nc.vector.tensor_tensor(out=qt[:, 2, :], in0=qt[:, 2, :], in1=qt[:, 0, :], op=ALU.mult), [Bqt], [Bqt])
                        K.op(dve, lambda: nc.vector.tensor_tensor(out=qt[:, 3, :], in0=a1i[:, 0, :], in1=lr_, op=ALU.mult), [Ba1i, Blam], [Bqt])
                        K.op(dve, lambda: nc.vector.tensor_tensor(out=qt[:, 4, :], in0=qt[:, 1, :], in1=li_, op=ALU.mult), [Bqt, Blam], [Bqt])
                        K.op(dve, lambda: nc.vector.tensor_tensor(out=qt[:, 3, :], in0=qt[:, 3, :], in1=qt[:, 4, :], op=ALU.subtract), [Bqt], [Bqt])
                        K.op(dve, lambda: nc.vector.tensor_tensor(out=qt[:, 3, :], in0=qt[:, 3, :], in1=qt[:, 0, :], op=ALU.mult), [Bqt], [Bqt])
                        qr_b = qt[:, 2, :].unsqueeze(2).to_broadcast([Q, 48, 16])
                        qi_b = qt[:, 3, :].unsqueeze(2).to_broadcast([Q, 48, 16])
                        Bn, BBn = small([Q, 2, 48, 16], "Bn")
                        K.dma(Bn[:, 0], P["b_re"][L, d].rearrange("g p c -> p g c"), writes=[BBn])
                        K.dma(Bn[:, 1], P["b_im"][L, d].rearrange("g p c -> p g c"), writes=[BBn])
                        Bb, BBb = small([Q, 2, 48, 16], "Bbar")
                        tq, Btq = small([Q, 48, 16], "tq")
                        K.op(dve, lambda: nc.vector.tensor_tensor(out=Bb[:, 0], in0=Bn[:, 0], in1=qr_b, op=ALU.mult), [BBn, Bqt], [BBb])
                        K.op(dve, lambda: nc.vector.tensor_tensor(out=tq[:], in0=Bn[:, 1], in1=qi_b, op=ALU.mult), [BBn, Bqt], [Btq])
                        K.op(dve, lambda: nc.vector.tensor_tensor(out=Bb[:, 0], in0=Bb[:, 0], in1=tq[:], op=ALU.subtract), [BBb, Btq], [BBb])
                        K.op(dve, lambda: nc.vector.tensor_tensor(out=Bb[:, 1], in0=Bn[:, 1], in1=qr_b, op=ALU.mult), [BBn, Bqt], [BBb])
                        K.op(dve, lambda: nc.vector.tensor_tensor(out=tq[:], in0=Bn[:, 0], in1=qi_b, op=ALU.mult), [BBn, Bqt], [Btq])
                        K.op(dve, lambda: nc.vector.tensor_tensor(out=Bb[:, 1], in0=Bb[:, 1], in1=tq[:], op=ALU.add), [BBb, Btq], [BBb])
                        Cn, BCn = small([128, 2, 6, 64], "Cn")
                        K.dma(Cn[:, 0], P["c_re"][L, d].rearrange("(gt gi) c p -> (gi c) gt p", gi=8), writes=[BCn])
                        K.dma(Cn[:, 1], P["c_im"][L, d].rearrange("(gt gi) c p -> (gi c) gt p", gi=8), writes=[BCn])
                        Cc, BCc = small([Q, 2, 48, 16], "Cc")
                        for r_ in range(2):
                            for gt in range(6):
                                b = 1 + gt // 4
                                K.op(pe, lambda: nc.tensor.transpose(psum[0:Q, b, (gt % 4) * 128:(gt % 4 + 1) * 128], Cn[:, r_, gt, :], CN(C_ID)), [BCn, Bc], [PB[b]])
                            K.op(dve, lambda: nc.vector.tensor_copy(out=Cc[:, r_, 0:32, :].rearrange("p g c -> p (g c)"), in_=psum[0:Q, 1, :]), [PB[1]], [BCc])
                            K.op(dve, lambda: nc.vector.tensor_copy(out=Cc[:, r_, 32:48, :].rearrange("p g c -> p (g c)"), in_=psum[0:Q, 2, 0:256]), [PB[2]], [BCc])
                        (pXr, BpXr), (pXi, BpXi) = powers(eX, -1, "pX")
                        (pZr, BpZr), (pZi, BpZi) = powers(eZ, 1, "pZ")
                        (pVr, BpVr), (pVi, BpVi) = powers(eV, 1, "pV")
                        (pCr, BpCr), (pCi, BpCi) = powers(eC, 1, "pC")
                        GH = 24
                        tA, BtA = small([Q, GH, 8, 16], "tA")
                        tB, BtB = small([Q, GH, 8, 16], "tB")
                        Xr, BXr = small([Q, GH, 8, 16], "Xr")
                        Xi, BXi = small([Q, GH, 8, 16], "Xi")
                        Zr, BZr = small([Q, GH, 8, 16], "Zr")
                        Zi, BZi = small([Q, GH, 8, 16], "Zi")

                        def cmul(outr, Boutr, outi, Bouti, pw_r, Bpw_r, pw_i, Bpw_i, M, BM, g0, neg_im=False):
                            pr_b = pw_r[:, :, g0:g0 + GH].rearrange("p t g -> p g t").unsqueeze(3).to_broadcast([Q, GH, 8, 16])
                            pi_b = pw_i[:, :, g0:g0 + GH].rearrange("p t g -> p g t").unsqueeze(3).to_broadcast([Q, GH, 8, 16])
                            mr_b = M[:, 0, g0:g0 + GH, :].unsqueeze(2).to_broadcast([Q, GH, 8, 16])
                            mi_b = M[:, 1, g0:g0 + GH, :].unsqueeze(2).to_broadcast([Q, GH, 8, 16])
                            K.op(dve, lambda: nc.vector.tensor_tensor(out=tA[:], in0=pr_b, in1=mr_b, op=ALU.mult), [Bpw_r, BM], [BtA])
                            K.op(dve, lambda: nc.vector.tensor_tensor(out=tB[:], in0=pi_b, in1=mi_b, op=ALU.mult), [Bpw_i, BM], [BtB])
                            K.op(dve, lambda: nc.vector.tensor_tensor(out=outr, in0=tA[:], in1=tB[:], op=ALU.subtract), [BtA, BtB], [Boutr])
                            K.op(dve, lambda: nc.vector.tensor_tensor(out=tA[:], in0=pr_b, in1=mi_b, op=ALU.mult), [Bpw_r, BM], [BtA])
                            K.op(dve, lambda: nc.vector.tensor_tensor(out=tB[:], in0=pi_b, in1=mr_b, op=ALU.mult), [Bpw_i, BM], [BtB])
                            if neg_im:
                                K.op(dve, lambda: nc.vector.scalar_tensor_tensor(out=outi, in0=tA[:], scalar=-1.0, in1=tB[:], op0=ALU.mult, op1=ALU.subtract), [BtA, BtB], [Bouti])
                            else:
                                K.op(dve, lambda: nc.vector.tensor_tensor(out=outi, in0=tA[:], in1=tB[:], op=ALU.add), [BtA, BtB], [Bouti])

                        MK = CN(C_MF) if d == 0 else CN(C_MB)
                        for g0 in (0, GH):
                            cmul(Xr[:], BXr, Xi[:], BXi, pXr, BpXr, pXi, BpXi, Bb, BBb, g0)
                            cmul(Zr[:], BZr, Zi[:], BZi, pZr, BpZr, pZi, BpZi, Cc, BCc, g0, neg_im=True)
                            for q4 in range(GH // 4):
                                b = 3 + (q4 % 2)
                                for j in range(4):
                                    g = q4 * 4 + j
                                    K.op(pe, lambda: nc.tensor.matmul(psum[:, b, j * 128:(j + 1) * 128], lhsT=Xr[:, g].rearrange("p t c -> p (t c)"), rhs=Zr[:, g].rearrange("p t c -> p (t c)"), start=True, stop=False), [BXr, BZr], [PB[b]])
                                    K.op(pe, lambda: nc.tensor.matmul(psum[:, b, j * 128:(j + 1) * 128], lhsT=Xi[:, g].rearrange("p t c -> p (t c)"), rhs=Zi[:, g].rearrange("p t c -> p (t c)"), start=False, stop=True), [BXi, BZi], [PB[b]])
                                K.op(dve, lambda: nc.vector.tensor_tensor(out=WY[:, g0 + q4 * 4:g0 + q4 * 4 + 4, :], in0=psum[:, b, :].rearrange("p (a b) -> p a b", b=128), in1=MK.unsqueeze(1).to_broadcast([128, 4, 128]), op=ALU.mult), [PB[b], Bc], [BWY])
                            cmul(Xr[:], BXr, Xi[:], BXi, pVr, BpVr, pVi, BpVi, Bb, BBb, g0)
                            for (src, Bsrc, dstW, BdstW) in ((Xr, BXr, WVr, BWVr), (Xi, BXi, WVi, BWVi)):
                                for q8 in range(GH // 8):
                                    b = 5 + (q8 % 2)
                                    for j in range(8):
                                        g = q8 * 8 + j
                                        K.op(pe, lambda: nc.tensor.transpose(psum[:, b, j * 64:(j + 1) * 64], src[:, g].rearrange("p t c -> p (t c)"), consts[0:Q, C_ID, 0:Q]), [Bsrc, Bc], [PB[b]])
                                    K.op(act, lambda: nc.scalar.activation(out=dstW[:, g0 + q8 * 8:g0 + q8 * 8 + 8, :], in_=psum[:, b, :].rearrange("p (a b) -> p a b", b=64), func=AF.Copy), [PB[b]], [BdstW])
                            pr_b = pCr[:, :, g0:g0 + GH].rearrange("p t g -> p g t").unsqueeze(3).to_broadcast([Q, GH, 8, 16])
                            pi_b = pCi[:, :, g0:g0 + GH].rearrange("p t g -> p g t").unsqueeze(3).to_broadcast([Q, GH, 8, 16])
                            mr_b = Cc[:, 0, g0:g0 + GH, :].unsqueeze(2).to_broadcast([Q, GH, 8, 16])
                            mi_b = Cc[:, 1, g0:g0 + GH, :].unsqueeze(2).to_broadcast([Q, GH, 8, 16])
                            wcr_v = WCr[0:Q, g0:g0 + GH, :].rearrange("p g (t c) -> p g t c", c=16)
                            wci_v = WCi[0:Q, g0:g0 + GH, :].rearrange("p g (t c) -> p g t c", c=16)
                            K.op(dve, lambda: nc.vector.tensor_tensor(out=tA[:], in0=pr_b, in1=mr_b, op=ALU.mult), [BpCr, BCc], [BtA])
                            K.op(dve, lambda: nc.vector.tensor_tensor(out=tB[:], in0=pi_b, in1=mi_b, op=ALU.mult), [BpCi, BCc], [BtB])
                            K.op(dve, lambda: nc.vector.tensor_tensor(out=wcr_v, in0=tA[:], in1=tB[:], op=ALU.subtract), [BtA, BtB], [BWCr])
                            K.op(dve, lambda: nc.vector.tensor_tensor(out=tA[:], in0=pr_b, in1=mi_b, op=ALU.mult), [BpCr, BCc], [BtA])
                            K.op(dve, lambda: nc.vector.tensor_tensor(out=tB[:], in0=pi_b, in1=mr_b, op=ALU.mult), [BpCi, BCc], [BtB])
                            K.op(dve, lambda: nc.vector.scalar_tensor_tensor(out=wci_v, in0=tA[:], scalar=-1.0, in1=tB[:], op0=ALU.mult, op1=ALU.subtract), [BtA, BtB], [BWCi])
                        if d == 0:
                            dn, Bdn = small([48, 16], "dn")
                            dT, BdT = small([16, 48], "dT")
                            dsk, Bdsk = small([128, 48], "dsk")
                            K.dma(dn[:], P["d_s5"][L].rearrange("(g c) -> g c", c=16), writes=[Bdn])
                            K.op(pe, lambda: nc.tensor.transpose(psum[0:16, 7, 0:48], dn[:], consts[0:48, C_ID, 0:48]), [Bdn, Bc], [PB[7]])
                            K.op(dve, lambda: nc.vector.tensor_copy(out=dT[:], in_=psum[0:16, 7, 0:48]), [PB[7]], [BdT])
                            K.op(pe, lambda: nc.tensor.matmul(psum[:, 7, 64:112], lhsT=consts[0:16, C_E, :], rhs=dT[:], start=True, stop=True), [Bc, BdT], [PB[7]])
                            K.op(dve, lambda: nc.vector.tensor_copy(out=dsk[:], in_=psum[:, 7, 64:112]), [PB[7]], [Bdsk])
                            for g in range(48):
                                K.op(dve, lambda: nc.vector.scalar_tensor_tensor(out=WY[:, g, :], in0=CN(C_ID), scalar=dsk[:, g:g + 1], in1=WY[:, g, :], op0=ALU.mult, op1=ALU.add), [Bc, Bdsk, BWY], [BWY])
                        K.barrier()

                    cw, Bcw = sb(ph, [128, 20, 4], F32, "cw")
                    cbias, Bcb = sb(ph, [128, 20], F32, "cbias")
                    with ExitStack() as tc_:
                        cwn, Bcwn = sb(tc_, [4, 20, 128], F32, "cwn")
                        cbn, Bcbn = sb(tc_, [20, 128], F32, "cbn")
                        K.dma(cwn[:], P["conv_w"][L].rearrange("j (t p) -> j t p", p=128), writes=[Bcwn])
                        for t_ in range(20):
                            K.op(pe, lambda: nc.tensor.transpose(psum[:, 0, t_ * 4:(t_ + 1) * 4], cwn[:, t_, :], consts[0:4, C_ID, 0:4]), [Bcwn, Bc], [PB[0]])
                        K.op(dve, lambda: nc.vector.tensor_copy(out=cw[:].rearrange("p a b -> p (a b)"), in_=psum[:, 0, 0:80]), [PB[0]], [Bcw])
                        K.dma(cbn[:], P["conv_b"][L].rearrange("(t p) -> t p", p=128), writes=[Bcbn])
                        K.op(pe, lambda: nc.tensor.transpose(psum[:, 1, 0:20], cbn[:], consts[0:20, C_ID, 0:20]), [Bcbn, Bc], [PB[1]])
                        K.op(dve, lambda: nc.vector.tensor_copy(out=cbias[:], in_=psum[:, 1, 0:20]), [PB[1]], [Bcb])
                        K.barrier()
                    CD, BCD = sb(ph, [128, 20, 4, 128], BF16, "CD")
                    for t_ in range(20):
                        for j in range(4):
                            if (t_ * 4 + j) % 2 == 0:
                                K.op(dve, lambda: nc.vector.tensor_scalar(out=CD[:, t_, j, :], in0=CN(C_ID), scalar1=cw[:, t_, j:j + 1], scalar2=None, op0=ALU.mult), [Bc, Bcw], [BCD])
                            else:
                                K.op(act, lambda: nc.scalar.activation(out=CD[:, t_, j, :], in_=CN(C_ID), func=AF.Copy, scale=cw[:, t_, j:j + 1]), [Bc, Bcw], [BCD])
                    dtb, Bdtb = bcast_row(ph, P["dt_bias"][L, d], 24, "dtb")
                    Arow, BAr = bcast_row(ph, P["a_log"][L, d], 24, "Arow")
                    K.op(act, lambda: nc.scalar.activation(out=Arow[:], in_=Arow[:], func=AF.Exp), [BAr], [BAr])
                    K.op(dve, lambda: nc.vector.tensor_scalar(out=Arow[:], in0=Arow[:], scalar1=-1.0, scalar2=None, op0=ALU.mult), [BAr], [BAr])
                    Drow, BDr = bcast_row(ph, P["d_m2"][L], 24, "Drow")
                    TRI = CN(C_UT) if d == 0 else CN(C_LT)
                    MST = CN(C_SL) if d == 0 else CN(C_SU)
                    MCB = CN(C_UT) if d == 0 else CN(C_LT)
                    xe = [sb(ph, [128, 20, 131], BF16, "xe") for _ in range(3)]
                    dtr = [sb(ph, [128, 24], F32, "dtr") for _ in range(3)]
                    uk, Buk = sb(ph, [16, 8, S5W], BF16, "uk")
                    ukg, Bukg = sb(ph, [16, 48, 8, 16], BF16, "ukg")
                    Ut2 = [sb(ph, [128, 48, 16], BF16, "Ut") for _ in range(2)]
                    Vsb, BVsb = sb(ph, [64, 16, 2, 48], F32, "Vsb")
                    Hh, BHh = sb(ph, [64, 17, 4, 48], F32, "Hh")
                    Sbf2 = [sb(ph, [128, 2, 48, 16], BF16, "Sbf") for _ in range(2)]
                    for t_, Bt_ in Sbf2:
                        K.op(dve, lambda: nc.vector.memset(t_[:], 0.0), [], [Bt_])
                    Qt, BQt = sb(ph, [64, 2, 2, 48], F32, "Qt")
                    Rt, BRt = sb(ph, [64, 2, 2, 48], F32, "Rt")
                    part = [sb(ph, [16, 8, 128], F32, "part") for _ in range(2)]
                    ysc = [sb(ph, [16, 8, 128], F32, "ysc") for _ in range(2)]
                    xS, BxS = sb(ph, [128, 20, 128], BF16, "xS")
                    xdt, Bxdt = sb(ph, [128, 24, 64], BF16, "xdt")
                    xdin, Bxdin = sb(ph, [128, 24, 64], BF16, "xdin")
                    Btok, BBtok = sb(ph, [128, 4, 128], BF16, "Btok")
                    dts, Bdts = sb(ph, [128, 8, 24], F32, "dts")
                    LHh2 = [sb(ph, [128, 6, 128], BF16, "LHh") for _ in range(2)]
                    LHl2 = [sb(ph, [128, 6, 128], BF16, "LHl") for _ in range(2)]
                    cb16, Bcb16 = sb(ph, [128, 2, 128], BF16, "cb16")
                    K.op(dve, lambda: nc.vector.tensor_copy(out=cb16[:, 0, :], in_=TRI), [Bc], [Bcb16])
                    K.op(dve, lambda: nc.vector.tensor_copy(out=cb16[:, 1, :], in_=MST), [Bc], [Bcb16])
                    TRIb, MSTb = cb16[:, 0, :], cb16[:, 1, :]
                    dthl, Bdthl = sb(ph, [128, 2, 24], BF16, "dthl")
                    dtf, Bdtf = sb(ph, [128, 24], F32, "dtf")
                    dlo, Bdlo = sb(ph, [128, 24], F32, "dlo")
                    Dm2 = [sb(ph, [128, 6, 128], BF16, "Dm") for _ in range(2)]
                    MT2 = [sb(ph, [128, 6, 128], BF16, "MT") for _ in range(2)]
                    cbm, Bcbm = sb(ph, [128, 4, 128], BF16, "cbm")
                    Hst, BHst = sb(ph, [128, 24, 64], F32, "Hst")
                    Hb, BHb = sb(ph, [128, 24, 64], BF16, "Hb")
                    t1s = [sb(ph, [128, 6, 64], F32, "t1") for _ in range(2)]
                    pm2, Bpm2 = sb(ph, [128, 24, 64], F32, "pm2")
                    if debug and L == 0:
                        print('SBUF remaining in pass d=%d: %d bytes' % (d, nc.sbuf_bytes_remaining))
                    K.op(dve, lambda: nc.vector.memset(Hst[:], 0.0), [], [BHst])
                    K.op(dve, lambda: nc.vector.memset(Hh[:], 0.0), [], [BHh])
                    for t_, Bt_ in xe:
                        K.op(pool, lambda: nc.gpsimd.memset(t_[:], 0.0), [], [Bt_])
                    order = list(range(NCH)) if d == 0 else list(range(NCH - 1, -1, -1))
                    su_v = lambda c: s_u[c].rearrange("(k t) c -> k t c", t=8)
                    sy_v = lambda c: s_ys5[c].rearrange("(k t) c -> k t c", t=8)
                    A8a = A8[:, 0].unsqueeze(1).to_broadcast([64, 2, 2, 48])
                    A8b = A8[:, 1].unsqueeze(1).to_broadcast([64, 2, 2, 48])

                    def load_xe(c):
                        t_, Bt_ = xe[c % 3]
                        K.dma(t_[:, :, 1:129], s_xbc[c], reads=[B_xbc[c]], writes=[Bt_])
                        q_, Bq_ = dtr[c % 3]
                        K.dma(q_[:], s_dt[c][:, d * 24:(d + 1) * 24], reads=[B_dt[c]], writes=[Bq_])

                    def load_u(c):
                        K.dma(uk[:], su_v(c), reads=[B_u[c]], writes=[Buk])

                    def load_part(c, ct):
                        p_, Bp_ = part[ct % 2]
                        K.dma(p_[:], sy_v(c)[:, :, ct * 128:(ct + 1) * 128], reads=[B_ys5[c]], writes=[Bp_])

                    def gen_s5ab(it, c):
                        Ut, BUt = Ut2[it % 2]
                        Sbf, BSbf = Sbf2[it % 2]
                        m_in64 = seqm[0:64, c:c + 1] if d == 0 else seqm[0:64, c + 1:c + 2]
                        K.op(dve, lambda: nc.vector.tensor_copy(out=ukg[:, 0:24], in_=uk[:, :, 0:384].rearrange("k t (g c) -> k g t c", c=16)), [Buk], [Bukg])
                        K.op(dve, lambda: nc.vector.tensor_copy(out=ukg[:, 24:48], in_=uk[:, :, 384:768].rearrange("k t (g c) -> k g t c", c=16)), [Buk], [Bukg])
                        if it + 1 < NCH:
                            load_u(order[it + 1])
                        pb = bank_bf(0)
                        for g in range(48):
                            K.op(pe, lambda: nc.tensor.transpose(pb[:, g * 16:(g + 1) * 16], ukg[:, g].rearrange("k t c -> k (t c)"), ident_bf[0:16, 0:16]), [Bukg, Bidb], [PB[0]])
                        K.op(act, lambda: nc.scalar.activation(out=Ut[:], in_=pb[:, 0:768].rearrange("p (a b) -> p a b", b=16), func=AF.Copy), [PB[0]], [BUt])
                        yield
                        for r3 in range(3):
                            b = 1 if r3 % 2 == 0 else 0
                            for gi in range(16):
                                g = r3 * 16 + gi
                                K.op(pe, lambda: nc.tensor.matmul(psum[0:64, b, gi * 16:(gi + 1) * 16], lhsT=WVr[:, g, :], rhs=Ut[:, g, :], start=True, stop=True), [BWVr, BUt], [PB[b]])
                                K.op(pe, lambda: nc.tensor.matmul(psum[0:64, b, 256 + gi * 16:256 + (gi + 1) * 16], lhsT=WVi[:, g, :], rhs=Ut[:, g, :], start=True, stop=True), [BWVi, BUt], [PB[b]])
                            src = psum[0:64, b, :].rearrange("p (r g k) -> p k r g", r=2, g=16)
                            dst = Vsb[:, :, :, r3 * 16:(r3 + 1) * 16]
                            K.op(act, lambda: nc.scalar.activation(out=dst, in_=src, func=AF.Copy), [PB[b]], [BVsb])
                        yield
                        hin0 = 0 if d == 0 else 16
                        hprev_last = 16 if d == 0 else 0
                        if it == 0:
                            K.op(dve, lambda: nc.vector.memset(Hh[:, hin0], 0.0), [], [BHh])
                        else:
                            K.op(dve, lambda: nc.vector.tensor_scalar(out=Hh[:, hin0], in0=Hh[:, hprev_last], scalar1=m_in64, scalar2=None, op0=ALU.mult), [BHh, Bsm], [BHh])
                        ks = list(range(16)) if d == 0 else list(range(15, -1, -1))
                        for k_ in ks:
                            si, so = (k_, k_ + 1) if d == 0 else (k_ + 1, k_)
                            HA = Hh[:, si, 0:2, :].unsqueeze(1).to_broadcast([64, 2, 2, 48])
                            HB = Hh[:, si, 1:3, :].unsqueeze(1).to_broadcast([64, 2, 2, 48])
                            Vb = Vsb[:, k_].unsqueeze(1).to_broadcast([64, 2, 2, 48])
                            K.op(dve, lambda: nc.vector.tensor_tensor(out=Qt[:], in0=A8a, in1=HA, op=ALU.mult), [BA8, BHh], [BQt])
                            K.op(dve, lambda: nc.vector.tensor_tensor(out=Rt[:], in0=A8b, in1=HB, op=ALU.mult), [BA8, BHh], [BRt])
                            K.op(dve, lambda: nc.vector.tensor_tensor(out=Qt[:], in0=Qt[:], in1=Rt[:], op=ALU.add), [BQt, BRt], [BQt])
                            K.op(dve, lambda: nc.vector.tensor_tensor(out=Hh[:, so].rearrange("p (a b) g -> p a b g", a=2), in0=Qt[:], in1=Vb, op=ALU.add), [BQt, BVsb], [BHh])
                            yield
                        hs0 = 0 if d == 0 else 1
                        K.op(act, lambda: nc.scalar.activation(out=Sbf[0:64], in_=Hh[:, hs0:hs0 + 16, 0:2, :].rearrange("p k r g -> p r g k"), func=AF.Copy), [BHh], [BSbf])

                    def gen_s5c(it, c):
                        Ut, BUt = Ut2[it % 2]
                        Sbf, BSbf = Sbf2[it % 2]
                        if d == 1:
                            load_part(c, 0)
                            load_part(c, 1)
                        for ct in range(6):
                            y_, By_ = ysc[ct % 2]
                            for hf in range(2):
                                b = 2 + hf
                                for g4 in range(4):
                                    g = ct * 8 + hf * 4 + g4
                                    o_ = psum[0:16, b, g4 * 128:(g4 + 1) * 128]
                                    K.op(pe, lambda: nc.tensor.matmul(o_, lhsT=Ut[:, g, :], rhs=WY[:, g, :], start=True, stop=False), [BUt, BWY], [PB[b]])
                                    K.op(pe, lambda: nc.tensor.matmul(o_, lhsT=Sbf[:, 0, g, :], rhs=WCr[:, g, :], start=False, stop=False), [BSbf, BWCr], [PB[b]])
                                    K.op(pe, lambda: nc.tensor.matmul(o_, lhsT=Sbf[:, 1, g, :], rhs=WCi[:, g, :], start=False, stop=True), [BSbf, BWCi], [PB[b]])
                                ov = y_[:, :, hf * 64:(hf + 1) * 64].rearrange("k t (g c) -> k g t c", c=16)
                                pv = psum[0:16, b, :].rearrange("k (g t c) -> k g t c", g=4, t=8)
                                if d == 1:
                                    p_, Bp_ = part[ct % 2]
                                    K.op(dve, lambda: nc.vector.tensor_tensor(out=ov, in0=pv, in1=p_[:, :, hf * 64:(hf + 1) * 64].rearrange("k t (g c) -> k g t c", c=16), op=ALU.add), [PB[b], Bp_], [By_])
                                else:
                                    K.op(dve, lambda: nc.vector.tensor_copy(out=ov, in_=pv), [PB[b]], [By_])
                                yield
                            K.dma(sy_v(c)[:, :, ct * 128:(ct + 1) * 128], y_[:], reads=[By_], writes=[B_ys5[c]])
                            if d == 1 and ct + 2 < 6:
                                load_part(c, ct + 2)

                    def gen_ssd(it, c):
                        m_in = seqm[:, c:c + 1] if d == 0 else seqm[:, c + 1:c + 2]
                        xe_, Bxe_ = xe[c % 3]
                        if c - 1 >= 0:
                            xl, Bxl = xe[(c - 1) % 3]
                            K.op(dve, lambda: nc.vector.tensor_scalar(out=xe_[:, :, 0:1], in0=xl[:, :, 128:129], scalar1=seqm[:, c:c + 1], scalar2=None, op0=ALU.mult), [Bxl, Bsm], [Bxe_])
                        else:
                            K.op(dve, lambda: nc.vector.memset(xe_[:, :, 0:1], 0.0), [], [Bxe_])
                        if c + 1 < NCH:
                            xr, Bxr = xe[(c + 1) % 3]
                            K.op(dve, lambda: nc.vector.tensor_scalar(out=xe_[:, :, 129:131], in0=xr[:, :, 1:3], scalar1=seqm[:, c + 1:c + 2], scalar2=None, op0=ALU.mult), [Bxr, Bsm], [Bxe_])
                        else:
                            K.op(dve, lambda: nc.vector.memset(xe_[:, :, 129:131], 0.0), [], [Bxe_])
                        if it + 2 < NCH:
                            load_xe(order[it + 2])
                        q_, Bq_ = dtr[c % 3]
                        K.op(dve, lambda: nc.vector.tensor_tensor(out=dts[:, 7, :], in0=q_[:], in1=dtb[:], op=ALU.add), [Bq_, Bdtb], [Bdts])
                        K.op(act, lambda: nc.scalar.activation(out=dts[:, 7, :], in_=dts[:, 7, :], func=AF.Exp), [Bdts], [Bdts])
                        K.op(act, lambda: nc.scalar.activation(out=dts[:, 0, :], in_=dts[:, 7, :], func=AF.Ln, bias=1.0), [Bdts], [Bdts])
                        K.op(dve, lambda: nc.vector.tensor_tensor(out=dts[:, 1, :], in0=dts[:, 0, :], in1=Arow[:], op=ALU.mult), [Bdts, BAr], [Bdts])
                        K.op(dve, lambda: nc.vector.tensor_copy(out=dthl[:, 0, :], in_=dts[:, 1, :]), [Bdts], [Bdthl])
                        K.op(dve, lambda: nc.vector.tensor_copy(out=dtf[:], in_=dthl[:, 0, :]), [Bdthl], [Bdtf])
                        K.op(dve, lambda: nc.vector.tensor_tensor(out=dlo[:], in0=dts[:, 1, :], in1=dtf[:], op=ALU.subtract), [Bdts, Bdtf], [Bdlo])
                        K.op(pe, lambda: nc.tensor.matmul(psum[:, 6, 0:24], lhsT=TRI, rhs=dts[:, 1, :], start=True, stop=True), [Bc, Bdts], [PB[6]])
                        K.op(pe, lambda: nc.tensor.matmul(psum[:, 6, 32:56], lhsT=CN(C_ONES), rhs=dts[:, 1, :], start=True, stop=True), [Bc, Bdts], [PB[6]])
                        K.op(act, lambda: nc.scalar.activation(out=dts[:, 2, :], in_=psum[:, 6, 0:24], func=AF.Copy), [PB[6]], [Bdts])
                        K.op(act, lambda: nc.scalar.activation(out=dts[:, 3, :], in_=psum[:, 6, 0:24], func=AF.Exp), [PB[6]], [Bdts])
                        K.op(dve, lambda: nc.vector.tensor_tensor(out=dts[:, 7, :], in0=psum[:, 6, 32:56], in1=dts[:, 2, :], op=ALU.subtract), [PB[6], Bdts], [Bdts])
                        K.op(act, lambda: nc.scalar.activation(out=dts[:, 4, :], in_=dts[:, 7, :], func=AF.Exp), [Bdts], [Bdts])
                        K.op(act, lambda: nc.scalar.activation(out=dts[:, 5, :], in_=psum[:, 6, 32:56], func=AF.Exp), [PB[6]], [Bdts])
                        K.op(dve, lambda: nc.vector.tensor_scalar(out=dts[:, 6, :], in0=dts[:, 5, :], scalar1=m_in, scalar2=None, op0=ALU.mult), [Bdts, Bsm], [Bdts])
                        yield
                        for q in range(5):
                            b = 4 + (q % 2)
                            for j in range(4):
                                t_ = q * 4 + j
                                for tap in range(4):
                                    K.op(pe, lambda: nc.tensor.matmul(psum[:, b, j * 128:(j + 1) * 128], lhsT=CD[:, t_, tap, :], rhs=xe_[:, t_, tap:tap + 128], start=(tap == 0), stop=(tap == 3)), [BCD, Bxe_], [PB[b]])
                            for j in range(4):
                                t_ = q * 4 + j
                                K.op(act, lambda: nc.scalar.activation(out=xS[:, t_, :], in_=psum[:, b, j * 128:(j + 1) * 128], func=AF.Silu, bias=cbias[:, t_:t_ + 1]), [PB[b], Bcb], [BxS])
                            yield
                        for j in range(12):
                            b = 4 + j // 8
                            K.op(pe, lambda: nc.tensor.transpose(bank_bf(b)[:, (j % 8) * 128:(j % 8 + 1) * 128], xS[:, j, :], ident_bf[:]), [BxS, Bidb], [PB[b]])
                        for j in range(4):
                            K.op(pe, lambda: nc.tensor.transpose(bank_bf(5)[:, 512 + j * 128:512 + (j + 1) * 128], xS[:, 12 + j, :], ident_bf[:]), [BxS, Bidb], [PB[5]])
                        xtok = lambda h0, h1: (bank_bf(4)[:, h0 * 64:h1 * 64] if h1 <= 16 else bank_bf(5)[:, (h0 - 16) * 64:(h1 - 16) * 64]).rearrange("p (h q) -> p h q", q=64)
                        yield
                        for (h0, h1, b) in ((0, 16, 4), (16, 24, 5)):
                            dtv = dts[:, 0, h0:h1].unsqueeze(2).to_broadcast([128, h1 - h0, 64])
                            K.op(dve, lambda: nc.vector.tensor_tensor(out=xdt[:, h0:h1, :], in0=xtok(h0, h1), in1=dtv, op=ALU.mult), [PB[b], Bdts], [Bxdt])
                            if d == 0:
                                Dv = Drow[:, h0:h1].unsqueeze(2).to_broadcast([128, h1 - h0, 64])
                                K.op(dve, lambda: nc.vector.tensor_tensor(out=pm2[:, h0:h1, :], in0=xtok(h0, h1), in1=Dv, op=ALU.mult), [PB[b], BDr], [Bpm2])
                        K.op(act, lambda: nc.scalar.activation(out=Btok[:], in_=bank_bf(5)[:, 512:1024].rearrange("p (a b) -> p a b", b=128), func=AF.Copy), [PB[5]], [BBtok])
                        K.op(dve, lambda: nc.vector.tensor_tensor(out=xdin[:], in0=xdt[:], in1=dts[:, 4, :].unsqueeze(2).to_broadcast([128, 24, 64]), op=ALU.mult), [Bxdt, Bdts], [Bxdin])
                        K.op(act, lambda: nc.scalar.activation(out=Hb[:], in_=Hst[:], func=AF.Copy, scale=m_in), [BHst, Bsm], [BHb])
                        yield
                        for g in range(4):
                            K.op(pe, lambda: nc.tensor.matmul(psum[:, 7, g * 128:(g + 1) * 128], lhsT=xS[:, 12 + g, :], rhs=xS[:, 16 + g, :], start=True, stop=True), [BxS], [PB[7]])
                        K.op(dve, lambda: nc.vector.tensor_tensor(out=cbm[:], in0=psum[:, 7, :].rearrange("p (a b) -> p a b", b=128), in1=MCB.unsqueeze(1).to_broadcast([128, 4, 128]), op=ALU.mult), [PB[7], Bc], [Bcbm])
                        yield
                        def s1a(g):
                            hs = slice(g * 6, g * 6 + 6)
                            (LHh, BLHh), (LHl, BLHl) = LHh2[g % 2], LHl2[g % 2]
                            for j in range(6):
                                h = g * 6 + j
                                K.op(act, lambda: nc.scalar.activation(out=LHh[:, j, :], in_=MSTb, func=AF.Copy, scale=dtf[:, h:h + 1]), [Bcb16, Bdtf], [BLHh])
                                K.op(act, lambda: nc.scalar.activation(out=LHl[:, j, :], in_=MSTb, func=AF.Copy, scale=dlo[:, h:h + 1]), [Bcb16, Bdlo], [BLHl])
                            for j in range(6):
                                b = 4 + j // 4
                                o_ = psum[:, b, (j % 4) * 128:(j % 4 + 1) * 128]
                                K.op(pe, lambda: nc.tensor.matmul(o_, lhsT=LHh[:, j, :], rhs=TRIb, start=True, stop=False), [BLHh, Bcb16], [PB[b]])
                                K.op(pe, lambda: nc.tensor.matmul(o_, lhsT=LHl[:, j, :], rhs=TRIb, start=False, stop=True), [BLHl, Bcb16], [PB[b]])

                        def s1b(g):
                            (Dm, BDm), (MT, BMT) = Dm2[g % 2], MT2[g % 2]
                            K.op(act, lambda: nc.scalar.activation(out=Dm[:, 0:4, :], in_=psum[:, 4, :].rearrange("p (a b) -> p a b", b=128), func=AF.Exp), [PB[4]], [BDm])
                            K.op(act, lambda: nc.scalar.activation(out=Dm[:, 4:6, :], in_=psum[:, 5, 0:256].rearrange("p (a b) -> p a b", b=128), func=AF.Exp), [PB[5]], [BDm])
                            K.op(dve, lambda: nc.vector.tensor_tensor(out=MT[:], in0=Dm[:], in1=cbm[:, g:g + 1, :].to_broadcast([128, 6, 128]), op=ALU.mult), [BDm, Bcbm], [BMT])

                        def s2(g):
                            hs = slice(g * 6, g * 6 + 6)
                            MT, BMT = MT2[g % 2]
                            tq_, Btq_ = t1s[g % 2]
                            for j in range(6):
                                h = g * 6 + j
                                K.op(pe, lambda: nc.tensor.matmul(psum[:, 6, j * 64:(j + 1) * 64], lhsT=MT[:, j, :], rhs=xdt[:, h, :], start=True, stop=True), [BMT, Bxdt], [PB[6]])
                                K.op(pe, lambda: nc.tensor.matmul(psum[:, 7, j * 64:(j + 1) * 64], lhsT=xS[:, 16 + g, :], rhs=Hb[:, h, :], start=True, stop=True), [BxS, BHb], [PB[7]])
                            K.op(dve, lambda: nc.vector.tensor_tensor(out=tq_[:], in0=psum[:, 7, 0:384].rearrange("p (h q) -> p h q", q=64), in1=dts[:, 3, hs].unsqueeze(2).to_broadcast([128, 6, 64]), op=ALU.mult), [PB[7], Bdts], [Btq_])
                            K.op(dve, lambda: nc.vector.tensor_tensor(out=tq_[:], in0=tq_[:], in1=psum[:, 6, 0:384].rearrange("p (h q) -> p h q", q=64), op=ALU.add), [Btq_, PB[6]], [Btq_])
                            K.op(dve, lambda: nc.vector.tensor_tensor(out=pm2[:, hs, :], in0=pm2[:, hs, :], in1=tq_[:], op=ALU.add), [Bpm2, Btq_], [Bpm2])

                        for stg in (lambda: s1a(0), lambda: s1b(0), lambda: s1a(1), lambda: s2(0), lambda: s1b(1), lambda: s1a(2), lambda: s2(1), lambda: s1b(2), lambda: s1a(3), lambda: s2(2), lambda: s1b(3), lambda: s2(3)):
                            stg()
                            yield
                        K.dma(s_ym2[c], pm2[:].rearrange("p a b -> p (a b)"), reads=[Bpm2], writes=[B_ym2[c]])
                        if d == 1 and it + 1 < NCH:
                            cn = order[it + 1]
                            K.dma(pm2[:].rearrange("p a b -> p (a b)"), s_ym2[cn], reads=[B_ym2[cn]], writes=[Bpm2])
                        K.op(dve, lambda: nc.vector.tensor_tensor(out=Hst[:], in0=Hst[:], in1=dts[:, 6, :].unsqueeze(2).to_broadcast([128, 24, 64]), op=ALU.mult), [BHst, Bdts], [BHst])
                        for g in range(4):
                            b = 6 + (g % 2)
                            hs = slice(g * 6, g * 6 + 6)
                            for j in range(6):
                                h = g * 6 + j
                                K.op(pe, lambda: nc.tensor.matmul(psum[:, b, j * 64:(j + 1) * 64], lhsT=Btok[:, g, :], rhs=xdin[:, h, :], start=True, stop=True), [BBtok, Bxdin], [PB[b]])
                            K.op(dve, lambda: nc.vector.tensor_tensor(out=Hst[:, hs, :], in0=Hst[:, hs, :], in1=psum[:, b, 0:384].rearrange("p (h q) -> p h q", q=64), op=ALU.add), [BHst, PB[b]], [BHst])
                            yield

                    load_xe(order[0])
                    if NCH > 1:
                        load_xe(order[1])
                    load_u(order[0])
                    if d == 1:
                        K.dma(pm2[:].rearrange("p a b -> p (a b)"), s_ym2[order[0]], reads=[B_ym2[order[0]]], writes=[Bpm2])
                    for it, c in enumerate(order):
                        gens = [[gen_s5ab(it, c), 0], [gen_ssd(it, c), 0]]
                        if it > 0:
                            gens.append([gen_s5c(it - 1, order[it - 1]), 9])
                        run_gens(gens)
                    run_gens([[gen_s5c(NCH - 1, order[NCH - 1]), 0]])
                    K.barrier()
            if LVL < 3:
                break
            with ExitStack() as ph:
                pw_ = [sb(ph, [128, 6, S5W], BF16, "wglu"), sb(ph, [128, 6, D], BF16, "ws5o"), sb(ph, [128, 12, D], BF16, "wm2o"), sb(ph, [128, 8, D], BF16, "wo")]
                with ExitStack() as st_:
                    stage = [sb(st_, [128, 1024], F32, "stg") for _ in range(3)]
                    wglu, Bwglu = load_weight(ph, P["w_glu"][L], 6, S5W, "wglu", stage, [act, dve], pre=pw_[0])
                    ws5o, Bws5o = load_weight(ph, P["w_s5_out"][L], 6, D, "ws5o", stage, [act, dve], pre=pw_[1])
                    wm2o, Bwm2o = load_weight(ph, P["w_m2_out"][L], 12, D, "wm2o", stage, [act, dve], pre=pw_[2])
                    wo, Bwo = load_weight(ph, P["w_o"][L], 8, D, "wo", stage, [act, dve], pre=pw_[3])
                    K.barrier()
                nwb, Bnwb = bcast_row(ph, P["m2_norm_w"][L], M2I, "nwb")
                bgn, Bbgn = sb(ph, [6, 128], F32, "bgn")
                bg, Bbg = sb(ph, [128, 6], F32, "bg")
                K.dma(bgn[:], P["b_glu"][L].rearrange("(t p) -> t p", p=128), writes=[Bbgn])
                K.op(pe, lambda: nc.tensor.transpose(psum[:, 0, 0:6], bgn[:], consts[0:6, C_ID, 0:6]), [Bbgn, Bc], [PB[0]])
                K.op(dve, lambda: nc.vector.tensor_copy(out=bg[:], in_=psum[:, 0, 0:6]), [PB[0]], [Bbg])
                NB = 2
                ysk = [sb(ph, [16, 8, S5W], F32, "ysk") for _ in range(1)] * NB
                ym = [sb(ph, [128, M2I], F32, "ym") for _ in range(NB)]
                zt = [sb(ph, [128, M2I], BF16, "zt") for _ in range(NB)]
                gt_ = [sb(ph, [128, 2 * D], BF16, "gt") for _ in range(NB)]
                xt = [sb(ph, [128, D], F32, "x") for _ in range(3)]
                sq, Bsq = sb(ph, [128, 6, 128], F32, "sq")
                tt, Btt = sb(ph, [128, 6, 128], F32, "tt")
                hT, BhT = sb(ph, [128, 6, 128], BF16, "hT5")
                sg, Bsg = sb(ph, [128, 6, 128], BF16, "sg")
                s5a, Bs5a = sb(ph, [128, 6, 128], BF16, "s5a")
                mg, Bmg = sb(ph, [128, D], F32, "mg")
                mg2, Bmg2 = sb(ph, [128, D], F32, "mg2")
                mgb2 = [sb(ph, [128, D], BF16, "mgb") for _ in range(2)]
                mT2 = [sb(ph, [128, 8, 128], BF16, "mT") for _ in range(2)]
                sz, Bsz = sb(ph, [128, M2I], F32, "sz")
                gz, Bgz = sb(ph, [128, M2I], F32, "gz")
                junk2, Bjunk2 = sb(ph, [128, 384], BF16, "junk2")
                ss4, Bss4 = sb(ph, [128, 4], F32, "ss4")
                gnb, Bgnb = sb(ph, [128, M2I], BF16, "gnb")
                gT, BgT = sb(ph, [128, 12, 128], BF16, "gT")
                x1t = [sb(ph, [128, D], F32, "x1") for _ in range(NB)]

                def tl_load(c):
                    i = c % NB
                    K.dma(ym[i][0][:], s_ym2[c], reads=[B_ym2[c]], writes=[ym[i][1]])
                    K.dma(zt[i][0][:], s_z[c], reads=[B_z[c]], writes=[zt[i][1]])
                    K.dma(gt_[i][0][:], s_gate[c], reads=[B_gate[c]], writes=[gt_[i][1]])
                    K.dma(xt[c % 3][0][:], xsrc[c], reads=([Bxsrc[c]] if Bxsrc else []), writes=[xt[c % 3][1]])

                def ys_load(c):
                    K.dma(ysk[0][0][:], s_ys5[c].rearrange("(k t) c -> k t c", t=8), reads=[B_ys5[c]], writes=[ysk[0][1]])

                if debug and L == 0:
                    print('SBUF remaining in tail: %d bytes' % nc.sbuf_bytes_remaining)
                tl_load(0)
                ys_load(0)
                for c in range(NCH):
                    if c + 1 < NCH:
                        tl_load(c + 1)
                    i = c % NB
                    (ys_, Bys_), (ym_, Bym_), (z_, Bz_), (g_, Bg_) = ysk[i], ym[i], zt[i], gt_[i]
                    def gen_t5():
                        for ct in range(6):
                            b = ct // 3
                            for t_ in range(8):
                                col = (ct % 3) * 128 + t_ * 16
                                K.op(pe, lambda: nc.tensor.transpose(psum[:, b, col:col + 16], ys_[:, t_, ct * 128:(ct + 1) * 128], consts[0:16, C_ID, 0:16]), [Bys_, Bc], [PB[b]])
                        if c + 1 < NCH:
                            ys_load(c + 1)
                        yield
                        yT = lambda b: psum[:, b, 0:384].rearrange("p (c t k) -> p c k t", c=3, t=8)
                        for b in (0, 1):
                            cs_ = slice(b * 3, b * 3 + 3)
                            sqv = sq[:, cs_, :].rearrange("p c (k t) -> p c k t", t=8)
                            ttv = tt[:, cs_, :].rearrange("p c (k t) -> p c k t", t=8)
                            K.op(act, lambda: nc.scalar.activation(out=sqv, in_=yT(b), func=AF.Square), [PB[b]], [Bsq])
                            K.op(dve, lambda: nc.vector.tensor_scalar(out=sq[:, cs_, :], in0=sq[:, cs_, :], scalar1=0.044715, scalar2=1.0, op0=ALU.mult, op1=ALU.add), [Bsq], [Bsq])
                            K.op(dve, lambda: nc.vector.tensor_tensor(out=ttv, in0=sqv, in1=yT(b), op=ALU.mult), [Bsq, PB[b]], [Btt])
                            K.op(act, lambda: nc.scalar.activation(out=sq[:, cs_, :], in_=tt[:, cs_, :], func=AF.Sigmoid, scale=1.5957691216057308), [Btt], [Bsq])
                            K.op(dve, lambda: nc.vector.tensor_tensor(out=hT[:, cs_, :].rearrange("p c (k t) -> p c k t", t=8), in0=sqv, in1=yT(b), op=ALU.mult), [Bsq, PB[b]], [BhT])
                            yield
                        for co in range(6):
                            b = 2
                            for kt in range(6):
                                K.op(pe, lambda: nc.tensor.matmul(psum[:, b, (co % 4) * 128:(co % 4 + 1) * 128], lhsT=wglu[:, kt, co * 128:(co + 1) * 128], rhs=hT[:, kt, :], start=(kt == 0), stop=(kt == 5)), [Bwglu, BhT], [PB[b]])
                            K.op(act, lambda: nc.scalar.activation(out=sg[:, co, :], in_=psum[:, b, (co % 4) * 128:(co % 4 + 1) * 128], func=AF.Sigmoid, bias=bg[:, co:co + 1]), [PB[b], Bbg], [Bsg])
                            if co % 2 == 1:
                                yield
                        K.op(dve, lambda: nc.vector.tensor_tensor(out=s5a[:], in0=hT[:], in1=sg[:], op=ALU.mult), [BhT, Bsg], [Bs5a])
                        for hf in range(2):
                            b = hf
                            mm_tok(b, s5a, Bs5a, 6, ws5o, Bws5o, hf * 512, hf * 512 + 512)
                            K.op(dve, lambda: nc.vector.tensor_tensor(out=mg[:, hf * 512:(hf + 1) * 512], in0=psum[:, b, :], in1=g_[:, hf * 512:(hf + 1) * 512], op=ALU.mult), [PB[b], Bg_], [Bmg])
                            yield

                    def gen_tm():
                        K.op(act, lambda: nc.scalar.activation(out=sz[:], in_=z_[:], func=AF.Silu), [Bz_], [Bsz])
                        K.op(dve, lambda: nc.vector.tensor_tensor(out=gz[:], in0=ym_[:], in1=sz[:], op=ALU.mult), [Bym_, Bsz], [Bgz])
                        yield
                        for g in range(4):
                            K.op(act, lambda: nc.scalar.activation(out=junk2[:], in_=gz[:, g * 384:(g + 1) * 384], func=AF.Square, accum_out=ss4[:, g:g + 1]), [Bgz], [Bjunk2, Bss4])
                        K.op(dve, lambda: nc.vector.tensor_scalar(out=ss4[:], in0=ss4[:], scalar1=1.0 / 384, scalar2=EPS, op0=ALU.mult, op1=ALU.add), [Bss4], [Bss4])
                        K.op(act, lambda: nc.scalar.activation(out=ss4[:], in_=ss4[:], func=AF.Sqrt), [Bss4], [Bss4])
                        K.op(dve, lambda: nc.vector.reciprocal(out=ss4[:], in_=ss4[:]), [Bss4], [Bss4])
                        yield
                        K.op(dve, lambda: nc.vector.tensor_tensor(out=gz[:].rearrange("p (g q) -> p g q", g=4), in0=gz[:].rearrange("p (g q) -> p g q", g=4), in1=ss4[:].unsqueeze(2).to_broadcast([128, 4, 384]), op=ALU.mult), [Bgz, Bss4], [Bgz])
                        K.op(dve, lambda: nc.vector.tensor_tensor(out=gnb[:], in0=gz[:], in1=nwb[:], op=ALU.mult), [Bgz, Bnwb], [Bgnb])
                        yield
                        transpose_to(gT, BgT, gnb, Bgnb, 12, 4, evac=act)
                        yield
                        for hf in range(2):
                            b = 4 + hf
                            mm_tok(b, gT, BgT, 12, wm2o, Bwm2o, hf * 512, hf * 512 + 512)
                            K.op(dve, lambda: nc.vector.tensor_tensor(out=mg2[:, hf * 512:(hf + 1) * 512], in0=psum[:, b, :], in1=g_[:, D + hf * 512:D + (hf + 1) * 512], op=ALU.mult), [PB[b], Bg_], [Bmg2])
                            yield

                    def gen_wo(cp):
                        mgb, Bmgb = mgb2[cp % 2]
                        mT, BmT = mT2[cp % 2]
                        xp_, Bxp_ = xt[cp % 3]
                        x1_, Bx1_ = x1t[cp % 2]
                        transpose_to(mT, BmT, mgb, Bmgb, 8, 3)
                        yield
                        for hf in range(2):
                            b = 6 + hf
                            mm_tok(b, mT, BmT, 8, wo, Bwo, hf * 512, hf * 512 + 512)
                            K.op(dve, lambda: nc.vector.tensor_tensor(out=x1_[:, hf * 512:(hf + 1) * 512], in0=psum[:, b, :], in1=xp_[:, hf * 512:(hf + 1) * 512], op=ALU.add), [PB[b], Bxp_], [Bx1_])
                            yield
                        K.dma(s_x1[cp], x1_[:], reads=[Bx1_], writes=[B_x1[cp]])

                    gl = [[gen_tm(), 0], [gen_t5(), 0]]
                    if c > 0:
                        gl.append([gen_wo(c - 1), 1])
                    run_gens(gl)
                    if debug:
                        K.dma(s_dbg1[c], mg[:], reads=[Bmg], writes=[])
                        K.dma(s_dbg2[c], mg2[:], reads=[Bmg2], writes=[])
                    K.op(dve, lambda: nc.vector.tensor_tensor(out=mgb2[c % 2][0][:], in0=mg[:], in1=mg2[:], op=ALU.add), [Bmg, Bmg2], [mgb2[c % 2][1]])
                run_gens([[gen_wo(NCH - 1), 0]])
                K.barrier()
            if LVL < 4:
                break
            with ExitStack() as ph:
                pw_ = [sb(ph, [128, 8, DFF], BF16, "wup"), sb(ph, [128, 32, D], BF16, "wdn")]
                with ExitStack() as st_:
                    stage = [sb(st_, [128, 1024], F32, "stg") for _ in range(3)]
                    wup, Bwup = load_weight(ph, P["w_up"][L], 8, DFF, "wup", stage, [act, dve], pre=pw_[0])
                    wdn, Bwdn = load_weight(ph, P["w_down"][L], 32, D, "wdn", stage, [act, dve], pre=pw_[1])
                    K.barrier()
                w2b, Bw2b = bcast_row(ph, P["norm2_w"][L], D, "w2b")
                last = (L == DEPTH - 1)
                if last:
                    wfb, Bwfb = bcast_row(ph, P["final_norm_w"], D, "wfb")
                NB = 2
                xt = [sb(ph, [128, D], F32, "x") for _ in range(NB)]
                junk = sb(ph, [128, D], BF16, "junk")
                ss = [sb(ph, [128, 1], F32, "ss") for _ in range(NB)]
                hb = [sb(ph, [128, D], BF16, "h") for _ in range(NB)]
                hT = [sb(ph, [128, 8, 128], BF16, "hT") for _ in range(NB)]
                aT, BaT = sb(ph, [128, 32, 128], BF16, "aT")
                rl = [sb(ph, [128, 4, 128], F32, "rl") for _ in range(2)]
                x2t = [sb(ph, [128, D], F32, "x2") for _ in range(NB)]
                yo = [sb(ph, [128, D], F32, "yo") for _ in range(NB)]

                def p3_load(c):
                    t, B = xt[c % NB]
                    K.dma(t[:], s_x1[c], reads=[B_x1[c]], writes=[B])

                ssf = [sb(ph, [128, 1], F32, "ssf") for _ in range(NB)]

                def p3_head(c):
                    i = c % NB
                    (x_, Bx), (ss_, Bss), (h_, Bh), (hT_, BhT) = xt[i], ss[i], hb[i], hT[i]
                    rms_rstd(x_[:], Bx, D, ss_, Bss, junk[0][:], junk[1])
                    K.op(dve, lambda: nc.vector.scalar_tensor_tensor(out=h_[:], in0=x_[:], scalar=ss_[:, 0:1], in1=w2b[:], op0=ALU.mult, op1=ALU.mult), [Bx, Bss, Bw2b], [Bh])
                    transpose_to(hT_, BhT, h_, Bh, 8, 0)

                p3_load(0)
                p3_head(0)
                for c in range(NCH):
                    if c + 1 < NCH:
                        p3_load(c + 1)
                    i = c % NB
                    (x_, Bx), (ss_, Bss), (h_, Bh), (hT_, BhT) = xt[i], ss[i], hb[i], hT[i]
                    for q in range(8):
                        b = 1 + (q % 4)
                        for j in range(4):
                            ft = q * 4 + j
                            for kt in range(8):
                                K.op(pe, lambda: nc.tensor.matmul(psum[:, b, j * 128:(j + 1) * 128], lhsT=wup[:, kt, ft * 128:(ft + 1) * 128], rhs=hT_[:, kt, :], start=(kt == 0), stop=(kt == 7)), [Bwup, BhT], [PB[b]])
                        rl_, Brl_ = rl[q % 2]
                        K.op(act, lambda: nc.scalar.activation(out=rl_[:], in_=psum[:, b, :].rearrange("p (a b) -> p a b", b=128), func=AF.Relu), [PB[b]], [Brl_])
                        if q % 2 == 0:
                            K.op(dve, lambda: nc.vector.tensor_tensor(out=aT[:, q * 4:q * 4 + 4, :], in0=rl_[:], in1=rl_[:], op=ALU.mult), [Brl_], [BaT])
                        else:
                            K.op(dve, lambda: nc.vector.tensor_tensor(out=aT[:, q * 4:q * 4 + 4, :], in0=rl_[:], in1=rl_[:], op=ALU.mult), [Brl_], [BaT])
                    if c + 1 < NCH:
                        p3_head(c + 1)
                    x2_, Bx2_ = x2t[i]
                    for hf in range(2):
                        b = 5 + hf
                        mm_tok(b, aT, BaT, 32, wdn, Bwdn, hf * 512, hf * 512 + 512)
                        K.op(dve, lambda: nc.vector.tensor_tensor(out=x2_[:, hf * 512:(hf + 1) * 512], in0=psum[:, b, :], in1=x_[:, hf * 512:(hf + 1) * 512], op=ALU.add), [PB[b], Bx], [Bx2_])
                    if not last:
                        K.dma(s_x2[c], x2_[:], reads=[Bx2_], writes=[B_x2[c]])
                    else:
                        y_, By_ = yo[i]
                        sf_, Bsf_ = ssf[i]
                        rms_rstd(x2_[:], Bx2_, D, sf_, Bsf_, junk[0][:], junk[1])
                        K.op(dve, lambda: nc.vector.scalar_tensor_tensor(out=y_[:], in0=x2_[:], scalar=sf_[:, 0:1], in1=wfb[:], op0=ALU.mult, op1=ALU.mult), [Bx2_, Bsf_, Bwfb], [By_])
                        K.dma(y_out[c], y_[:], reads=[By_], writes=[])
                K.barrier()
            if LVL < 5:
                break

        K.barrier()
    return nc


PNAMES = ["norm1_w", "w_in", "lam_re", "lam_im", "log_dt", "b_re", "b_im", "c_re", "c_im", "d_s5", "w_glu", "b_glu",
          "w_s5_out", "conv_w", "conv_b", "dt_bias", "a_log", "d_m2", "m2_norm_w", "w_m2_out", "w_o", "norm2_w",
          "w_up", "w_down", "final_norm_w"]


def kernel(**inputs):
    xp = np.asarray(inputs["x_prompt"], np.float32)
    xs = np.asarray(inputs["x_sample"], np.float32)
    NCH = 64
    params = {k: np.ascontiguousarray(np.asarray(inputs[k], np.float32)) for k in PNAMES}
    consts = host_consts()
    in_maps = []
    for core in range(8):
        m = np.ones((128, NCH + 1), np.float32)
        m[:, 0] = 0.0
        m[:, NCH] = 0.0
        if core < 4:
            x = xs[core].reshape(NCH, 128, D)
        else:
            j = core - 4
            x = np.zeros((NCH, 128, D), np.float32)
            x[0:16] = xp[2 * j].reshape(16, 128, D)
            x[16:32] = xp[2 * j + 1].reshape(16, 128, D)
            m[:, 16] = 0.0
            m[:, 32] = 0.0
            m[:, 48] = 0.0
        im = dict(params)
        im.update(x=np.ascontiguousarray(x), seqmask=m, consts=consts)
        in_maps.append(im)
    nc = build_program(4, 16)
    res = run_bass_kernel_spmd(nc, in_maps, core_ids=list(range(8)))
    yp = np.zeros((8, 2048, D), np.float32)
    ysm = np.zeros((4, 8192, D), np.float32)
    for core in range(8):
        y = np.asarray(res.results[core]["y"], np.float32).reshape(NCH * 128, D)
        if core < 4:
            ysm[core] = y
        else:
            j = core - 4
            yp[2 * j] = y[0:2048]
            yp[2 * j + 1] = y[2048:4096]
    return (yp, ysm)
```
